# Optimizing a Trainium2 kernel written in Bass

```python
import math
import jax
import jax.numpy as jnp
from jax import lax
import numpy as np

D_MODEL = 1024
BATCH = 16
SEQ = 2048
DEPTH = 2
DEC_BATCH = 32
DEC_SEQ = 16
PAST_LEN = 2048

CHUNK = 64
Q_BLOCK = 128
MLA_HEADS = 8
MLA_NOPE = 64
MLA_ROPE = 32
MLA_V = 64
MLA_Q_RANK = 384
MLA_KV_RANK = 256
ROPE_BASE = 10000.0
MLA_SCALE = (MLA_NOPE + MLA_ROPE) ** -0.5
GDN_HEADS = 4
GDN_DK = 64
GDN_DV = 64
GDN_CONV = 4
QKV_DIM = 2 * GDN_HEADS * GDN_DK + GDN_HEADS * GDN_DV
CONV_CH = 256
CONV_WIDTH = 31
D_MIX = MLA_HEADS * MLA_V + GDN_HEADS * GDN_DV + CONV_CH
D_FF = ((8 * D_MODEL // 3 + 255) // 256) * 256
DN_ALPHA = (2 * DEPTH) ** 0.25
DN_BETA = (8 * DEPTH) ** -0.25
IN_SIZES = (MLA_Q_RANK, MLA_KV_RANK, MLA_ROPE, QKV_DIM, GDN_HEADS, GDN_HEADS, GDN_HEADS * GDN_DV, 2 * CONV_CH)
D_IN = sum(IN_SIZES)
IN_SPLITS = tuple(int(s) for s in np.cumsum(IN_SIZES)[:-1])

kernel_name = 'hybrid_streaming_encoder_step'


def _rmsnorm(x, g, eps=1e-6):
    xf = x.astype(jnp.float32)
    y = xf * lax.rsqrt(jnp.mean(xf * xf, -1, keepdims=True) + eps)
    return (y * g.astype(jnp.float32)).astype(x.dtype)


def _layernorm(x, g, b, eps=1e-5):
    xf = x.astype(jnp.float32)
    mu = jnp.mean(xf, -1, keepdims=True)
    var = jnp.mean(jnp.square(xf - mu), -1, keepdims=True)
    y = (xf - mu) * lax.rsqrt(var + eps) * g.astype(jnp.float32) + b.astype(jnp.float32)
    return y.astype(x.dtype)


def _l2norm(x, eps=1e-6):
    xf = x.astype(jnp.float32)
    return xf * lax.rsqrt(jnp.sum(xf * xf, -1, keepdims=True) + eps)


def _rope(x, pos):
    half = x.shape[-1] // 2
    inv = jnp.exp(-math.log(ROPE_BASE) * jnp.arange(half, dtype=jnp.float32) / half)
    ang = pos.astype(jnp.float32)[:, None, None] * inv
    cos, sin = jnp.cos(ang), jnp.sin(ang)
    xf = x.astype(jnp.float32)
    x1, x2 = xf[..., :half], xf[..., half:]
    return jnp.concatenate([x1 * cos - x2 * sin, x1 * sin + x2 * cos], -1).astype(x.dtype)


def _causal_dwconv(x, buf, w):
    xp = jnp.concatenate([buf.astype(x.dtype), x], axis=1)
    y = lax.conv_general_dilated(xp, w[:, None, :].astype(x.dtype), window_strides=(1,), padding='VALID',
                                 dimension_numbers=('NWC', 'WIO', 'NWC'), feature_group_count=x.shape[-1])
    return y, xp[:, -(w.shape[0] - 1):]


def _latent_attend(q_lat, q_pe, ckv, kpe, visible):
    s = (jnp.einsum('bqhc,bsc->bhqs', q_lat, ckv) + jnp.einsum('bqhr,bsr->bhqs', q_pe, kpe)).astype(jnp.float32) * MLA_SCALE
    if visible is not None:
        s = jnp.where(visible, s, -jnp.inf)
    p = jax.nn.softmax(s, axis=-1).astype(ckv.dtype)
    return jnp.einsum('bhqs,bsc->bqhc', p, ckv)


def _mla_prompt(q_lat, q_pe, ckv, kpe):
    B, S, H, C = q_lat.shape
    nb = S // Q_BLOCK
    key_pos = jnp.arange(S)

    def blk(args):
        ql, qp, i = args
        qpos = i * Q_BLOCK + jnp.arange(Q_BLOCK)
        visible = key_pos[None, :] < (qpos[:, None] // CHUNK + 1) * CHUNK
        return _latent_attend(ql, qp, ckv, kpe, visible)

    qb = jnp.moveaxis(q_lat.reshape(B, nb, Q_BLOCK, H, C), 1, 0)
    pb = jnp.moveaxis(q_pe.reshape(B, nb, Q_BLOCK, H, MLA_ROPE), 1, 0)
    o = lax.map(blk, (qb, pb, jnp.arange(nb)))
    return jnp.moveaxis(o, 0, 1).reshape(B, S, H, C)


def _gdn_chunk(S0, inp):
    q, k, v, g, beta = inp
    L = q.shape[2]
    G = jnp.cumsum(g, axis=-1)
    idx = jnp.arange(L)
    lower = idx[:, None] >= idx[None, :]
    strict = idx[:, None] > idx[None, :]
    decay = jnp.exp(jnp.where(lower, G[..., :, None] - G[..., None, :], -jnp.inf))
    A = jnp.where(strict, beta[..., :, None] * jnp.einsum('bhid,bhjd->bhij', k, k) * decay, 0.0)
    gam = jnp.exp(G)
    rhs = jnp.concatenate([(beta * gam)[..., None] * k, beta[..., None] * v], axis=-1)

    def sub(i, X):
        row = rhs[:, :, i] - jnp.einsum('bhj,bhjd->bhd', A[:, :, i], X)
        return X.at[:, :, i].set(row)

    X = lax.fori_loop(0, L, sub, rhs)
    W, Uv = X[..., :GDN_DK], X[..., GDN_DK:]
    U = Uv - jnp.einsum('bhld,bhde->bhle', W, S0)
    o = jnp.einsum('bhld,bhde->bhle', gam[..., None] * q, S0) + jnp.einsum(
        'bhij,bhje->bhie', jnp.einsum('bhid,bhjd->bhij', q, k) * decay, U)
    S_new = gam[..., -1][..., None, None] * S0 + jnp.einsum(
        'bhld,bhle->bhde', k * jnp.exp(G[..., -1:] - G)[..., None], U)
    return S_new, o


def _gated_delta(q, k, v, g, beta, S0):
    B, T, H, _ = q.shape
    L = CHUNK if T % CHUNK == 0 else T
    n = T // L

    def to_chunks(a):
        a = a.astype(jnp.float32).reshape((B, n, L, H) + a.shape[3:])
        return jnp.moveaxis(a, (1, 3), (0, 2))

    S_T, o = lax.scan(_gdn_chunk, S0.astype(jnp.float32),
                      (to_chunks(q), to_chunks(k), to_chunks(v), to_chunks(g), to_chunks(beta)))
    o = jnp.moveaxis(o, (0, 2), (1, 3)).reshape(B, T, H, GDN_DV)
    return o, S_T


def _mixer(u, p, ckv_past, kpe_past, S0, gbuf, cbuf, start_pos):
    B, T, _ = u.shape
    h = jnp.einsum('btd,de->bte', u, p['w_in'])
    cq, ckv_raw, kpe_raw, qkv, b_raw, a_raw, z, glu_in = jnp.split(h, IN_SPLITS, axis=-1)
    pos = start_pos + jnp.arange(T)

    q = jnp.einsum('btr,rf->btf', _rmsnorm(cq, p['mla_q_norm']), p['w_uq']).reshape(B, T, MLA_HEADS, MLA_NOPE + MLA_ROPE)
    q_nope, q_pe = q[..., :MLA_NOPE], _rope(q[..., MLA_NOPE:], pos)
    ckv_new = _rmsnorm(ckv_raw, p['mla_kv_norm'])
    kpe_new = _rope(kpe_raw[:, :, None, :], pos)[:, :, 0]
    q_lat = jnp.einsum('bthd,chd->bthc', q_nope, p['w_uk'])
    if ckv_past is None:
        o_lat = _mla_prompt(q_lat, q_pe, ckv_new, kpe_new)
    else:
        ckv_all = jnp.concatenate([ckv_past.astype(ckv_new.dtype), ckv_new], axis=1)
        kpe_all = jnp.concatenate([kpe_past.astype(kpe_new.dtype), kpe_new], axis=1)
        o_lat = _latent_attend(q_lat, q_pe, ckv_all, kpe_all, None)
    o_mla = jnp.einsum('bthc,chd->bthd', o_lat, p['w_uv']).reshape(B, T, MLA_HEADS * MLA_V)

    qkv, gbuf_new = _causal_dwconv(qkv, gbuf, p['gdn_conv_w'])
    qkv = jax.nn.silu(qkv)
    gq, gk, gv = jnp.split(qkv, [GDN_HEADS * GDN_DK, 2 * GDN_HEADS * GDN_DK], axis=-1)
    gq = _l2norm(gq.reshape(B, T, GDN_HEADS, GDN_DK)) * (GDN_DK ** -0.5)
    gk = _l2norm(gk.reshape(B, T, GDN_HEADS, GDN_DK))
    gv = gv.reshape(B, T, GDN_HEADS, GDN_DV)
    beta = jax.nn.sigmoid(b_raw.astype(jnp.float32))
    g = -jnp.exp(p['gdn_a_log'].astype(jnp.float32)) * jax.nn.softplus(
        a_raw.astype(jnp.float32) + p['gdn_dt_bias'].astype(jnp.float32))
    o, S_new = _gated_delta(gq, gk, gv, g, beta, S0)
    o_gdn = (_rmsnorm(o, p['gdn_norm']) * jax.nn.silu(z.reshape(B, T, GDN_HEADS, GDN_DV))).astype(u.dtype)
    o_gdn = o_gdn.reshape(B, T, GDN_HEADS * GDN_DV)

    a, gate = jnp.split(glu_in, 2, axis=-1)
    c = a * jax.nn.sigmoid(gate)
    c, cbuf_new = _causal_dwconv(c, cbuf, p['conv_w'])
    c = c + p['conv_b']
    o_conv = jax.nn.silu(_layernorm(c, p['conv_ln_g'], p['conv_ln_b']))

    y = jnp.einsum('bte,ed->btd', jnp.concatenate([o_mla, o_gdn, o_conv], axis=-1), p['w_out'])
    return y, (ckv_new, kpe_new, S_new.astype(u.dtype), gbuf_new, cbuf_new)


def _layer(x, p, ckv_past, kpe_past, S0, gbuf, cbuf, start_pos):
    y, st = _mixer(x, p, ckv_past, kpe_past, S0, gbuf, cbuf, start_pos)
    x = _layernorm(DN_ALPHA * x + y, p['ln1_g'], p['ln1_b'])
    f = jax.nn.silu(jnp.einsum('btd,df->btf', x, p['w_gate'])) * jnp.einsum('btd,df->btf', x, p['w_up'])
    f = jnp.einsum('btf,fd->btd', f, p['w_down'])
    x = _layernorm(DN_ALPHA * x + f, p['ln2_g'], p['ln2_b'])
    return x, st


def setup_inputs(seed: int = 0) -> dict:
    key = jax.random.key(seed)
    ks = iter(jax.random.split(key, 40))

    def nrm(shape, scale):
        return jax.random.normal(next(ks), shape, jnp.float32) * scale

    def gain(shape):
        return 1.0 + nrm(shape, 0.01)

    dt = jnp.exp(jax.random.uniform(next(ks), (DEPTH, GDN_HEADS), jnp.float32,
                                    minval=math.log(1e-3), maxval=math.log(1e-1)))
    return {
        'x_prompt': nrm((BATCH, SEQ, D_MODEL), 1.0),
        'x_sample': nrm((DEC_BATCH, DEC_SEQ, D_MODEL), 1.0),
        'cache_mla_ckv': nrm((DEPTH, DEC_BATCH, PAST_LEN, MLA_KV_RANK), 1.0),
        'cache_mla_kpe': nrm((DEPTH, DEC_BATCH, PAST_LEN, MLA_ROPE), 1.0),
        'state_gdn': nrm((DEPTH, DEC_BATCH, GDN_HEADS, GDN_DK, GDN_DV), 0.1),
        'state_gdn_conv': nrm((DEPTH, DEC_BATCH, GDN_CONV - 1, QKV_DIM), 1.0),
        'state_conv': nrm((DEPTH, DEC_BATCH, CONV_WIDTH - 1, CONV_CH), 0.5),
        'w_in': nrm((DEPTH, D_MODEL, D_IN), D_MODEL ** -0.5),
        'mla_q_norm': gain((DEPTH, MLA_Q_RANK)),
        'w_uq': nrm((DEPTH, MLA_Q_RANK, MLA_HEADS * (MLA_NOPE + MLA_ROPE)), MLA_Q_RANK ** -0.5),
        'mla_kv_norm': gain((DEPTH, MLA_KV_RANK)),
        'w_uk': nrm((DEPTH, MLA_KV_RANK, MLA_HEADS, MLA_NOPE), MLA_KV_RANK ** -0.5),
        'w_uv': nrm((DEPTH, MLA_KV_RANK, MLA_HEADS, MLA_V), DN_BETA * MLA_KV_RANK ** -0.5),
        'gdn_conv_w': nrm((DEPTH, GDN_CONV, QKV_DIM), GDN_CONV ** -0.5),
        'gdn_a_log': jnp.log(jax.random.uniform(next(ks), (DEPTH, GDN_HEADS), jnp.float32, minval=1.0, maxval=16.0)),
        'gdn_dt_bias': dt + jnp.log(-jnp.expm1(-dt)),
        'gdn_norm': gain((DEPTH, GDN_DV)),
        'conv_w': nrm((DEPTH, CONV_WIDTH, CONV_CH), CONV_WIDTH ** -0.5),
        'conv_b': nrm((DEPTH, CONV_CH), 0.01),
        'conv_ln_g': gain((DEPTH, CONV_CH)),
        'conv_ln_b': nrm((DEPTH, CONV_CH), 0.01),
        'w_out': nrm((DEPTH, D_MIX, D_MODEL), DN_BETA * D_MIX ** -0.5),
        'ln1_g': gain((DEPTH, D_MODEL)),
        'ln1_b': nrm((DEPTH, D_MODEL), 0.01),
        'w_gate': nrm((DEPTH, D_MODEL, D_FF), D_MODEL ** -0.5),
        'w_up': nrm((DEPTH, D_MODEL, D_FF), DN_BETA * D_MODEL ** -0.5),
        'w_down': nrm((DEPTH, D_FF, D_MODEL), DN_BETA * D_FF ** -0.5),
        'ln2_g': gain((DEPTH, D_MODEL)),
        'ln2_b': nrm((DEPTH, D_MODEL), 0.01),
    }


def reference(x_prompt, x_sample, cache_mla_ckv, cache_mla_kpe, state_gdn, state_gdn_conv, state_conv,
              w_in, mla_q_norm, w_uq, mla_kv_norm, w_uk, w_uv, gdn_conv_w, gdn_a_log, gdn_dt_bias, gdn_norm,
              conv_w, conv_b, conv_ln_g, conv_ln_b, w_out, ln1_g, ln1_b, w_gate, w_up, w_down, ln2_g, ln2_b):
    B = x_prompt.shape[0]
    xp, xs = x_prompt, x_sample
    st_p = [[], [], [], [], []]
    st_s = [[], [], [], [], []]
    for l in range(DEPTH):
        p = {'w_in': w_in[l], 'mla_q_norm': mla_q_norm[l], 'w_uq': w_uq[l], 'mla_kv_norm': mla_kv_norm[l],
             'w_uk': w_uk[l], 'w_uv': w_uv[l], 'gdn_conv_w': gdn_conv_w[l], 'gdn_a_log': gdn_a_log[l],
             'gdn_dt_bias': gdn_dt_bias[l], 'gdn_norm': gdn_norm[l], 'conv_w': conv_w[l], 'conv_b': conv_b[l],
             'conv_ln_g': conv_ln_g[l], 'conv_ln_b': conv_ln_b[l], 'w_out': w_out[l], 'ln1_g': ln1_g[l],
             'ln1_b': ln1_b[l], 'w_gate': w_gate[l], 'w_up': w_up[l], 'w_down': w_down[l],
             'ln2_g': ln2_g[l], 'ln2_b': ln2_b[l]}
        zS = jnp.zeros((B, GDN_HEADS, GDN_DK, GDN_DV), jnp.float32)
        zg = jnp.zeros((B, GDN_CONV - 1, QKV_DIM), xp.dtype)
        zc = jnp.zeros((B, CONV_WIDTH - 1, CONV_CH), xp.dtype)
        xp, sp = _layer(xp, p, None, None, zS, zg, zc, 0)
        xs, ss = _layer(xs, p, cache_mla_ckv[l], cache_mla_kpe[l], state_gdn[l], state_gdn_conv[l],
                        state_conv[l], PAST_LEN)
        for i in range(5):
            st_p[i].append(sp[i])
            st_s[i].append(ss[i])
    ckv_p, kpe_p, gdn_p, gconv_p, conv_p = [jnp.stack(o) for o in st_p]
    ckv_s, kpe_s, gdn_s, gconv_s, conv_s = [jnp.stack(o) for o in st_s]
    return (xp, xs, ckv_p, kpe_p, gdn_p, gconv_p, conv_p, ckv_s, kpe_s, gdn_s, gconv_s, conv_s)
```

```python
import contextlib
import numpy as np
import concourse.bass as bass
import concourse.mybir as mybir
from concourse.bass_utils import run_bass_kernel_spmd

F32 = mybir.dt.float32
BF16 = mybir.dt.bfloat16
AF = mybir.ActivationFunctionType
ALU = mybir.AluOpType
AX = mybir.AxisListType

D = 1024
DEPTH = 2
SEQ = 2048
PAST = 2048
DSEQ = 16
H = 8
DIN = 2216
DFF = 2816
NF = DFF // 128
ALPHA = float((2 * DEPTH) ** 0.25)
MLA_SCALE = float(96 ** -0.5)
NCORES = 8


class Sched:
    NDS = 12

    def __init__(self, nc, es):
        self.nc = nc
        self.eng = {'pe': nc.tensor, 'act': nc.scalar, 'dve': nc.vector, 'pool': nc.gpsimd, 'sp': nc.sync}
        self.sem = {e: es.enter_context(nc.semaphore('sem_' + e)) for e in self.eng}
        self.cnt = {e: 0 for e in self.eng}
        self.dsem = {q: [es.enter_context(nc.semaphore('d%s%d' % (q, i))) for i in range(self.NDS)]
                     for q in ('sp', 'pool')}
        self.dcnt = {q: [0] * self.NDS for q in ('sp', 'pool')}
        self.dnext = {'sp': 0, 'pool': 0}
        self.waited = {e: {} for e in self.eng}
        self.res = {}
        self.ninst = 0

    def _need(self, e, ev):
        name, handle, val = ev
        w = self.waited[e]
        if w.get(name, 0) >= val:
            return
        self.eng[e].wait_ge(handle, val)
        w[name] = val
        self.ninst += 1

    def _deps(self, e, R, W):
        for k in R:
            r = self.res.get(k)
            if r is not None and r[0] is not None:
                self._need(e, r[0])
        for k in W:
            r = self.res.get(k)
            if r is not None:
                if r[0] is not None:
                    self._need(e, r[0])
                for ev in r[1].values():
                    self._need(e, ev)

    def _commit(self, ev, R, W):
        for k in R:
            r = self.res.get(k)
            if r is None:
                r = [None, {}]
                self.res[k] = r
            r[1][ev[0]] = ev
        for k in W:
            self.res[k] = [ev, {}]

    @staticmethod
    def _bankify(R, W):
        banks = set()
        for k in list(R) + list(W):
            if isinstance(k, tuple) and k[0] == 'ps':
                banks.add(k[1])
        if not banks:
            return W
        return list(W) + [('bank', b) for b in sorted(banks)]

    def op(self, e, fn, R=(), W=()):
        W = self._bankify(R, W)
        self._deps(e, R, W)
        ins = fn(self.eng[e])
        self.cnt[e] += 1
        ins.then_inc(self.sem[e], 1)
        self.ninst += 1
        self._commit(('E' + e, self.sem[e], self.cnt[e]), R, W)

    def mm(self, fns, R=(), W=()):
        W = self._bankify(R, W)
        self._deps('pe', R, W)
        ins = None
        for fn in fns:
            ins = fn(self.nc.tensor)
            self.ninst += 1
        self.cnt['pe'] += 1
        ins.then_inc(self.sem['pe'], 1)
        self._commit(('Epe', self.sem['pe'], self.cnt['pe']), R, W)

    def dma(self, q, out, in_, R=(), W=(), **kw):
        i = self.dnext[q]
        self.dnext[q] = (i + 1) % self.NDS
        name = 'D%s%d' % (q, i)
        if self.dcnt[q][i] > 0:
            self._need(q, (name, self.dsem[q][i], self.dcnt[q][i]))
        self._deps(q, R, W)
        ins = self.eng[q].dma_start(out=out, in_=in_, **kw)
        self.dcnt[q][i] += 16
        ins.then_inc(self.dsem[q][i], 16)
        self.ninst += 1
        self._commit((name, self.dsem[q][i], self.dcnt[q][i]), R, W)

    def barrier(self):
        evs = []
        for e in self.eng:
            if self.cnt[e] > 0:
                evs.append(('E' + e, self.sem[e], self.cnt[e]))
        for q in self.dsem:
            for i in range(self.NDS):
                if self.dcnt[q][i] > 0:
                    evs.append(('D%s%d' % (q, i), self.dsem[q][i], self.dcnt[q][i]))
        for e in self.eng:
            for ev in evs:
                self._need(e, ev)
        self.res.clear()


class Seq:
    def __init__(self, T, c0, L, idx, prompt):
        self.T, self.c0, self.L, self.idx, self.prompt = T, c0, L, idx, prompt
        self.blocks = [(b, min(512, T - b)) for b in range(0, T, 512)]


class Job:
    def __init__(self, prompt, pidx):
        self.prompt = prompt
        self.pidx = pidx
        if prompt:
            self.T = SEQ
            self.seqs = [Seq(SEQ, 0, 64, pidx, True)]
            self.Tk = SEQ
        else:
            self.T = 4 * DSEQ
            self.seqs = [Seq(DSEQ, DSEQ * s, DSEQ, s, False) for s in range(4)]
            self.Tk = PAST + DSEQ
        self.tiles = [(r, min(128, self.T - r)) for r in range(0, self.T, 128)]
        self.halves = [(r, min(1024, self.T - r)) for r in range(0, self.T, 1024)]


def build_program():
    nc = bass.Bass("TRN2", target_bir_lowering=False)

    def din(name, shape):
        return nc.dram_tensor(name, list(shape), F32, kind="ExternalInput").ap()

    def dout(name, shape):
        return nc.dram_tensor(name, list(shape), F32, kind="ExternalOutput").ap()

    xp = din("xp", [2 * SEQ, D])
    xs = din("xs", [4 * DSEQ, D])
    c_ckv = din("c_ckv", [DEPTH, 4, PAST, 256])
    c_kpe = din("c_kpe", [DEPTH, 4, PAST, 32])
    s_gdn = din("s_gdn", [DEPTH, 4, 4, 64, 64])
    s_gcv = din("s_gcv", [DEPTH, 4, 3, 768])
    s_cv = din("s_cv", [DEPTH, 4, 30, 256])
    w_in = din("w_in", [DEPTH, D, DIN])
    q_norm = din("mla_q_norm", [DEPTH, 384])
    w_uq = din("w_uq", [DEPTH, 384, 768])
    kv_norm = din("mla_kv_norm", [DEPTH, 256])
    w_uk = din("w_uk", [DEPTH, 256, 8, 64])
    w_uv = din("w_uv", [DEPTH, 256, 8, 64])
    gcw_d = din("gdn_conv_w", [DEPTH, 4, 768])
    a_log = din("gdn_a_log", [DEPTH, 4])
    dt_bias = din("gdn_dt_bias", [DEPTH, 4])
    gdn_norm = din("gdn_norm", [DEPTH, 64])
    conv_w = din("conv_w", [DEPTH, 31, 256])
    conv_b = din("conv_b", [DEPTH, 256])
    cln_g = din("conv_ln_g", [DEPTH, 256])
    cln_b = din("conv_ln_b", [DEPTH, 256])
    w_out = din("w_out", [DEPTH, D, D])
    ln1_g = din("ln1_g", [DEPTH, D])
    ln1_b = din("ln1_b", [DEPTH, D])
    w_gate = din("w_gate", [DEPTH, D, DFF])
    w_up = din("w_up", [DEPTH, D, DFF])
    w_down = din("w_down", [DEPTH, DFF, D])
    ln2_g = din("ln2_g", [DEPTH, D])
    ln2_b = din("ln2_b", [DEPTH, D])
    consts = din("consts", [128, 512])
    rope_p = din("rope_p", [SEQ, 32])
    rope_s = din("rope_s", [4 * DSEQ, 32])

    y_p = dout("y_p", [2 * SEQ, D])
    y_s = dout("y_s", [4 * DSEQ, D])
    o_ckv_p = dout("o_ckv_p", [DEPTH, 2, SEQ, 256])
    o_kpe_p = dout("o_kpe_p", [DEPTH, 2, SEQ, 32])
    o_gdn_p = dout("o_gdn_p", [DEPTH, 2, 4, 64, 64])
    o_gcv_p = dout("o_gcv_p", [DEPTH, 2, 3, 768])
    o_cv_p = dout("o_cv_p", [DEPTH, 2, 30, 256])
    o_ckv_s = dout("o_ckv_s", [DEPTH, 4, DSEQ, 256])
    o_kpe_s = dout("o_kpe_s", [DEPTH, 4, DSEQ, 32])
    o_gdn_s = dout("o_gdn_s", [DEPTH, 4, 4, 64, 64])
    o_gcv_s = dout("o_gcv_s", [DEPTH, 4, 3, 768])
    o_cv_s = dout("o_cv_s", [DEPTH, 4, 30, 256])

    xs1 = nc.dram_tensor("xs1", [SEQ, D], F32, kind="Internal").ap()
    xs2 = nc.dram_tensor("xs2", [SEQ, D], F32, kind="Internal").ap()

    es = contextlib.ExitStack()
    with es:
        S = Sched(nc, es)

        uid = [0]

        def sb(st, name, shape, dt=F32):
            uid[0] += 1
            return st.enter_context(nc.sbuf_tensor("%s_%d" % (name, uid[0]), list(shape), dt))

        CB = sb(es, "CB", [128, 512])
        CBb = sb(es, "CBb", [128, 128], BF16)
        ONES = sb(es, "ONES", [128, 128])
        E32 = sb(es, "E32", [32, 96], BF16)
        xT = sb(es, "xT", [128, 8, SEQ], BF16)
        PS = [es.enter_context(nc.psum_tensor("PS%d" % i, [128, 512], F32)) for i in range(8)]

        S.dma('sp', CB[:], consts, W=['CB'])
        S.dma('pool', CBb[:], consts[:, 0:128], W=['CBb'])
        S.op('dve', lambda e: e.memset(ONES[:], 1.0), W=['ONES'])
        S.op('dve', lambda e: e.memset(E32[:], 0.0), W=['E32'])
        S.op('dve', lambda e: e.tensor_copy(E32[:, 64:96], CBb[0:32, 0:32]), R=['CBb'], W=['E32'])
        IDF = CB[:, 0:128]
        IDB = CBb[:, 0:128]

        def maskU(L):
            return CB[0:L, 128:128 + L]

        def maskLs(L):
            return CB[0:L, 320:320 + L]

        BONES = CB[:, 384:512]
        S.barrier()

        FMAX = int(nc.vector.BN_STATS_FMAX)
        assert D % FMAX == 0 or FMAX >= D
        NBN = max(1, D // FMAX)
        BNW = D // NBN

        def tr_bf(out_ps, data, K, R, W):
            S.mm([lambda e: e.matmul(out_ps, data, IDB[0:K, 0:K], start=True, stop=True)], R=R + ['CBb'], W=W)

        def tr_f32(out_ps, data, K, R, W):
            S.mm([lambda e: e.matmul(out_ps, data, IDF[0:K, 0:K], start=True, stop=True)], R=R + ['CB'], W=W)

        def rsqrt_act(out, in_, scale, eps, R, W, tmp, tmpkey):
            S.op('act', lambda e: e.activation(out=tmp, in_=in_, func=AF.Ln, bias=float(eps), scale=float(scale)),
                 R=R, W=[tmpkey])
            S.op('act', lambda e: e.activation(out=out, in_=tmp, func=AF.Exp, scale=-0.5), R=[tmpkey], W=W)

        def tiles_of(c0, n):
            return list(range(c0 // 128, (c0 + n - 1) // 128 + 1))

        def xkeys(c0, n):
            return [('xT', t) for t in tiles_of(c0, n)]

        def layernorm_tile(st_tiles, pk, tsz, res_src, res_keys, g_b, b_b, dst, dst_keys, xt_cols, write_xT, tag):
            xres, xb, st6, mv, sc = st_tiles
            z = xres
            xn = xres
            S.dma('sp', xres[0:tsz, :], res_src, R=res_keys, W=[(tag + 'z', 0), (tag + 'z', 1)])
            for hf in range(2):
                S.op('dve', lambda e, hf=hf: e.scalar_tensor_tensor(
                    out=z[0:tsz, hf * 512:(hf + 1) * 512], in0=xres[0:tsz, hf * 512:(hf + 1) * 512], scalar=ALPHA,
                    in1=PS[pk[hf]][0:tsz, 0:512], op0=ALU.mult, op1=ALU.add),
                     R=[('ps', pk[hf])], W=[(tag + 'z', hf)])
            for c in range(NBN):
                S.op('dve', lambda e, c=c: e.bn_stats(st6[0:tsz, c, :], z[0:tsz, c * BNW:(c + 1) * BNW]),
                     R=[(tag + 'z', 0), (tag + 'z', 1)], W=[(tag + 'st', c)])
            S.op('dve', lambda e: e.bn_aggr(mv[0:tsz, :], st6[0:tsz, :, :]),
                 R=[(tag + 'st', c) for c in range(NBN)], W=[tag + 'mv'])
            rsqrt_act(sc[0:tsz, 0:1], mv[0:tsz, 1:2], 1.0, 1e-5, [tag + 'mv'], [tag + 'sc0'], sc[0:tsz, 2:3], tag + 'sc2')
            S.op('dve', lambda e: e.scalar_tensor_tensor(out=sc[0:tsz, 1:2], in0=mv[0:tsz, 0:1], scalar=-1.0,
                                                         in1=sc[0:tsz, 0:1], op0=ALU.mult, op1=ALU.mult),
                 R=[tag + 'mv', tag + 'sc0'], W=[tag + 'sc1'])
            S.op('act', lambda e: e.activation(out=xn[0:tsz, :], in_=z[0:tsz, :], func=AF.Identity,
                                               bias=sc[0:tsz, 1:2], scale=sc[0:tsz, 0:1]),
                 R=[tag + 'sc0', tag + 'sc1'], W=[(tag + 'z', 0), (tag + 'z', 1)])
            S.op('dve', lambda e: e.tensor_tensor(xn[0:tsz, :], xn[0:tsz, :], g_b[0:tsz, :], ALU.mult),
                 R=['lnp'], W=[(tag + 'z', 0), (tag + 'z', 1)])
            S.op('dve', lambda e: e.tensor_tensor(xn[0:tsz, :], xn[0:tsz, :], b_b[0:tsz, :], ALU.add),
                 R=['lnp'], W=[(tag + 'z', 0), (tag + 'z', 1)])
            S.dma('sp', dst, xn[0:tsz, :], R=[(tag + 'z', 0), (tag + 'z', 1)], W=dst_keys)
            if write_xT:
                S.op('act', lambda e: e.activation(out=xb[0:tsz, :], in_=xn[0:tsz, :], func=AF.Copy),
                     R=[(tag + 'z', 0), (tag + 'z', 1)], W=[tag + 'xb'])
                for hf in range(2):
                    fns = []
                    for k in range(4):
                        kk = hf * 4 + k
                        fns.append(lambda e, k=k, kk=kk: e.matmul(PS[pk[hf]][:, k * 128:k * 128 + tsz],
                                                                 xb[0:tsz, kk * 128:(kk + 1) * 128],
                                                                 IDB[0:tsz, 0:tsz], start=True, stop=True))
                    S.mm(fns, R=[tag + 'xb', 'CBb'], W=[('ps', pk[hf])])
                    c0, _ = xt_cols
                    S.op('act' if hf == 0 else 'dve', lambda e, hf=hf, c0=c0: (
                        e.activation(out=xT[:, hf * 4:(hf + 1) * 4, c0:c0 + tsz],
                                     in_=PS[pk[hf]][:, :].rearrange("p (k t) -> p k t", t=128)[:, :, 0:tsz],
                                     func=AF.Copy)
                        if hf == 0 else
                        e.tensor_copy(xT[:, hf * 4:(hf + 1) * 4, c0:c0 + tsz],
                                      PS[pk[hf]][:, :].rearrange("p (k t) -> p k t", t=128)[:, :, 0:tsz])),
                         R=[('ps', pk[hf])], W=[('xT', c0 // 128, hf)])

        def xT_R(c0, n):
            ks = []
            for t in tiles_of(c0, n):
                ks.append(('xT', t, 0))
                ks.append(('xT', t, 1))
            return ks

        import os
        STAGE = os.environ.get("KSTAGE", "full")

        def stub_zero(st, job, mixB):
            S.op('dve', lambda e: e.memset(mixB[:], 0.0), W=[('mixB', t) for t in range(len(job.tiles))])

        def gdn_phase(st, job, l, mixB):
            if STAGE in ("ffn",):
                return stub_zero(st, job, mixB)
            return gdn_phase_real(st, job, l, mixB)

        def conv_phase(st, job, l, mixB):
            if STAGE in ("ffn", "gdn"):
                return
            return conv_phase_real(st, job, l, mixB)

        def mla_proj_phase(st, job, l, QT, ckvT, kpeT, ckvN, kpeN, rope_src):
            if STAGE in ("ffn", "gdn", "conv"):
                return
            return mla_proj_phase_real(st, job, l, QT, ckvT, kpeT, ckvN, kpeN, rope_src)

        def attn_phase(st, job, l, QT, ckvT, kpeT, ckvN, kpeN):
            if STAGE in ("ffn", "gdn", "conv", "mlaproj"):
                for ti, (r0, tsz) in enumerate(job.tiles):
                    S.op('dve', lambda e: e.memset(xT[:, 0:4, r0:r0 + tsz], 0.0), W=[('xT', ti, 0)])
                return
            return attn_phase_real(st, job, l, QT, ckvT, kpeT, ckvN, kpeN)


        CSTOP = int(os.environ.get('KCSTOP', '0'))

        def conv_phase_real(st, job, l, mixB):
            wgl = sb(st, "wgl", [128, 8, 512], BF16)
            diag = sb(st, "diag", [128, 2, 31, 128], BF16)
            cwT = sb(st, "cwT", [128, 2, 32])
            cw_st = sb(st, "cw_st", [32, 256])
            prm = sb(st, "cprm", [128, 3, 2])
            c32 = sb(st, "c32", [128, 2, 30 + 512])
            cbf = sb(st, "cbf", [128, 2, 30 + 512], BF16)
            sgm = sb(st, "sgm", [128, 2, 512])
            cb = sb(st, "cb", [128, 2, 512])
            sq = sb(st, "csq", [128, 2, 512])
            mean = sb(st, "cmean", [128, 512])
            msq = sb(st, "cmsq", [128, 512])
            var = sb(st, "cvar", [128, 512])
            rstd = sb(st, "crstd", [128, 512])
            ltmp = sb(st, "cltmp", [128, 512])
            tt = sb(st, "ctt", [128, 2, 512])
            stg = sb(st, "cstg", [30, 256])
            S.dma('pool', wgl[:], w_in[l, :, 1704:2216].rearrange("(k p) c -> p k c", p=128), W=['wgl'])
            S.dma('sp', cw_st[0:31, :], conv_w[l], W=['cw_st'])
            for fc in range(2):
                tr_f32(PS[6][:, fc * 32:fc * 32 + 31], cw_st[0:31, fc * 128:(fc + 1) * 128], 31, ['cw_st'], [('ps', 6)])
            S.op('dve', lambda e: e.tensor_copy(cwT[:, :, 0:31],
                                                PS[6][:, 0:64].rearrange("p (f j) -> p f j", j=32)[:, :, 0:31]),
                 R=[('ps', 6)], W=['cwT'])
            if CSTOP == 1:
                return
            for fc in range(2):
                for j in range(31):
                    S.op('dve', lambda e, fc=fc, j=j: e.tensor_scalar(diag[:, fc, j, :], IDB, cwT[:, fc, j:j + 1], None,
                                                                     ALU.mult),
                         R=['cwT', 'CBb'], W=[('diag', fc)])
            if CSTOP == 2:
                return
            for wi, src in enumerate((conv_b, cln_g, cln_b)):
                for fc in range(2):
                    S.dma('sp', prm[:, wi, fc:fc + 1],
                          src[l, fc * 128:(fc + 1) * 128].rearrange("(p o) -> p o", o=1), W=['cprm'])
            if CSTOP == 3:
                return
            for seq in job.seqs:
                if seq.prompt:
                    S.op('dve', lambda e: e.memset(c32[:, :, 0:30], 0.0), W=['c32'])
                    o_cv = o_cv_p[l, seq.idx]
                else:
                    S.dma('sp', stg[:], s_cv[l, seq.idx], W=['cstg'])
                    for fc in range(2):
                        tr_f32(PS[6][:, fc * 32:fc * 32 + 30], stg[0:30, fc * 128:(fc + 1) * 128], 30, ['cstg'],
                               [('ps', 6)])
                    S.op('dve', lambda e: e.tensor_copy(
                        c32[:, :, 0:30], PS[6][:, 0:64].rearrange("p (f j) -> p f j", j=32)[:, :, 0:30]),
                         R=[('ps', 6)], W=['c32'])
                    o_cv = o_cv_s[l, seq.idx]
                for bi, (b0, nt) in enumerate(seq.blocks):
                    cols = seq.c0 + b0
                    lastb = (bi == len(seq.blocks) - 1)
                    for fc in range(2):
                        for which in range(2):
                            pk = 2 * which + fc
                            fns = []
                            for k in range(8):
                                fns.append(lambda e, k=k, pk=pk, which=which, fc=fc: e.matmul(
                                    PS[pk][:, 0:nt], wgl[:, k, which * 256 + fc * 128:which * 256 + (fc + 1) * 128],
                                    xT[:, k, cols:cols + nt], start=(k == 0), stop=(k == 7)))
                            S.mm(fns, R=['wgl'] + xT_R(cols, nt), W=[('ps', pk)])
                        S.op('act', lambda e, fc=fc: e.activation(out=sgm[:, fc, 0:nt], in_=PS[2 + fc][:, 0:nt],
                                                                 func=AF.Sigmoid),
                             R=[('ps', 2 + fc)], W=[('sgm', fc)])
                        S.op('dve', lambda e, fc=fc: e.tensor_tensor(c32[:, fc, 30:30 + nt], PS[fc][:, 0:nt],
                                                                    sgm[:, fc, 0:nt], ALU.mult),
                             R=[('ps', fc), ('sgm', fc)], W=['c32'])
                    if CSTOP == 5:
                        return
                    S.op('act', lambda e: e.activation(out=cbf[:, :, 0:30 + nt], in_=c32[:, :, 0:30 + nt],
                                                       func=AF.Copy), R=['c32'], W=['cbf'])
                    if CSTOP == 6:
                        return
                    for fc in range(2):
                        fns = []
                        for j in range(31):
                            fns.append(lambda e, j=j, fc=fc: e.matmul(PS[4 + fc][:, 0:nt], diag[:, fc, j, :],
                                                                      cbf[:, fc, j:j + nt], start=(j == 0),
                                                                      stop=(j == 30)))
                        S.mm(fns, R=['cbf', ('diag', fc)], W=[('ps', 4 + fc)])
                        S.op('act', lambda e, fc=fc: e.activation(out=cb[:, fc, 0:nt], in_=PS[4 + fc][:, 0:nt],
                                                                 func=AF.Identity, bias=prm[:, 0, fc:fc + 1],
                                                                 scale=1.0),
                             R=[('ps', 4 + fc), 'cprm'], W=[('cb', fc)])
                        S.op('act', lambda e, fc=fc: e.activation(out=sq[:, fc, 0:nt], in_=cb[:, fc, 0:nt],
                                                                 func=AF.Square), R=[('cb', fc)], W=[('csq', fc)])
                    if CSTOP == 7:
                        return
                    S.mm([lambda e, fc=fc: e.matmul(PS[6][:, 0:nt], ONES[:, :], cb[:, fc, 0:nt], start=(fc == 0),
                                                    stop=(fc == 1)) for fc in range(2)],
                         R=[('cb', 0), ('cb', 1), 'ONES'], W=[('ps', 6)])
                    S.mm([lambda e, fc=fc: e.matmul(PS[7][:, 0:nt], ONES[:, :], sq[:, fc, 0:nt], start=(fc == 0),
                                                    stop=(fc == 1)) for fc in range(2)],
                         R=[('csq', 0), ('csq', 1), 'ONES'], W=[('ps', 7)])
                    if CSTOP == 8:
                        return
                    S.op('act', lambda e: e.activation(out=mean[:, 0:nt], in_=PS[6][:, 0:nt], func=AF.Identity,
                                                       bias=0.0, scale=1.0 / 256), R=[('ps', 6)], W=['cmean'])
                    S.op('dve', lambda e: e.tensor_tensor(msq[:, 0:nt], mean[:, 0:nt], mean[:, 0:nt], ALU.mult),
                         R=['cmean'], W=['cmsq'])
                    S.op('dve', lambda e: e.scalar_tensor_tensor(out=var[:, 0:nt], in0=PS[7][:, 0:nt],
                                                                 scalar=1.0 / 256, in1=msq[:, 0:nt], op0=ALU.mult,
                                                                 op1=ALU.subtract),
                         R=[('ps', 7), 'cmsq'], W=['cvar'])
                    rsqrt_act(rstd[:, 0:nt], var[:, 0:nt], 1.0, 1e-5, ['cvar'], ['crstd'], ltmp[:, 0:nt], 'cltmp')
                    if CSTOP == 9:
                        return
                    for fc in range(2):
                        S.op('dve', lambda e, fc=fc: e.tensor_tensor(tt[:, fc, 0:nt], cb[:, fc, 0:nt], mean[:, 0:nt],
                                                                    ALU.subtract),
                             R=[('cb', fc), 'cmean'], W=[('ctt', fc)])
                        S.op('dve', lambda e, fc=fc: e.tensor_tensor(tt[:, fc, 0:nt], tt[:, fc, 0:nt], rstd[:, 0:nt],
                                                                    ALU.mult),
                             R=['crstd'], W=[('ctt', fc)])
                        S.op('act', lambda e, fc=fc: e.activation(out=mixB[:, 2 + fc, cols:cols + nt],
                                                                 in_=tt[:, fc, 0:nt], func=AF.Silu,
                                                                 bias=prm[:, 2, fc:fc + 1], scale=prm[:, 1, fc:fc + 1]),
                             R=[('ctt', fc), 'cprm'], W=[('mixB', t) for t in tiles_of(cols, nt)])
                    if CSTOP == 10:
                        return
                    if lastb:
                        for fc in range(2):
                            tr_f32(PS[4][0:30, fc * 128:(fc + 1) * 128], c32[:, fc, nt:nt + 30], 128, ['c32'],
                                   [('ps', 4)])
                        S.op('dve', lambda e: e.tensor_copy(stg[0:30, :], PS[4][0:30, 0:256]), R=[('ps', 4)],
                             W=['cstg'])
                        S.dma('sp', o_cv, stg[0:30, :], R=['cstg'], W=['o_cv'])
                    else:
                        S.op('act', lambda e: e.activation(out=c32[:, :, 0:30], in_=c32[:, :, nt:nt + 30],
                                                           func=AF.Copy), R=['c32'], W=['c32'])


        MSTOP = int(os.environ.get('KMSTOP', '0'))
        KVAR = int(os.environ.get('KVAR', '0'))

        def mla_proj_phase_real(st, job, l, QT, ckvT, kpeT, ckvN, kpeN, rope_src):
            nt_tiles = len(job.tiles)
            wm = sb(st, "wm", [128, 8, 672], BF16)
            wq = sb(st, "wq", [128, 3, 768], BF16)
            gq_b = sb(st, "gq_b", [128, 384])
            gk_b = sb(st, "gk_b", [128, 256])
            rope_t = sb(st, "rope_t", [128, nt_tiles, 32])
            S.dma('pool', wm[:], w_in[l, :, 0:672].rearrange("(k p) c -> p k c", p=128), W=['wm'])
            S.dma('pool', wq[:], w_uq[l].rearrange("(k p) c -> p k c", p=128), W=['wq'])
            S.dma('sp', gq_b[:], q_norm[l:l + 1, :].partition_broadcast(128), W=['gq_b'])
            S.dma('sp', gk_b[:], kv_norm[l:l + 1, :].partition_broadcast(128), W=['gk_b'])
            if job.prompt:
                S.dma('sp', rope_t[:], rope_src.rearrange("(t p) c -> p t c", p=128), W=['rope_t'])
                o_ckv = o_ckv_p[l, job.pidx]
                o_kpe = o_kpe_p[l, job.pidx]
            else:
                S.dma('sp', rope_t[0:job.T, 0, :], rope_src, W=['rope_t'])
                o_ckv = o_ckv_s[l].rearrange("s t c -> (s t) c")
                o_kpe = o_kpe_s[l].rearrange("s t c -> (s t) c")
            if MSTOP == 1:
                return
            tl = []
            for i in range(2):
                tl.append(dict(
                    sqt=sb(st, "sqt%d" % i, [128, 384]), ssq=sb(st, "ssq%d" % i, [128, 2]),
                    rr=sb(st, "rr%d" % i, [128, 4]), cqn=sb(st, "cqn%d" % i, [128, 384], BF16),
                    ckv32=sb(st, "ckv32%d" % i, [128, 256]), ckvb=sb(st, "ckvb%d" % i, [128, 256], BF16),
                    kpe32=sb(st, "kpe32%d" % i, [128, 32]), kpeb=sb(st, "kpeb%d" % i, [128, 32], BF16),
                    rt=sb(st, "rt%d" % i, [128, 4, 16]), cqT=sb(st, "cqT%d" % i, [128, 3, 128], BF16),
                    qb=sb(st, "qb%d" % i, [128, 8, 96], BF16), qt=sb(st, "qt%d" % i, [128, 4, 4, 16])))
            for ti, (r0, tsz) in enumerate(job.tiles):
                b = ti % 2
                t = tl[b]
                A, B, Q0, Q1 = 4 * b, 4 * b + 1, 4 * b + 2, 4 * b + 3
                kA, kB = ('ps', A), ('ps', B)
                tg = 'm%d' % b
                for (pk, c0w, c1w) in ((A, 0, 384), (B, 384, 672)):
                    fns = []
                    for k in range(8):
                        fns.append(lambda e, k=k, pk=pk, c0w=c0w, c1w=c1w: e.matmul(
                            PS[pk][0:tsz, 0:c1w - c0w], xT[:, k, r0:r0 + tsz], wm[:, k, c0w:c1w], start=(k == 0),
                            stop=(k == 7)))
                    S.mm(fns, R=['wm', ('xT', ti, 0), ('xT', ti, 1)], W=[('ps', pk)])
                if MSTOP == 2:
                    return
                S.op('act', lambda e: e.activation(out=t['sqt'][0:tsz, 0:384], in_=PS[A][0:tsz, 0:384], func=AF.Square),
                     R=[kA], W=[tg + 'sqt'])
                S.op('dve', lambda e: e.tensor_reduce(out=t['ssq'][0:tsz, 0:1], in_=t['sqt'][0:tsz, 0:384], axis=AX.X,
                                                      op=ALU.add), R=[tg + 'sqt'], W=[tg + 'ssq0'])
                S.op('act', lambda e: e.activation(out=t['sqt'][0:tsz, 0:256], in_=PS[B][0:tsz, 0:256], func=AF.Square),
                     R=[kB, tg + 'ssq0'], W=[tg + 'sqt'])
                S.op('dve', lambda e: e.tensor_reduce(out=t['ssq'][0:tsz, 1:2], in_=t['sqt'][0:tsz, 0:256], axis=AX.X,
                                                      op=ALU.add), R=[tg + 'sqt'], W=[tg + 'ssq1'])
                rsqrt_act(t['rr'][0:tsz, 0:1], t['ssq'][0:tsz, 0:1], 1.0 / 384, 1e-6, [tg + 'ssq0'], [tg + 'rr0'],
                          t['rr'][0:tsz, 2:3], tg + 'rr2')
                rsqrt_act(t['rr'][0:tsz, 1:2], t['ssq'][0:tsz, 1:2], 1.0 / 256, 1e-6, [tg + 'ssq1'], [tg + 'rr1'],
                          t['rr'][0:tsz, 3:4], tg + 'rr3')
                S.op('dve', lambda e: e.scalar_tensor_tensor(out=t['cqn'][0:tsz, :], in0=PS[A][0:tsz, 0:384],
                                                             scalar=t['rr'][0:tsz, 0:1], in1=gq_b[0:tsz, :],
                                                             op0=ALU.mult, op1=ALU.mult),
                     R=[kA, tg + 'rr0', 'gq_b'], W=[tg + 'cqn'])
                S.op('dve', lambda e: e.scalar_tensor_tensor(out=t['ckv32'][0:tsz, :], in0=PS[B][0:tsz, 0:256],
                                                             scalar=t['rr'][0:tsz, 1:2], in1=gk_b[0:tsz, :],
                                                             op0=ALU.mult, op1=ALU.mult),
                     R=[kB, tg + 'rr1', 'gk_b'], W=[tg + 'ckv32'])
                S.dma('sp', o_ckv[r0:r0 + tsz, :], t['ckv32'][0:tsz, :], R=[tg + 'ckv32'], W=[('o_ckv', ti)])
                S.op('act', lambda e: e.activation(out=t['ckvb'][0:tsz, :], in_=t['ckv32'][0:tsz, :], func=AF.Copy),
                     R=[tg + 'ckv32'], W=[tg + 'ckvb'])
                if MSTOP == 3:
                    return
                cos = rope_t[0:tsz, ti, 0:16]
                sin = rope_t[0:tsz, ti, 16:32]
                x1 = PS[B][0:tsz, 256:272]
                x2 = PS[B][0:tsz, 272:288]
                rt = t['rt']
                for idx, (xa, cs) in enumerate(((x1, cos), (x2, sin), (x1, sin), (x2, cos))):
                    S.op('dve', lambda e, idx=idx, xa=xa, cs=cs: e.tensor_tensor(rt[0:tsz, idx, :], xa, cs, ALU.mult),
                         R=[kB, 'rope_t'], W=[(tg + 'rt', idx)])
                S.op('dve', lambda e: e.tensor_tensor(t['kpe32'][0:tsz, 0:16], rt[0:tsz, 0, :], rt[0:tsz, 1, :],
                                                      ALU.subtract),
                     R=[(tg + 'rt', 0), (tg + 'rt', 1)], W=[tg + 'kpe32a'])
                S.op('dve', lambda e: e.tensor_tensor(t['kpe32'][0:tsz, 16:32], rt[0:tsz, 2, :], rt[0:tsz, 3, :],
                                                      ALU.add),
                     R=[(tg + 'rt', 2), (tg + 'rt', 3)], W=[tg + 'kpe32b'])
                S.dma('sp', o_kpe[r0:r0 + tsz, :], t['kpe32'][0:tsz, :], R=[tg + 'kpe32a', tg + 'kpe32b'],
                      W=[('o_kpe', ti)])
                S.op('act', lambda e: e.activation(out=t['kpeb'][0:tsz, :], in_=t['kpe32'][0:tsz, :], func=AF.Copy),
                     R=[tg + 'kpe32a', tg + 'kpe32b'], W=[tg + 'kpeb'])
                if MSTOP == 4:
                    return
                S.mm([lambda e, k3=k3: e.matmul(PS[A][:, k3 * 128:k3 * 128 + tsz],
                                                t['cqn'][0:tsz, k3 * 128:(k3 + 1) * 128], IDB[0:tsz, 0:tsz],
                                                start=True, stop=True) for k3 in range(3)],
                     R=[tg + 'cqn', 'CBb'], W=[kA])
                if KVAR == 2:
                    return
                S.op('act', lambda e: e.activation(
                    out=t['cqT'][:, :, 0:tsz],
                    in_=PS[A][:, 0:384].rearrange("p (k t) -> p k t", t=128)[:, :, 0:tsz], func=AF.Copy),
                     R=[kA], W=[tg + 'cqT'])
                if KVAR == 3:
                    return
                fns = [lambda e, c=c: e.matmul(PS[B][:, c * 128:c * 128 + tsz], t['ckvb'][0:tsz, c * 128:(c + 1) * 128],
                                               IDB[0:tsz, 0:tsz], start=True, stop=True) for c in range(2)]
                if KVAR != 1:
                    fns.append(lambda e: e.matmul(PS[B][0:32, 256:256 + tsz], t['kpeb'][0:tsz, 0:32], IDB[0:tsz, 0:tsz],
                                                  start=True, stop=True))
                S.mm(fns, R=[tg + 'ckvb', tg + 'kpeb', 'CBb'], W=[kB])
                if KVAR == 4:
                    return
                cdst = ckvT if job.prompt else ckvN
                kdst = kpeT if job.prompt else kpeN
                if KVAR != 6:
                    S.op('dve', lambda e: e.tensor_copy(
                        cdst[:, :, r0:r0 + tsz], PS[B][:, 0:256].rearrange("p (k t) -> p k t", t=128)[:, :, 0:tsz]),
                         R=[kB], W=[('ckvT', ti)])
                if KVAR != 5:
                    S.op('act', lambda e: e.activation(out=kdst[0:32, r0:r0 + tsz], in_=PS[B][0:32, 256:256 + tsz],
                                                       func=AF.Copy), R=[kB], W=[('kpeT', ti)])
                if MSTOP == 5:
                    return
                for b2, pk in enumerate((Q0, Q1)):
                    S.mm([lambda e, k=k, pk=pk, b2=b2: e.matmul(PS[pk][0:tsz, 0:384], t['cqT'][:, k, 0:tsz],
                                                                wq[:, k, b2 * 384:(b2 + 1) * 384], start=(k == 0),
                                                                stop=(k == 2)) for k in range(3)],
                         R=[tg + 'cqT', 'wq'], W=[('ps', pk)])
                    view = PS[pk][0:tsz, 0:384].rearrange("p (h e) -> p h e", e=96)
                    qb = t['qb']
                    qt = t['qt']
                    S.op('act', lambda e, view=view, b2=b2: e.activation(out=qb[0:tsz, 4 * b2:4 * b2 + 4, 0:64],
                                                                        in_=view[:, :, 0:64], func=AF.Copy),
                         R=[('ps', pk)], W=[(tg + 'qb', b2, 0)])
                    cosb = cos.unsqueeze(1).to_broadcast([tsz, 4, 16])
                    sinb = sin.unsqueeze(1).to_broadcast([tsz, 4, 16])
                    xx1 = view[:, :, 64:80]
                    xx2 = view[:, :, 80:96]
                    for idx, (xa, cs) in enumerate(((xx1, cosb), (xx2, sinb), (xx1, sinb), (xx2, cosb))):
                        S.op('dve', lambda e, idx=idx, xa=xa, cs=cs: e.tensor_tensor(qt[0:tsz, idx, :, :], xa, cs,
                                                                                  ALU.mult),
                             R=[('ps', pk), 'rope_t'], W=[(tg + 'qt', idx)])
                    S.op('dve', lambda e, b2=b2: e.tensor_tensor(qb[0:tsz, 4 * b2:4 * b2 + 4, 64:80], qt[0:tsz, 0, :, :],
                                                                qt[0:tsz, 1, :, :], ALU.subtract),
                         R=[(tg + 'qt', 0), (tg + 'qt', 1)], W=[(tg + 'qb', b2, 1)])
                    S.op('dve', lambda e, b2=b2: e.tensor_tensor(qb[0:tsz, 4 * b2:4 * b2 + 4, 80:96], qt[0:tsz, 2, :, :],
                                                                qt[0:tsz, 3, :, :], ALU.add),
                         R=[(tg + 'qt', 2), (tg + 'qt', 3)], W=[(tg + 'qb', b2, 2)])
                    if MSTOP == 6:
                        return
                    S.mm([lambda e, hh=hh, pk=pk, b2=b2: e.matmul(PS[pk][0:96, hh * 128:hh * 128 + tsz],
                                                                  qb[0:tsz, 4 * b2 + hh, :], IDB[0:tsz, 0:tsz],
                                                                  start=True, stop=True) for hh in range(4)],
                         R=[(tg + 'qb', b2, 0), (tg + 'qb', b2, 1), (tg + 'qb', b2, 2), 'CBb'], W=[('ps', pk)])
                    S.op('act' if b2 == 0 else 'dve', lambda e, b2=b2, pk=pk: (
                        e.activation(out=QT[0:96, 4 * b2:4 * b2 + 4, r0:r0 + tsz],
                                     in_=PS[pk][0:96, :].rearrange("p (h t) -> p h t", t=128)[:, :, 0:tsz],
                                     func=AF.Copy) if b2 == 0 else
                        e.tensor_copy(QT[0:96, 4 * b2:4 * b2 + 4, r0:r0 + tsz],
                                      PS[pk][0:96, :].rearrange("p (h t) -> p h t", t=128)[:, :, 0:tsz])),
                         R=[('ps', pk)], W=[('QT', ti, b2)])

        def attn_phase_real(st, job, l, QT, ckvT, kpeT, ckvN, kpeN):
            Tk = job.Tk
            NKT = (Tk + 127) // 128
            wk = sb(st, "wk", [128, 2, 8, 96], BF16)
            wv = sb(st, "wv", [128, 2, 8, 64], BF16)
            KT = [sb(st, "KT%d" % i, [96, Tk], BF16) for i in range(2)]
            VPall = sb(st, "VPall", [128, NKT, 8, 128], BF16)
            PT = [sb(st, "PT%d" % i, [128, 512], BF16) for i in range(3)]
            rs = [sb(st, "rs%d" % i, [128, 512]) for i in range(2)]
            S.op('dve', lambda e: e.memset(wk[:], 0.0), W=['wk'])
            for c in range(2):
                S.dma('pool', wk[:, c, :, 0:64], w_uk[l, c * 128:(c + 1) * 128], W=['wk'])
                S.dma('pool', wv[:, c, :, :], w_uv[l, c * 128:(c + 1) * 128], W=['wv'])
            for kt in range(NKT):
                S.op('dve', lambda e, kt=kt: e.memset(VPall[:, kt, :, 64:128], 1.0), W=['VP1'])
            if not job.prompt:
                cst = sb(st, "cst", [128, 16, 256], BF16)
                cpt = sb(st, "cpt", [128, 16, 32], BF16)
                ckvT = sb(st, "ckvTs", [128, 2, Tk], BF16)
                kpeT = sb(st, "kpeTs", [32, Tk], BF16)
            hcount = 0
            pocount = 0
            for seq in job.seqs:
                if not seq.prompt:
                    S.dma('pool', cst[:], c_ckv[l, seq.idx].rearrange("(t p) c -> p t c", p=128), W=['cst'])
                    S.dma('pool', cpt[:], c_kpe[l, seq.idx].rearrange("(t p) c -> p t c", p=128), W=['cpt'])
                    for t4 in range(4):
                        for c in range(2):
                            S.mm([lambda e, tt=tt, c=c, t4=t4: e.matmul(PS[7][:, tt * 128:(tt + 1) * 128],
                                                                        cst[:, t4 * 4 + tt, c * 128:(c + 1) * 128],
                                                                        IDB, start=True, stop=True) for tt in range(4)],
                                 R=['cst', 'CBb'], W=[('ps', 7)])
                            S.op('act', lambda e, c=c, t4=t4: e.activation(out=ckvT[:, c, t4 * 512:(t4 + 1) * 512],
                                                                          in_=PS[7][:, 0:512], func=AF.Copy),
                                 R=[('ps', 7)], W=['ckvT'])
                        S.mm([lambda e, tt=tt, t4=t4: e.matmul(PS[0][0:32, tt * 128:(tt + 1) * 128],
                                                               cpt[:, t4 * 4 + tt, 0:32], IDB, start=True, stop=True)
                              for tt in range(4)], R=['cpt', 'CBb'], W=[('ps', 0)])
                        S.op('dve', lambda e, t4=t4: e.tensor_copy(kpeT[0:32, t4 * 512:(t4 + 1) * 512],
                                                                   PS[0][0:32, 0:512]),
                             R=[('ps', 0)], W=['kpeT'])
                    S.op('dve', lambda e: e.tensor_copy(ckvT[:, :, PAST:PAST + DSEQ],
                                                        ckvN[:, :, seq.c0:seq.c0 + DSEQ]), W=['ckvT'])
                    S.op('dve', lambda e: e.tensor_copy(kpeT[0:32, PAST:PAST + DSEQ],
                                                        kpeN[0:32, seq.c0:seq.c0 + DSEQ]), W=['kpeT'])
                for kt in range(NKT):
                    ksz = min(128, Tk - kt * 128)
                    pk = kt % 2
                    S.mm([lambda e, c=c: e.matmul(PS[pk][0:ksz, 0:512], ckvT[:, c, kt * 128:kt * 128 + ksz],
                                                  wv[:, c, :, :].rearrange("p h e -> p (h e)"), start=(c == 0),
                                                  stop=(c == 1)) for c in range(2)],
                         R=['wv', 'ckvT'], W=[('ps', pk)])
                    S.op('dve' if kt % 2 == 0 else 'act', lambda e: (
                        e.tensor_copy(VPall[0:ksz, kt, :, 0:64],
                                      PS[pk][0:ksz, 0:512].rearrange("p (h e) -> p h e", e=64))
                        if kt % 2 == 0 else
                        e.activation(out=VPall[0:ksz, kt, :, 0:64],
                                     in_=PS[pk][0:ksz, 0:512].rearrange("p (h e) -> p h e", e=64), func=AF.Copy)),
                         R=[('ps', pk)], W=['VP'])
                for h in range(H):
                    hb = hcount % 2
                    hcount += 1
                    kt_ = KT[hb]
                    vp_ = VPall[:, :, h, :]
                    for bi, kb0 in enumerate(range(0, Tk, 512)):
                        nk = min(512, Tk - kb0)
                        pk = bi % 2
                        S.mm([lambda e: e.matmul(PS[pk][0:96, 0:nk], wk[:, 0, h, :], ckvT[:, 0, kb0:kb0 + nk],
                                                 start=True, stop=False),
                              lambda e: e.matmul(PS[pk][0:96, 0:nk], wk[:, 1, h, :], ckvT[:, 1, kb0:kb0 + nk],
                                                 start=False, stop=False),
                              lambda e: e.matmul(PS[pk][0:96, 0:nk], E32[:, :], kpeT[0:32, kb0:kb0 + nk],
                                                 start=False, stop=True)],
                             R=['wk', 'ckvT', 'kpeT', 'E32'], W=[('ps', pk)])
                        S.op('act', lambda e: e.activation(out=kt_[0:96, kb0:kb0 + nk], in_=PS[pk][0:96, 0:nk],
                                                           func=AF.Copy), R=[('ps', pk)], W=[('KT', hb)])
                    for (q0, nq) in seq.blocks:
                        jq0 = seq.c0 + q0
                        po = 5 + (pocount % 2)
                        pocount += 1
                        if seq.prompt:
                            kts = list(range((q0 + nq) // 128))
                        else:
                            kts = list(range(NKT))
                        pend = None
                        nk_ = len(kts)
                        if not seq.prompt:
                            psb = 2
                            pt_ = PT[0]
                            kszs = [min(128, Tk - kt * 128) for kt in kts]
                            S.mm([lambda e, kt=kt, ksz=ksz: e.matmul(PS[psb][0:ksz, kt * nq:(kt + 1) * nq],
                                                                     kt_[0:96, kt * 128:kt * 128 + ksz],
                                                                     QT[0:96, h, jq0:jq0 + nq], start=True, stop=True)
                                  for kt, ksz in zip(kts, kszs)],
                                 R=[('KT', hb)] + [('QT', t, h // 4) for t in tiles_of(jq0, nq)], W=[('ps', psb)])
                            nfull = len([k_ for k_ in kszs if k_ == 128])
                            S.op('act', lambda e: e.activation(out=pt_[:, 0:nfull * nq], in_=PS[psb][:, 0:nfull * nq],
                                                               func=AF.Exp, scale=MLA_SCALE),
                                 R=[('ps', psb)], W=[('PT', 0)])
                            for kt, ksz in list(zip(kts, kszs))[nfull:]:
                                S.op('act', lambda e, kt=kt, ksz=ksz: e.activation(
                                    out=pt_[0:ksz, kt * nq:(kt + 1) * nq], in_=PS[psb][0:ksz, kt * nq:(kt + 1) * nq],
                                    func=AF.Exp, scale=MLA_SCALE), R=[('ps', psb)], W=[('PT', 0)])
                            S.mm([lambda e, i=i, kt=kt, ksz=ksz: e.matmul(PS[po][:, 0:nq], vp_[0:ksz, kt, :],
                                                                          pt_[0:ksz, kt * nq:(kt + 1) * nq],
                                                                          start=(i == 0), stop=(i == nk_ - 1))
                                  for i, (kt, ksz) in enumerate(zip(kts, kszs))],
                                 R=['VP', 'VP1', ('PT', 0)], W=[('ps', po)])
                            kts = []
                        for i, kt in enumerate(kts):
                            ksz = min(128, Tk - kt * 128)
                            qs = max(q0, kt * 128) if seq.prompt else q0
                            w = q0 + nq - qs
                            psb = 2 + (i % 3)
                            pt_ = PT[i % 3]
                            S.mm([lambda e: e.matmul(PS[psb][0:ksz, 0:w], kt_[0:96, kt * 128:kt * 128 + ksz],
                                                     QT[0:96, h, seq.c0 + qs:seq.c0 + qs + w], start=True, stop=True)],
                                 R=[('KT', hb)] + [('QT', t, h // 4) for t in tiles_of(seq.c0 + qs, w)],
                                 W=[('ps', psb)])
                            S.op('act', lambda e: e.activation(out=pt_[0:ksz, 0:w], in_=PS[psb][0:ksz, 0:w],
                                                               func=AF.Exp, scale=MLA_SCALE),
                                 R=[('ps', psb)], W=[('PT', i % 3)])
                            if seq.prompt and qs == kt * 128:
                                S.op('dve', lambda e: e.memset(pt_[64:128, 0:64], 0.0), W=[('PT', i % 3)])
                            if pend is not None:
                                pend()
                            def pv(i=i, kt=kt, ksz=ksz, qs=qs, w=w, pt_=pt_):
                                S.mm([lambda e: e.matmul(PS[po][:, qs - q0:qs - q0 + w], vp_[0:ksz, kt, :],
                                                         pt_[0:ksz, 0:w], start=(i == 0), stop=(i == nk_ - 1))],
                                     R=['VP', 'VP1', ('PT', i % 3)], W=[('ps', po)])
                            pend = pv
                        if pend is not None:
                            pend()
                        rs_ = rs[pocount % 2]
                        S.op('dve', lambda e: e.reciprocal(rs_[64:128, 0:nq], PS[po][64:128, 0:nq]),
                             R=[('ps', po)], W=[('rs', pocount % 2)])
                        pr = (h % 2) * 64
                        S.op('dve', lambda e: e.tensor_tensor(xT[pr:pr + 64, h // 2, jq0:jq0 + nq], PS[po][0:64, 0:nq],
                                                              rs_[64:128, 0:nq], ALU.mult),
                             R=[('ps', po), ('rs', pocount % 2)], W=[('xT', t, 0) for t in tiles_of(jq0, nq)])


        GSTOP = int(os.environ.get('KGSTOP', '0'))

        def gdn_phase_real(st, job, l, mixB):
            if GSTOP:
                stub_zero(st, job, mixB)
            wg = sb(st, "wg", [128, 8, 1032], BF16)
            gcw = sb(st, "gcw", [128, 6, 4])
            gst = sb(st, "gst", [4, 768])
            gn2 = sb(st, "gn2", [128, 1])
            dtb = sb(st, "dtb", [64, 4])
            negA = sb(st, "negA", [64, 4])
            raw = sb(st, "raw", [128, 6, 3 + 512])
            acc = [sb(st, "gacc%d" % i, [128, 512]) for i in range(2)]
            s32 = [sb(st, "s32%d" % i, [128, 512]) for i in range(2)]
            sq32 = sb(st, "sq32", [128, 512])
            rn = sb(st, "grn", [128, 512])
            rtmp = sb(st, "grtmp", [128, 512])
            qk64 = sb(st, "qk64", [64, 8, 512], BF16)
            vT = sb(st, "vT", [128, 2, 512], BF16)
            zsT = sb(st, "zsT", [128, 2, 512], BF16)
            Sst = sb(st, "Sst", [64, 4, 64])
            Sb = sb(st, "Sb", [64, 4, 64], BF16)
            beta = sb(st, "beta", [64, 32])
            nbeta = sb(st, "nbeta", [64, 32])
            t4 = sb(st, "t4", [64, 32])
            gal = sb(st, "gal", [64, 32])
            Gs = sb(st, "Gs", [64, 32])
            gam = sb(st, "gam", [64, 32])
            ee = sb(st, "ee", [64, 32])
            bg = sb(st, "bg", [64, 32])
            gamL = sb(st, "gamL", [128, 32])
            CH = []
            for bs_ in range(3):
                n_ = lambda x: "%s%d" % (x, bs_)
                CH.append((sb(st, n_("kbg"), [64, 4, 64], BF16), sb(st, n_("ke"), [64, 4, 64], BF16),
                           sb(st, n_("vb"), [64, 4, 64], BF16), sb(st, n_("gTri"), [64, 4, 64]),
                           sb(st, n_("D0"), [64, 4, 64]), sb(st, n_("Dn"), [64, 4, 64]), sb(st, n_("Dp"), [64, 4, 64]),
                           sb(st, n_("decT"), [64, 4, 64]), sb(st, n_("decA"), [64, 4, 64]),
                           sb(st, n_("gamB"), [128, 4, 64]), sb(st, n_("qgT"), [64, 4, 64], BF16),
                           [sb(st, n_("BC%d" % i), [64, 8, 64], BF16) for i in range(2)],
                           sb(st, n_("Pm"), [64, 4, 64], BF16), sb(st, n_("MTd"), [64, 4, 64], BF16),
                           sb(st, n_("nWT"), [64, 4, 64], BF16), sb(st, n_("Ub"), [64, 4, 64], BF16)))
            NPAR = int(os.environ.get("KNPAR", "3"))
            osb = sb(st, "osb", [64, 8, 256])
            sqo = sb(st, "sqo", [64, 8, 256])
            on_ = sb(st, "on_", [64, 8, 256], BF16)
            sso = sb(st, "sso", [64, 32])
            rno = sb(st, "rno", [64, 32])
            rtm = sb(st, "rtm", [64, 32])

            for (ca, cb_) in ((0, 512), (512, 1032)):
                S.dma('pool', wg[:, :, ca:cb_], w_in[l, :, 672 + ca:672 + cb_].rearrange("(k p) c -> p k c", p=128),
                      W=['wg'])
            S.dma('sp', gst[0:4, :], gcw_d[l], W=['gst'])
            for fc in range(6):
                tr_f32(PS[1][:, fc * 4:fc * 4 + 4], gst[0:4, fc * 128:(fc + 1) * 128], 4, ['gst'], [('ps', 1)])
            S.op('dve', lambda e: e.tensor_copy(gcw[:, :, :], PS[1][:, 0:24].rearrange("p (f j) -> p f j", j=4)),
                 R=[('ps', 1)], W=['gcw'])
            for hf in range(2):
                S.dma('sp', gn2[hf * 64:(hf + 1) * 64, :], gdn_norm[l, :].rearrange("(p o) -> p o", o=1), W=['gn2'])
            S.dma('sp', dtb[:], dt_bias[l:l + 1, :].partition_broadcast(64), W=['dtb'])
            S.dma('sp', negA[:], a_log[l:l + 1, :].partition_broadcast(64), W=['negA'])
            S.op('act', lambda e: e.activation(out=negA[:], in_=negA[:], func=AF.Exp), W=['negA'])
            S.op('dve', lambda e: e.tensor_scalar(negA[:], negA[:], -1.0, None, ALU.mult), W=['negA'])

            if GSTOP == 1:
                return
            for seq in job.seqs:
                L = seq.L
                L4 = 4 * L
                nlev = {64: 5, 16: 3}[L]
                if seq.prompt:
                    S.op('dve', lambda e: e.memset(Sst[:], 0.0), W=['Sst'])
                    S.op('dve', lambda e: e.memset(raw[:, :, 0:3], 0.0), W=['raw'])
                    o_gdn = o_gdn_p[l, seq.idx]
                    o_gcv = o_gcv_p[l, seq.idx]
                else:
                    S.dma('sp', Sst[:], s_gdn[l, seq.idx].rearrange("h d e -> d h e"), W=['Sst'])
                    S.dma('sp', gst[0:3, :], s_gcv[l, seq.idx], W=['gst'])
                    for fc in range(6):
                        tr_f32(PS[1][:, fc * 4:fc * 4 + 3], gst[0:3, fc * 128:(fc + 1) * 128], 3, ['gst'], [('ps', 1)])
                    S.op('dve', lambda e: e.tensor_copy(raw[:, :, 0:3],
                                                        PS[1][:, 0:24].rearrange("p (f j) -> p f j", j=4)[:, :, 0:3]),
                         R=[('ps', 1)], W=['raw'])
                    o_gdn = o_gdn_s[l, seq.idx]
                    o_gcv = o_gcv_s[l, seq.idx]
                S.op('act', lambda e: e.activation(out=Sb[:], in_=Sst[:], func=AF.Copy), R=['Sst'], W=['Sb'])
                for bi, (b0, nt) in enumerate(seq.blocks):
                    cols = seq.c0 + b0
                    lastb = (bi == len(seq.blocks) - 1)
                    nch = nt // L
                    n4 = nch * 4
                    xr = xT_R(cols, nt)
                    QKR = [('qkT', f_, h_) for f_ in range(4) for h_ in range(2)]
                    for fc in range(6):
                        pk = fc % 2
                        S.mm([lambda e, k=k, fc=fc, pk=pk: e.matmul(PS[pk][:, 0:nt], wg[:, k, fc * 128:(fc + 1) * 128],
                                                                    xT[:, k, cols:cols + nt], start=(k == 0),
                                                                    stop=(k == 7)) for k in range(8)],
                             R=['wg'] + xr, W=[('ps', pk)])
                        S.op('act', lambda e, fc=fc, pk=pk: e.activation(out=raw[:, fc, 3:3 + nt], in_=PS[pk][:, 0:nt],
                                                                        func=AF.Copy), R=[('ps', pk)], W=['raw'])
                    if GSTOP == 2:
                        return
                    for fc in range(6):
                        a_ = acc[fc % 2]
                        ak = ('gacc', fc % 2)
                        S.op('dve', lambda e, fc=fc, a_=a_: e.tensor_scalar(a_[:, 0:nt], raw[:, fc, 0:nt],
                                                                           gcw[:, fc, 0:1], None, ALU.mult),
                             R=['raw', 'gcw'], W=[ak])
                        for j in range(1, 4):
                            S.op('dve', lambda e, fc=fc, a_=a_, j=j: e.scalar_tensor_tensor(
                                out=a_[:, 0:nt], in0=raw[:, fc, j:j + nt], scalar=gcw[:, fc, j:j + 1], in1=a_[:, 0:nt],
                                op0=ALU.mult, op1=ALU.add), R=['raw', 'gcw'], W=[ak])
                        if fc >= 4:
                            S.op('act', lambda e, fc=fc, a_=a_: e.activation(out=vT[:, fc - 4, 0:nt], in_=a_[:, 0:nt],
                                                                            func=AF.Silu), R=[ak], W=['vT'])
                            continue
                        s_ = s32[fc % 2]
                        sk = ('s32', fc % 2)
                        S.op('act', lambda e, a_=a_, s_=s_: e.activation(out=s_[:, 0:nt], in_=a_[:, 0:nt], func=AF.Silu),
                             R=[ak], W=[sk])
                        S.op('act', lambda e, s_=s_: e.activation(out=sq32[:, 0:nt], in_=s_[:, 0:nt], func=AF.Square),
                             R=[sk], W=['sq32'])
                        pk = fc % 2
                        S.mm([lambda e, pk=pk: e.matmul(PS[pk][:, 0:nt], BONES, sq32[:, 0:nt], start=True, stop=True)],
                             R=['sq32', 'CB'], W=[('ps', pk)])
                        rsqrt_act(rn[:, 0:nt], PS[pk][:, 0:nt], 1.0, 1e-6, [('ps', pk)], ['grn'], rtmp[:, 0:nt], 'grtmp')
                        cq_ = 0.125 if fc < 2 else 1.0
                        for half in range(2):
                            hp = slice(half * 64, half * 64 + 64)
                            S.op('dve', lambda e, fc=fc, s_=s_, cq_=cq_, hp=hp, half=half: e.scalar_tensor_tensor(
                                out=qk64[0:64, fc * 2 + half, 0:nt], in0=s_[hp, 0:nt], scalar=cq_, in1=rn[hp, 0:nt],
                                op0=ALU.mult, op1=ALU.mult), R=[sk, 'grn'], W=[('qkT', fc, half)])
                    if GSTOP == 3:
                        return
                    if lastb:
                        S.mm([lambda e, fc=fc: e.matmul(PS[0][0:3, fc * 128:(fc + 1) * 128], raw[:, fc, nt:nt + 3], IDF,
                                                        start=True, stop=True) for fc in range(4)],
                             R=['raw', 'CB'], W=[('ps', 0)])
                        S.mm([lambda e, fc=fc: e.matmul(PS[1][0:3, (fc - 4) * 128:(fc - 3) * 128], raw[:, fc, nt:nt + 3],
                                                        IDF, start=True, stop=True) for fc in range(4, 6)],
                             R=['raw', 'CB'], W=[('ps', 1)])
                        S.op('dve', lambda e: e.tensor_copy(gst[0:3, 0:512], PS[0][0:3, 0:512]), R=[('ps', 0)],
                             W=['gst'])
                        S.op('dve', lambda e: e.tensor_copy(gst[0:3, 512:768], PS[1][0:3, 0:256]), R=[('ps', 1)],
                             W=['gst'])
                        S.dma('sp', o_gcv, gst[0:3, :], R=['gst'], W=['o_gcv'])
                    else:
                        S.op('act', lambda e: e.activation(out=raw[:, :, 0:3], in_=raw[:, :, nt:nt + 3], func=AF.Copy),
                             R=['raw'], W=['raw'])
                    if GSTOP == 4:
                        return
                    for fc in range(2):
                        S.mm([lambda e, k=k, fc=fc: e.matmul(PS[fc][:, 0:nt], wg[:, k, 776 + fc * 128:776 + (fc + 1) * 128],
                                                             xT[:, k, cols:cols + nt], start=(k == 0), stop=(k == 7))
                              for k in range(8)], R=['wg'] + xr, W=[('ps', fc)])
                        S.op('act', lambda e, fc=fc: e.activation(out=zsT[:, fc, 0:nt], in_=PS[fc][:, 0:nt],
                                                                 func=AF.Silu), R=[('ps', fc)], W=['zsT'])
                    if GSTOP == 5:
                        return
                    for ci in range(nch):
                        cc = cols + ci * L
                        S.mm([lambda e, k=k, ci=ci, cc=cc: e.matmul(PS[1][0:L, ci * 8:(ci + 1) * 8], xT[:, k, cc:cc + L],
                                                                    wg[:, k, 768:776], start=(k == 0), stop=(k == 7))
                              for k in range(8)], R=['wg'] + xr, W=[('ps', 1)])
                    pba = PS[1][0:L, 0:nch * 8].rearrange("p (c e) -> p c e", e=8)
                    v3 = lambda tl_: tl_[0:L, 0:n4].rearrange("p (c h) -> p c h", h=4)
                    S.op('act', lambda e: e.activation(out=v3(beta), in_=pba[:, :, 0:4], func=AF.Sigmoid),
                         R=[('ps', 1)], W=['beta'])
                    S.op('dve', lambda e: e.tensor_scalar(nbeta[0:L, 0:n4], beta[0:L, 0:n4], -1.0, None, ALU.mult),
                         R=['beta'], W=['nbeta'])
                    S.op('dve', lambda e: e.tensor_tensor(v3(t4), pba[:, :, 4:8],
                                                          dtb[0:L, :].unsqueeze(1).to_broadcast([L, nch, 4]), ALU.add),
                         R=[('ps', 1), 'dtb'], W=['t4'])
                    S.op('act', lambda e: e.activation(out=t4[0:L, 0:n4], in_=t4[0:L, 0:n4], func=AF.Exp), W=['t4'])
                    S.op('act', lambda e: e.activation(out=t4[0:L, 0:n4], in_=t4[0:L, 0:n4], func=AF.Ln, bias=1.0,
                                                       scale=1.0), W=['t4'])
                    S.op('dve', lambda e: e.tensor_tensor(v3(gal), v3(t4),
                                                          negA[0:L, :].unsqueeze(1).to_broadcast([L, nch, 4]), ALU.mult),
                         R=['t4', 'negA'], W=['gal'])
                    if GSTOP == 6:
                        return
                    S.mm([lambda e: e.matmul(PS[1][0:L, 64:64 + n4], maskU(L), gal[0:L, 0:n4], start=True, stop=True),
                          lambda e: e.matmul(PS[1][:, 96:96 + n4], ONES[0:L, :], gal[0:L, 0:n4], start=True, stop=True)],
                         R=['gal', 'CB', 'ONES'], W=[('ps', 1)])
                    S.op('dve', lambda e: e.tensor_copy(Gs[0:L, 0:n4], PS[1][0:L, 64:64 + n4]), R=[('ps', 1)],
                         W=['Gs'])
                    S.op('act', lambda e: e.activation(out=gam[0:L, 0:n4], in_=PS[1][0:L, 64:64 + n4], func=AF.Exp),
                         R=[('ps', 1)], W=['gam'])
                    S.op('act', lambda e: e.activation(out=gamL[:, 0:n4], in_=PS[1][:, 96:96 + n4], func=AF.Exp),
                         R=[('ps', 1)], W=['gamL'])
                    S.op('dve', lambda e: e.tensor_tensor(ee[0:L, 0:n4], PS[1][0:L, 96:96 + n4], Gs[0:L, 0:n4],
                                                          ALU.subtract), R=[('ps', 1), 'Gs'], W=['ee'])
                    S.op('act', lambda e: e.activation(out=ee[0:L, 0:n4], in_=ee[0:L, 0:n4], func=AF.Exp), W=['ee'])
                    S.op('dve', lambda e: e.tensor_tensor(bg[0:L, 0:n4], beta[0:L, 0:n4], gam[0:L, 0:n4], ALU.mult),
                         R=['beta', 'gam'], W=['bg'])
                    if GSTOP == 7:
                        return
                    tail_turn = [0]

                    def chunk_gen(ci):
                        ps_, bs = ci % 2, ci % 3
                        x0, x1, x2 = 2 + 3 * ps_, 3 + 3 * ps_, 4 + 3 * ps_
                        X0, X1, X2 = PS[x0], PS[x1], PS[x2]
                        kbg, ke, vb, gTri, D0, Dn, Dp, decT, decA, gamB, qgT, BC, Pm, MTd, nWT, Ub = CH[bs]
                        K_ = lambda n: (n, bs)
                        lc = ci * L
                        c4 = slice(ci * 4, ci * 4 + 4)
                        b3 = lambda tl_, n=L: tl_[0:L, c4].unsqueeze(2).to_broadcast([L, 4, n])
                        fns = []
                        for hh in range(4):
                            fns.append(lambda e, hh=hh: e.matmul(X0[0:L, hh * 64:(hh + 1) * 64],
                                                                 qk64[0:64, 4 + hh, lc:lc + L], IDB[0:64, 0:64],
                                                                 start=True, stop=True))
                        for fc in range(2):
                            fns.append(lambda e, fc=fc: e.matmul(X0[0:L, 256 + fc * 128:256 + (fc + 1) * 128],
                                                                 vT[:, fc, lc:lc + L], IDB, start=True, stop=True))
                        S.mm(fns, R=QKR + ['vT', 'CBb'], W=[('ps', x0, 'a'), ('ps', x0, 'b')])
                        yield
                        ktok = X0[0:L, 0:256].rearrange("p (h d) -> p h d", d=64)
                        vtok = X0[0:L, 256:512].rearrange("p (h d) -> p h d", d=64)
                        S.op('dve', lambda e: e.tensor_tensor(kbg[0:L, :, :], ktok, b3(bg, 64), ALU.mult),
                             R=[('ps', x0, 'a'), 'bg'], W=[K_('kbg')])
                        yield
                        S.op('dve', lambda e: e.tensor_tensor(ke[0:L, :, :], ktok, b3(ee, 64), ALU.mult),
                             R=[('ps', x0, 'a'), 'ee'], W=[K_('ke')])
                        yield
                        S.op('dve', lambda e: e.tensor_tensor(vb[0:L, :, :], vtok, b3(beta, 64), ALU.mult),
                             R=[('ps', x0, 'b'), 'beta'], W=[K_('vb')])
                        yield
                        S.op('dve', lambda e: e.tensor_tensor(gTri[0:L, :, 0:L],
                                                              maskU(L).unsqueeze(1).to_broadcast([L, 4, L]), b3(gal),
                                                              ALU.mult), R=['gal', 'CB'], W=[K_('gTri')])
                        yield
                        if L == 64:
                            gtv = gTri[0:L, :, :].rearrange("p h i -> p (h i)")
                            S.mm([lambda e: e.matmul(X1[:, 0:L4], ONES[0:L, :], gtv, start=True, stop=True)],
                                 R=[K_('gTri'), 'ONES'], W=[('ps', x1, 'a')])
                            yield
                        else:
                            S.mm([lambda e, hh=hh: e.matmul(X1[:, hh * L:(hh + 1) * L], ONES[0:L, :],
                                                            gTri[0:L, hh, 0:L], start=True, stop=True)
                                  for hh in range(4)], R=[K_('gTri'), 'ONES'], W=[('ps', x1, 'a')])
                            yield
                        pgb = X1[0:L, 0:L4].rearrange("p (h i) -> p h i", i=L)
                        S.op('dve', lambda e: e.tensor_tensor(D0[0:L, :, 0:L], pgb, b3(Gs), ALU.subtract),
                             R=[('ps', x1, 'a'), 'Gs'], W=[K_('D0')])
                        yield
                        S.op('dve', lambda e: e.tensor_scalar(Dn[0:L, :, 0:L], D0[0:L, :, 0:L], 0.0, None, ALU.min),
                             R=[K_('D0')], W=[K_('Dn')])
                        yield
                        S.op('dve', lambda e: e.tensor_scalar(Dp[0:L, :, 0:L], D0[0:L, :, 0:L], 0.0, None, ALU.max),
                             R=[K_('D0')], W=[K_('Dp')])
                        yield
                        S.op('act', lambda e: e.activation(out=decT[0:L, :, 0:L], in_=Dn[0:L, :, 0:L], func=AF.Exp),
                             R=[K_('Dn')], W=[K_('decT')])
                        yield
                        S.op('dve', lambda e: e.tensor_tensor(decT[0:L, :, 0:L], decT[0:L, :, 0:L],
                                                              maskU(L).unsqueeze(1).to_broadcast([L, 4, L]), ALU.mult),
                             R=['CB'], W=[K_('decT')])
                        yield
                        S.op('act', lambda e: e.activation(out=decA[0:L, :, 0:L], in_=Dp[0:L, :, 0:L], func=AF.Exp,
                                                           scale=-1.0), R=[K_('Dp')], W=[K_('decA')])
                        yield
                        S.op('dve', lambda e: e.tensor_tensor(decA[0:L, :, 0:L], decA[0:L, :, 0:L],
                                                              maskLs(L).unsqueeze(1).to_broadcast([L, 4, L]), ALU.mult),
                             R=['CB'], W=[K_('decA')])
                        yield
                        S.op('dve', lambda e: e.tensor_tensor(decA[0:L, :, 0:L], decA[0:L, :, 0:L], b3(nbeta), ALU.mult),
                             R=['nbeta'], W=[K_('decA')])
                        yield
                        S.op('act', lambda e: e.activation(out=gamB[:, :, 0:L],
                                                           in_=X1[:, 0:L4].rearrange("p (h i) -> p h i", i=L),
                                                           func=AF.Exp), R=[('ps', x1, 'a')], W=[K_('gamB')])
                        yield
                        S.op('dve', lambda e: e.tensor_tensor(qgT[0:64, :, 0:L], qk64[0:64, 0:4, lc:lc + L],
                                                              gamB[0:64, :, 0:L], ALU.mult),
                             R=QKR + [K_('gamB')], W=[K_('qgT')])
                        yield
                        fns = []
                        for hh in range(4):
                            kTh = qk64[0:64, 4 + hh, lc:lc + L]
                            qTh = qk64[0:64, hh, lc:lc + L]
                            fns.append(lambda e, hh=hh, kTh=kTh: e.matmul(X0[0:L, hh * L:(hh + 1) * L], kTh, kTh,
                                                                          start=True, stop=True))
                            fns.append(lambda e, hh=hh, kTh=kTh, qTh=qTh: e.matmul(
                                X0[0:L, 256 + hh * L:256 + (hh + 1) * L], kTh, qTh, start=True, stop=True))
                        S.mm(fns, R=QKR, W=[('ps', x0, 'a'), ('ps', x0, 'b')])
                        yield
                        B0, C0 = BC[0][0:L, 0:4, 0:L], BC[0][0:L, 4:8, 0:L]
                        S.op('dve', lambda e: e.tensor_tensor(B0, X0[0:L, 0:L4].rearrange("p (h i) -> p h i", i=L),
                                                              decA[0:L, :, 0:L], ALU.mult),
                             R=[('ps', x0, 'a'), K_('decA')], W=[('BC', bs, 0)])
                        yield
                        S.op('dve', lambda e: e.tensor_tensor(MTd[0:L, :, 0:L],
                                                              X0[0:L, 256:256 + L4].rearrange("p (h i) -> p h i", i=L),
                                                              decT[0:L, :, 0:L], ALU.mult),
                             R=[('ps', x0, 'b'), K_('decT')], W=[K_('MTd')])
                        yield
                        S.mm([lambda e, hh=hh: e.matmul(X1[0:L, 256 + hh * L:256 + (hh + 1) * L],
                                                        BC[0][0:L, hh, 0:L], IDB[0:L, 0:L], start=True, stop=True)
                              for hh in range(4)], R=[('BC', bs, 0), 'CBb'], W=[('ps', x1, 'b')])
                        yield
                        pc = X1[0:L, 256:256 + L4].rearrange("p (h i) -> p h i", i=L)
                        S.op('act', lambda e: e.activation(out=C0, in_=pc, func=AF.Copy), R=[('ps', x1, 'b')],
                             W=[('BC', bs, 0)])
                        yield
                        S.op('dve', lambda e: e.tensor_tensor(Pm[0:L, :, 0:L], pc,
                                                              IDF[0:L, 0:L].unsqueeze(1).to_broadcast([L, 4, L]),
                                                              ALU.add), R=[('ps', x1, 'b'), 'CB'], W=[K_('Pm')])
                        yield
                        cur = 0
                        for lev in range(1, nlev + 1):
                            nxt = 1 - cur
                            Bc = lambda hh: BC[cur][0:L, hh, 0:L]
                            Cc = lambda hh: BC[cur][0:L, 4 + hh, 0:L]
                            fns = []
                            for hh in range(4):
                                fns.append(lambda e, hh=hh: e.matmul(X2[0:L, hh * L:(hh + 1) * L], Cc(hh), Bc(hh),
                                                                     start=True, stop=True))
                                if lev < nlev:
                                    fns.append(lambda e, hh=hh: e.matmul(X2[0:L, 256 + hh * L:256 + (hh + 1) * L],
                                                                         Bc(hh), Cc(hh), start=True, stop=True))
                            S.mm(fns, R=[('BC', bs, cur)], W=[('ps', x2)])
                            yield
                            ncp = 8 if lev < nlev else 4
                            if L == 64:
                                S.op('act', lambda e: e.activation(
                                    out=BC[nxt][0:L, 0:ncp, 0:L],
                                    in_=X2[0:L, 0:ncp * L].rearrange("p (h i) -> p h i", i=L), func=AF.Copy),
                                     R=[('ps', x2)], W=[('BC', bs, nxt)])
                                yield
                            else:
                                for part in range(ncp // 4):
                                    S.op('act', lambda e, part=part: e.activation(
                                        out=BC[nxt][0:L, 4 * part:4 * part + 4, 0:L],
                                        in_=X2[0:L, 256 * part:256 * part + L4].rearrange("p (h i) -> p h i", i=L),
                                        func=AF.Copy), R=[('ps', x2)], W=[('BC', bs, nxt)])
                                    yield
                            cur = nxt
                            S.mm([lambda e, hh=hh: e.matmul(X1[0:L, hh * L:(hh + 1) * L], BC[cur][0:L, hh, 0:L],
                                                            Pm[0:L, hh, 0:L], start=True, stop=True)
                                  for hh in range(4)], R=[('BC', bs, cur), K_('Pm')], W=[('ps', x1, 'a')])
                            yield
                            S.op('dve', lambda e: e.tensor_tensor(
                                Pm[0:L, :, 0:L], X1[0:L, 0:L4].rearrange("p (h i) -> p h i", i=L), Pm[0:L, :, 0:L],
                                ALU.add), R=[('ps', x1, 'a')], W=[K_('Pm')])
                            yield
                        S.mm([lambda e, hh=hh: e.matmul(X1[0:64, 256 + hh * L:256 + (hh + 1) * L], kbg[0:L, hh, :],
                                                        Pm[0:L, hh, 0:L], start=True, stop=True) for hh in range(4)],
                             R=[K_('kbg'), K_('Pm')], W=[('ps', x1, 'b')])
                        yield
                        S.op('act', lambda e: e.activation(
                            out=nWT[0:64, :, 0:L], in_=X1[0:64, 256:256 + L4].rearrange("p (h i) -> p h i", i=L),
                            func=AF.Identity, bias=0.0, scale=-1.0), R=[('ps', x1, 'b')], W=[K_('nWT')])
                        yield
                        nsolve[0] -= 1
                        while tail_turn[0] != ci:
                            yield
                        fns = []
                        for hh in range(4):
                            fns.append(lambda e, hh=hh: e.matmul(PS[0][0:L, hh * 64:(hh + 1) * 64], Pm[0:L, hh, 0:L],
                                                                 vb[0:L, hh, :], start=True, stop=False))
                            fns.append(lambda e, hh=hh: e.matmul(PS[0][0:L, hh * 64:(hh + 1) * 64], nWT[0:64, hh, 0:L],
                                                                 Sb[0:64, hh, :], start=False, stop=True))
                        S.mm(fns, R=[K_('Pm'), K_('vb'), K_('nWT'), 'Sb'], W=[('ps', 0)])
                        yield
                        S.op('act', lambda e: e.activation(out=Ub[0:L, :, :],
                                                           in_=PS[0][0:L, 0:256].rearrange("p (h d) -> p h d", d=64),
                                                           func=AF.Copy), R=[('ps', 0)], W=[K_('Ub')])
                        yield
                        fns = []
                        for hh in range(4):
                            fns.append(lambda e, hh=hh: e.matmul(PS[0][0:L, 256 + hh * 64:256 + (hh + 1) * 64],
                                                                 MTd[0:L, hh, 0:L], Ub[0:L, hh, :], start=True,
                                                                 stop=False))
                            fns.append(lambda e, hh=hh: e.matmul(PS[0][0:L, 256 + hh * 64:256 + (hh + 1) * 64],
                                                                 qgT[0:64, hh, 0:L], Sb[0:64, hh, :], start=False,
                                                                 stop=True))
                        S.mm(fns, R=[K_('MTd'), K_('Ub'), K_('qgT'), 'Sb'], W=[('ps', 0)])
                        yield
                        S.op('act', lambda e: e.activation(out=osb[0:L, ci, :], in_=PS[0][0:L, 256:512], func=AF.Copy),
                             R=[('ps', 0)], W=[('osb', ci)])
                        yield
                        S.mm([lambda e, hh=hh: e.matmul(PS[1][0:64, hh * 64:(hh + 1) * 64], ke[0:L, hh, :],
                                                        Ub[0:L, hh, :], start=True, stop=True) for hh in range(4)],
                             R=[K_('ke'), K_('Ub')], W=[('ps', 1)])
                        yield
                        S.op('dve', lambda e: e.tensor_tensor(Sst[:, :, :], Sst[:, :, :],
                                                              gamL[0:64, c4].unsqueeze(2).to_broadcast([64, 4, 64]),
                                                              ALU.mult), R=['gamL'], W=['Sst'])
                        yield
                        S.op('dve', lambda e: e.tensor_tensor(Sst[:, :, :], Sst[:, :, :],
                                                              PS[1][0:64, 0:256].rearrange("p (h d) -> p h d", d=64),
                                                              ALU.add), R=[('ps', 1)], W=['Sst'])
                        yield
                        S.op('act', lambda e: e.activation(out=Sb[:], in_=Sst[:], func=AF.Copy), R=['Sst'], W=['Sb'])
                        yield
                        tail_turn[0] = ci + 1


                    pending = list(range(nch))
                    running = []
                    nsolve = [0]
                    while pending or running:
                        while pending and nsolve[0] < 2 and len(running) < NPAR:
                            c_ = pending.pop(0)
                            nsolve[0] += 1
                            running.append(chunk_gen(c_))
                        for g_ in list(running):
                            try:
                                next(g_)
                            except StopIteration:
                                running.remove(g_)
                    ov = osb[0:L, 0:nch, :]
                    S.op('act', lambda e: e.activation(out=sqo[0:L, 0:nch, :], in_=ov, func=AF.Square),
                         R=[('osb', ci) for ci in range(nch)], W=['sqo'])
                    S.op('dve', lambda e: e.tensor_reduce(out=sso[0:L, 0:n4],
                                                          in_=sqo[0:L, 0:nch, :].rearrange("p c (h d) -> p (c h) d", d=64),
                                                          axis=AX.X, op=ALU.add), R=['sqo'], W=['sso'])
                    rsqrt_act(rno[0:L, 0:n4], sso[0:L, 0:n4], 1.0 / 64, 1e-6, ['sso'], ['rno'], rtm[0:L, 0:n4], 'rtm')
                    S.op('dve', lambda e: e.tensor_tensor(
                        on_[0:L, 0:nch, :].rearrange("p c (h d) -> p (c h) d", d=64),
                        osb[0:L, 0:nch, :].rearrange("p c (h d) -> p (c h) d", d=64),
                        rno[0:L, 0:n4].unsqueeze(2).to_broadcast([L, n4, 64]), ALU.mult),
                         R=['rno'] + [('osb', ci) for ci in range(nch)], W=['on_'])
                    for fc in range(2):
                        S.mm([lambda e, ci=ci, fc=fc: e.matmul(PS[fc][:, ci * L:(ci + 1) * L],
                                                               on_[0:L, ci, fc * 128:(fc + 1) * 128], IDB[0:L, 0:L],
                                                               start=True, stop=True) for ci in range(nch)],
                             R=['on_', 'CBb'], W=[('ps', fc)])
                        S.op('dve', lambda e, fc=fc: e.scalar_tensor_tensor(
                            out=mixB[:, fc, cols:cols + nt], in0=PS[fc][:, 0:nt], scalar=gn2[:, 0:1], in1=zsT[:, fc, 0:nt],
                            op0=ALU.mult, op1=ALU.mult), R=[('ps', fc), 'gn2', 'zsT'],
                             W=[('mixB', t) for t in tiles_of(cols, nt)])
                S.dma('sp', o_gdn.rearrange("h d e -> d h e"), Sst[:], R=['Sst'], W=['o_gdn'])


        jobs = [Job(True, 0), Job(True, 1), Job(False, 0)]
        KJ = os.environ.get("KJOBS", "")
        if KJ == "s":
            jobs = [Job(False, 0)]
        elif KJ == "p":
            jobs = [Job(True, 0)]
        NLAY = int(os.environ.get("KLAYERS", str(DEPTH)))

        for job in jobs:
            T = job.T
            if job.prompt:
                x_src = xp[job.pidx * SEQ:(job.pidx + 1) * SEQ, :]
                y_dst = y_p[job.pidx * SEQ:(job.pidx + 1) * SEQ, :]
                rope_src = rope_p
            else:
                x_src = xs
                y_dst = y_s
                rope_src = rope_s

            with contextlib.ExitStack() as st:
                xin = [sb(st, "xin%d" % i, [128, D]) for i in range(2)]
                xinb = [sb(st, "xinb%d" % i, [128, D], BF16) for i in range(2)]
                for ti, (r0, tsz) in enumerate(job.tiles):
                    b = ti % 2
                    S.dma('sp', xin[b][0:tsz, :], x_src[r0:r0 + tsz, :], W=[('xin', b)])
                    S.op('act', lambda e: e.activation(out=xinb[b][0:tsz, :], in_=xin[b][0:tsz, :], func=AF.Copy),
                         R=[('xin', b)], W=[('xinb', b)])
                    for hf in range(2):
                        pk = 2 * b + hf
                        fns = []
                        for k in range(4):
                            kk = hf * 4 + k
                            fns.append(lambda e, k=k, kk=kk, pk=pk: e.matmul(
                                PS[pk][:, k * 128:k * 128 + tsz], xinb[b][0:tsz, kk * 128:(kk + 1) * 128],
                                IDB[0:tsz, 0:tsz], start=True, stop=True))
                        S.mm(fns, R=[('xinb', b), 'CBb'], W=[('ps', pk)])
                        S.op('dve' if hf == 0 else 'act', lambda e, hf=hf, pk=pk: (
                            e.tensor_copy(xT[:, hf * 4:(hf + 1) * 4, r0:r0 + tsz],
                                          PS[pk][:, :].rearrange("p (k t) -> p k t", t=128)[:, :, 0:tsz])
                            if hf == 0 else
                            e.activation(out=xT[:, hf * 4:(hf + 1) * 4, r0:r0 + tsz],
                                         in_=PS[pk][:, :].rearrange("p (k t) -> p k t", t=128)[:, :, 0:tsz],
                                         func=AF.Copy)),
                             R=[('ps', pk)], W=[('xT', r0 // 128, hf)])
                S.barrier()

            for l in range(NLAY):
                last = (l == DEPTH - 1)
                res1_src = x_src if l == 0 else xs2[0:T, :]
                out2_dst = y_dst if last else xs2[0:T, :]
                with contextlib.ExitStack() as mix:
                    mixB = sb(mix, "mixB", [128, 4, T], BF16)
                    wo = sb(mix, "wo", [128, 8, D], BF16)
                    with contextlib.ExitStack() as st:
                        gdn_phase(st, job, l, mixB)
                        S.barrier()
                    with contextlib.ExitStack() as st:
                        conv_phase(st, job, l, mixB)
                        S.barrier()
                    with contextlib.ExitStack() as mla:
                        QT = sb(mla, "QT", [96, 8, T], BF16)
                        if job.prompt:
                            ckvT = sb(mla, "ckvT", [128, 2, job.Tk], BF16)
                            kpeT = sb(mla, "kpeT", [32, job.Tk], BF16)
                            ckvN = kpeN = None
                        else:
                            ckvT = kpeT = None
                            ckvN = sb(mla, "ckvN", [128, 2, T], BF16)
                            kpeN = sb(mla, "kpeN", [32, T], BF16)
                        with contextlib.ExitStack() as st:
                            mla_proj_phase(st, job, l, QT, ckvT, kpeT, ckvN, kpeN, rope_src)
                            S.barrier()
                        with contextlib.ExitStack() as st:
                            attn_phase(st, job, l, QT, ckvT, kpeT, ckvN, kpeN)
                            for k in range(8):
                                S.dma('pool', wo[:, k, :], w_out[l, k * 128:(k + 1) * 128, :], W=[('wo', k)])
                            S.barrier()
                    with contextlib.ExitStack() as st:
                        g_b = sb(st, "g_b", [128, D])
                        b_b = sb(st, "b_b", [128, D])
                        S.dma('sp', g_b[:], ln1_g[l:l + 1, :].partition_broadcast(128), W=['lnp'])
                        S.dma('sp', b_b[:], ln1_b[l:l + 1, :].partition_broadcast(128), W=['lnp'])
                        lnt = [(sb(st, "xres%d" % i, [128, D]),
                                sb(st, "xb%d" % i, [128, D], BF16), sb(st, "st6%d" % i, [128, NBN, 6]),
                                sb(st, "mv%d" % i, [128, 2]), sb(st, "sc%d" % i, [128, 3])) for i in range(2)]
                        for ti, (r0, tsz) in enumerate(job.tiles):
                            b = ti % 2
                            pk = (2 * b, 2 * b + 1)
                            for hf in range(2):
                                fns = []
                                for k in range(8):
                                    lhs = xT[:, k, r0:r0 + tsz] if k < 4 else mixB[:, k - 4, r0:r0 + tsz]
                                    fns.append(lambda e, k=k, lhs=lhs, hf=hf: e.matmul(
                                        PS[pk[hf]][0:tsz, 0:512], lhs, wo[:, k, hf * 512:(hf + 1) * 512],
                                        start=(k == 0), stop=(k == 7)))
                                S.mm(fns, R=[('wo', k) for k in range(8)] + [('xT', ti, 0), ('mixB', ti)],
                                     W=[('ps', pk[hf])])
                            layernorm_tile(lnt[b], pk, tsz, res1_src[r0:r0 + tsz, :], [('xs2', ti)], g_b, b_b,
                                           xs1[r0:r0 + tsz, :], [('xs1', ti)], (r0, tsz), True, 'l%d' % b)
                        S.barrier()

                with contextlib.ExitStack() as st:
                    wd = sb(st, "wd", [128, NF, D], BF16)
                    hT = sb(st, "hT", [128, NF, min(T, 1024)], BF16)
                    wgu = [sb(st, "wgu%d" % i, [128, 8, 256], BF16) for i in range(2)]
                    sg = [sb(st, "sg%d" % i, [128, 512]) for i in range(2)]
                    g_b = sb(st, "g2_b", [128, D])
                    b_b = sb(st, "b2_b", [128, D])
                    lnt = [(sb(st, "fxres%d" % i, [128, D]),
                            sb(st, "fxb%d" % i, [128, D], BF16), sb(st, "fst6%d" % i, [128, NBN, 6]),
                            sb(st, "fmv%d" % i, [128, 2]), sb(st, "fsc%d" % i, [128, 3])) for i in range(2)]
                    S.dma('sp', g_b[:], ln2_g[l:l + 1, :].partition_broadcast(128), W=['lnp'])
                    S.dma('sp', b_b[:], ln2_b[l:l + 1, :].partition_broadcast(128), W=['lnp'])
                    it = 0
                    wd_loaded = [False]
                    for (h0, hn) in job.halves:
                        blocks = [(b0, min(512, hn - b0)) for b0 in range(0, hn, 512)]
                        for f in range(NF):
                            wb = f % 2
                            S.dma('pool', wgu[wb][:, :, 0:128],
                                  w_gate[l, :, f * 128:(f + 1) * 128].rearrange("(k p) c -> p k c", p=128),
                                  W=[('wgu', wb, 0)])
                            S.dma('pool', wgu[wb][:, :, 128:256],
                                  w_up[l, :, f * 128:(f + 1) * 128].rearrange("(k p) c -> p k c", p=128),
                                  W=[('wgu', wb, 1)])
                            if not wd_loaded[0] and f >= 1:
                                S.dma('pool', wd[:, f - 1, :], w_down[l, (f - 1) * 128:f * 128, :], W=[('wd', f - 1)])
                                if f == NF - 1:
                                    S.dma('pool', wd[:, f, :], w_down[l, f * 128:(f + 1) * 128, :], W=[('wd', f)])
                                    wd_loaded[0] = True
                            for (b0, nt) in blocks:
                                c0 = h0 + b0
                                pb = it % 2
                                it += 1
                                pkg, pku = 4 + 2 * pb, 5 + 2 * pb
                                for which, pkk in ((0, pkg), (1, pku)):
                                    fns = []
                                    for k in range(8):
                                        fns.append(lambda e, k=k, which=which, pkk=pkk: e.matmul(
                                            PS[pkk][:, 0:nt], wgu[wb][:, k, which * 128:(which + 1) * 128],
                                            xT[:, k, c0:c0 + nt], start=(k == 0), stop=(k == 7)))
                                    S.mm(fns, R=[('wgu', wb, which)] + xT_R(c0, nt), W=[('ps', pkk)])
                                S.op('act', lambda e: e.activation(out=sg[pb][:, 0:nt], in_=PS[pkg][:, 0:nt],
                                                                   func=AF.Silu),
                                     R=[('ps', pkg)], W=[('sg', pb)])
                                S.op('dve', lambda e: e.tensor_tensor(hT[:, f, b0:b0 + nt], sg[pb][:, 0:nt],
                                                                      PS[pku][:, 0:nt], ALU.mult),
                                     R=[('sg', pb), ('ps', pku)], W=[('hT', f, b0 // 512)])
                        for (r0, tsz) in [(r, s) for (r, s) in job.tiles if h0 <= r < h0 + hn]:
                            ti = r0 // 128
                            b = ti % 2
                            pk = (2 * b, 2 * b + 1)
                            lr = r0 - h0
                            for hf in range(2):
                                fns = []
                                for f in range(NF):
                                    fns.append(lambda e, f=f, hf=hf: e.matmul(
                                        PS[pk[hf]][0:tsz, 0:512], hT[:, f, lr:lr + tsz],
                                        wd[:, f, hf * 512:(hf + 1) * 512], start=(f == 0), stop=(f == NF - 1)))
                                S.mm(fns, R=[('wd', f) for f in range(NF)] + [('hT', f, lr // 512) for f in range(NF)],
                                     W=[('ps', pk[hf])])
                            layernorm_tile(lnt[b], pk, tsz, xs1[r0:r0 + tsz, :], [('xs1', ti)], g_b, b_b,
                                           out2_dst[r0:r0 + tsz, :], [('xs2', ti)], (r0, tsz), not last, 'f%d' % b)
                    S.barrier()
        S.barrier()
    return nc


_NC_CACHE = {}


def _consts():
    c = np.zeros((128, 512), np.float32)
    c[:, 0:128] = np.eye(128, dtype=np.float32)
    j = np.arange(64)[:, None]
    i = np.arange(64)[None, :]
    c[0:64, 128:192] = (i >= j)
    c[0:64, 192:256] = (i > j)
    c[0:64, 256:320] = (j >= i)
    c[0:64, 320:384] = (j > i)
    c[0:64, 384:448] = 1.0
    c[64:128, 448:512] = 1.0
    return c


def _rope(pos):
    inv = np.exp(-np.log(10000.0) * np.arange(16, dtype=np.float32) / 16).astype(np.float32)
    ang = (pos.astype(np.float32)[:, None] * inv[None, :]).astype(np.float32)
    return np.concatenate([np.cos(ang), np.sin(ang)], axis=1).astype(np.float32)


def kernel(**inputs):
    f = lambda a: np.ascontiguousarray(np.asarray(a, dtype=np.float32))
    inp = {k: f(v) for k, v in inputs.items()}
    if 'nc' not in _NC_CACHE:
        _NC_CACHE['nc'] = build_program()
    nc = _NC_CACHE['nc']
    consts = _consts()
    rope_p = _rope(np.arange(SEQ))
    rope_s = _rope(PAST + (np.arange(4 * DSEQ) % DSEQ))
    wnames = ['w_in', 'mla_q_norm', 'w_uq', 'mla_kv_norm', 'w_uk', 'w_uv', 'gdn_conv_w', 'gdn_a_log', 'gdn_dt_bias',
              'gdn_norm', 'conv_w', 'conv_b', 'conv_ln_g', 'conv_ln_b', 'w_out', 'ln1_g', 'ln1_b', 'w_gate', 'w_up',
              'w_down', 'ln2_g', 'ln2_b']
    in_maps = []
    for c in range(NCORES):
        m = {w: inp[w] for w in wnames}
        m['xp'] = inp['x_prompt'][2 * c:2 * c + 2].reshape(2 * SEQ, D)
        m['xs'] = inp['x_sample'][4 * c:4 * c + 4].reshape(4 * DSEQ, D)
        m['c_ckv'] = np.ascontiguousarray(inp['cache_mla_ckv'][:, 4 * c:4 * c + 4])
        m['c_kpe'] = np.ascontiguousarray(inp['cache_mla_kpe'][:, 4 * c:4 * c + 4])
        m['s_gdn'] = np.ascontiguousarray(inp['state_gdn'][:, 4 * c:4 * c + 4])
        m['s_gcv'] = np.ascontiguousarray(inp['state_gdn_conv'][:, 4 * c:4 * c + 4])
        m['s_cv'] = np.ascontiguousarray(inp['state_conv'][:, 4 * c:4 * c + 4])
        m['consts'] = consts
        m['rope_p'] = rope_p
        m['rope_s'] = rope_s
        in_maps.append(m)
    res = run_bass_kernel_spmd(nc, in_maps, core_ids=list(range(NCORES)))
    R = res.results
    cat = lambda name, ax: np.concatenate([R[c][name] for c in range(NCORES)], axis=ax)
    y_p = cat('y_p', 0).reshape(16, SEQ, D)
    y_s = cat('y_s', 0).reshape(32, DSEQ, D)
    outs = (y_p, y_s, cat('o_ckv_p', 1), cat('o_kpe_p', 1), cat('o_gdn_p', 1), cat('o_gcv_p', 1), cat('o_cv_p', 1),
            cat('o_ckv_s', 1), cat('o_kpe_s', 1), cat('o_gdn_s', 1), cat('o_gcv_s', 1), cat('o_cv_s', 1))
    return tuple(np.ascontiguousarray(o.astype(np.float32)) for o in outs)
```

```python
import contextlib
import numpy as np
import concourse.bass as bass
import concourse.mybir as mybir
from concourse.bass_utils import run_bass_kernel_spmd

F32 = mybir.dt.float32
BF16 = mybir.dt.bfloat16
AF = mybir.ActivationFunctionType
ALU = mybir.AluOpType
AX = mybir.AxisListType

D = 1024
DEPTH = 2
SEQ = 2048
PAST = 2048
DSEQ = 16
H = 8
DIN = 2216
DFF = 2816
NF = DFF // 128
ALPHA = float((2 * DEPTH) ** 0.25)
MLA_SCALE = float(96 ** -0.5)
NCORES = 8


class Sched:
    NDS = 12

    def __init__(self, nc, es):
        self.nc = nc
        self.eng = {'pe': nc.tensor, 'act': nc.scalar, 'dve': nc.vector, 'pool': nc.gpsimd, 'sp': nc.sync}
        self.sem = {e: es.enter_context(nc.semaphore('sem_' + e)) for e in self.eng}
        self.cnt = {e: 0 for e in self.eng}
        self.dsem = {q: [es.enter_context(nc.semaphore('d%s%d' % (q, i))) for i in range(self.NDS)]
                     for q in ('sp', 'pool')}
        self.dcnt = {q: [0] * self.NDS for q in ('sp', 'pool')}
        self.dnext = {'sp': 0, 'pool': 0}
        self.waited = {e: {} for e in self.eng}
        self.res = {}
        self.ninst = 0

    def _need(self, e, ev):
        name, handle, val = ev
        w = self.waited[e]
        if w.get(name, 0) >= val:
            return
        self.eng[e].wait_ge(handle, val)
        w[name] = val
        self.ninst += 1

    def _deps(self, e, R, W):
        for k in R:
            r = self.res.get(k)
            if r is not None and r[0] is not None:
                self._need(e, r[0])
        for k in W:
            r = self.res.get(k)
            if r is not None:
                if r[0] is not None:
                    self._need(e, r[0])
                for ev in r[1].values():
                    self._need(e, ev)

    def _commit(self, ev, R, W):
        for k in R:
            r = self.res.get(k)
            if r is None:
                r = [None, {}]
                self.res[k] = r
            r[1][ev[0]] = ev
        for k in W:
            self.res[k] = [ev, {}]

    @staticmethod
    def _bankify(R, W):
        banks = set()
        for k in list(R) + list(W):
            if isinstance(k, tuple) and k[0] == 'ps':
                banks.add(k[1])
        if not banks:
            return W
        return list(W) + [('bank', b) for b in sorted(banks)]

    def op(self, e, fn, R=(), W=()):
        W = self._bankify(R, W)
        self._deps(e, R, W)
        ins = fn(self.eng[e])
        self.cnt[e] += 1
        ins.then_inc(self.sem[e], 1)
        self.ninst += 1
        self._commit(('E' + e, self.sem[e], self.cnt[e]), R, W)

    def mm(self, fns, R=(), W=()):
        W = self._bankify(R, W)
        self._deps('pe', R, W)
        ins = None
        for fn in fns:
            ins = fn(self.nc.tensor)
            self.ninst += 1
        self.cnt['pe'] += 1
        ins.then_inc(self.sem['pe'], 1)
        self._commit(('Epe', self.sem['pe'], self.cnt['pe']), R, W)

    def dma(self, q, out, in_, R=(), W=(), **kw):
        i = self.dnext[q]
        self.dnext[q] = (i + 1) % self.NDS
        name = 'D%s%d' % (q, i)
        if self.dcnt[q][i] > 0:
            self._need(q, (name, self.dsem[q][i], self.dcnt[q][i]))
        self._deps(q, R, W)
        ins = self.eng[q].dma_start(out=out, in_=in_, **kw)
        self.dcnt[q][i] += 16
        ins.then_inc(self.dsem[q][i], 16)
        self.ninst += 1
        self._commit((name, self.dsem[q][i], self.dcnt[q][i]), R, W)

    def barrier(self):
        evs = []
        for e in self.eng:
            if self.cnt[e] > 0:
                evs.append(('E' + e, self.sem[e], self.cnt[e]))
        for q in self.dsem:
            for i in range(self.NDS):
                if self.dcnt[q][i] > 0:
                    evs.append(('D%s%d' % (q, i), self.dsem[q][i], self.dcnt[q][i]))
        for e in self.eng:
            for ev in evs:
                self._need(e, ev)
        self.res.clear()


class Seq:
    def __init__(self, T, c0, L, idx, prompt):
        self.T, self.c0, self.L, self.idx, self.prompt = T, c0, L, idx, prompt
        self.blocks = [(b, min(512, T - b)) for b in range(0, T, 512)]


class Job:
    def __init__(self, prompt, pidx):
        self.prompt = prompt
        self.pidx = pidx
        if prompt:
            self.T = SEQ
            self.seqs = [Seq(SEQ, 0, 64, pidx, True)]
            self.Tk = SEQ
        else:
            self.T = 4 * DSEQ
            self.seqs = [Seq(DSEQ, DSEQ * s, DSEQ, s, False) for s in range(4)]
            self.Tk = PAST + DSEQ
        self.tiles = [(r, min(128, self.T - r)) for r in range(0, self.T, 128)]
        self.halves = [(r, min(1024, self.T - r)) for r in range(0, self.T, 1024)]


def build_program():
    nc = bass.Bass("TRN2", target_bir_lowering=False)

    def din(name, shape):
        return nc.dram_tensor(name, list(shape), F32, kind="ExternalInput").ap()

    def dout(name, shape):
        return nc.dram_tensor(name, list(shape), F32, kind="ExternalOutput").ap()

    xp = din("xp", [2 * SEQ, D])
    xs = din("xs", [4 * DSEQ, D])
    c_ckv = din("c_ckv", [DEPTH, 4, PAST, 256])
    c_kpe = din("c_kpe", [DEPTH, 4, PAST, 32])
    s_gdn = din("s_gdn", [DEPTH, 4, 4, 64, 64])
    s_gcv = din("s_gcv", [DEPTH, 4, 3, 768])
    s_cv = din("s_cv", [DEPTH, 4, 30, 256])
    w_in = din("w_in", [DEPTH, D, DIN])
    q_norm = din("mla_q_norm", [DEPTH, 384])
    w_uq = din("w_uq", [DEPTH, 384, 768])
    kv_norm = din("mla_kv_norm", [DEPTH, 256])
    w_uk = din("w_uk", [DEPTH, 256, 8, 64])
    w_uv = din("w_uv", [DEPTH, 256, 8, 64])
    gcw_d = din("gdn_conv_w", [DEPTH, 4, 768])
    a_log = din("gdn_a_log", [DEPTH, 4])
    dt_bias = din("gdn_dt_bias", [DEPTH, 4])
    gdn_norm = din("gdn_norm", [DEPTH, 64])
    conv_w = din("conv_w", [DEPTH, 31, 256])
    conv_b = din("conv_b", [DEPTH, 256])
    cln_g = din("conv_ln_g", [DEPTH, 256])
    cln_b = din("conv_ln_b", [DEPTH, 256])
    w_out = din("w_out", [DEPTH, D, D])
    ln1_g = din("ln1_g", [DEPTH, D])
    ln1_b = din("ln1_b", [DEPTH, D])
    w_gate = din("w_gate", [DEPTH, D, DFF])
    w_up = din("w_up", [DEPTH, D, DFF])
    w_down = din("w_down", [DEPTH, DFF, D])
    ln2_g = din("ln2_g", [DEPTH, D])
    ln2_b = din("ln2_b", [DEPTH, D])
    consts = din("consts", [128, 512])
    rope_p = din("rope_p", [SEQ, 32])
    rope_s = din("rope_s", [4 * DSEQ, 32])

    y_p = dout("y_p", [2 * SEQ, D])
    y_s = dout("y_s", [4 * DSEQ, D])
    o_ckv_p = dout("o_ckv_p", [DEPTH, 2, SEQ, 256])
    o_kpe_p = dout("o_kpe_p", [DEPTH, 2, SEQ, 32])
    o_gdn_p = dout("o_gdn_p", [DEPTH, 2, 4, 64, 64])
    o_gcv_p = dout("o_gcv_p", [DEPTH, 2, 3, 768])
    o_cv_p = dout("o_cv_p", [DEPTH, 2, 30, 256])
    o_ckv_s = dout("o_ckv_s", [DEPTH, 4, DSEQ, 256])
    o_kpe_s = dout("o_kpe_s", [DEPTH, 4, DSEQ, 32])
    o_gdn_s = dout("o_gdn_s", [DEPTH, 4, 4, 64, 64])
    o_gcv_s = dout("o_gcv_s", [DEPTH, 4, 3, 768])
    o_cv_s = dout("o_cv_s", [DEPTH, 4, 30, 256])

    xs1 = nc.dram_tensor("xs1", [SEQ, D], F32, kind="Internal").ap()
    xs2 = nc.dram_tensor("xs2", [SEQ, D], F32, kind="Internal").ap()

    es = contextlib.ExitStack()
    with es:
        S = Sched(nc, es)

        uid = [0]

        def sb(st, name, shape, dt=F32):
            uid[0] += 1
            return st.enter_context(nc.sbuf_tensor("%s_%d" % (name, uid[0]), list(shape), dt))

        CB = sb(es, "CB", [128, 512])
        CBb = sb(es, "CBb", [128, 128], BF16)
        ONES = sb(es, "ONES", [128, 128])
        E32 = sb(es, "E32", [32, 96], BF16)
        xT = sb(es, "xT", [128, 8, SEQ], BF16)
        wgP = sb(es, "wgP", [128, 8, 1032], BF16)

        def load_wg(l_):
            for (ca, cb_) in ((0, 512), (512, 1032)):
                S.dma('pool', wgP[:, :, ca:cb_], w_in[l_, :, 672 + ca:672 + cb_].rearrange("(k p) c -> p k c", p=128),
                      W=['wg'])
        PS = [es.enter_context(nc.psum_tensor("PS%d" % i, [128, 512], F32)) for i in range(8)]

        S.dma('sp', CB[:], consts, W=['CB'])
        S.dma('pool', CBb[:], consts[:, 0:128], W=['CBb'])
        S.op('dve', lambda e: e.memset(ONES[:], 1.0), W=['ONES'])
        S.op('dve', lambda e: e.memset(E32[:], 0.0), W=['E32'])
        S.op('dve', lambda e: e.tensor_copy(E32[:, 64:96], CBb[0:32, 0:32]), R=['CBb'], W=['E32'])
        IDF = CB[:, 0:128]
        IDB = CBb[:, 0:128]

        def maskU(L):
            return CB[0:L, 128:128 + L]

        def maskLs(L):
            return CB[0:L, 320:320 + L]

        BONES = CB[:, 384:512]
        S.barrier()

        FMAX = int(nc.vector.BN_STATS_FMAX)
        assert D % FMAX == 0 or FMAX >= D
        NBN = max(1, D // FMAX)
        BNW = D // NBN

        def tr_bf(out_ps, data, K, R, W):
            S.mm([lambda e: e.matmul(out_ps, data, IDB[0:K, 0:K], start=True, stop=True)], R=R + ['CBb'], W=W)

        def tr_f32(out_ps, data, K, R, W):
            S.mm([lambda e: e.matmul(out_ps, data, IDF[0:K, 0:K], start=True, stop=True)], R=R + ['CB'], W=W)

        def rsqrt_act(out, in_, scale, eps, R, W, tmp, tmpkey):
            S.op('act', lambda e: e.activation(out=tmp, in_=in_, func=AF.Ln, bias=float(eps), scale=float(scale)),
                 R=R, W=[tmpkey])
            S.op('act', lambda e: e.activation(out=out, in_=tmp, func=AF.Exp, scale=-0.5), R=[tmpkey], W=W)

        def tiles_of(c0, n):
            return list(range(c0 // 128, (c0 + n - 1) // 128 + 1))

        def xkeys(c0, n):
            return [('xT', t) for t in tiles_of(c0, n)]

        def layernorm_tile(st_tiles, pk, tsz, res_src, res_keys, g_b, b_b, dst, dst_keys, xt_cols, write_xT, tag):
            xres, xb, st6, mv, sc = st_tiles
            z = xres
            xn = xres
            S.dma('sp', xres[0:tsz, :], res_src, R=res_keys, W=[(tag + 'z', 0), (tag + 'z', 1)])
            for hf in range(2):
                S.op('dve', lambda e, hf=hf: e.scalar_tensor_tensor(
                    out=z[0:tsz, hf * 512:(hf + 1) * 512], in0=xres[0:tsz, hf * 512:(hf + 1) * 512], scalar=ALPHA,
                    in1=PS[pk[hf]][0:tsz, 0:512], op0=ALU.mult, op1=ALU.add),
                     R=[('ps', pk[hf])], W=[(tag + 'z', hf)])
            for c in range(NBN):
                S.op('dve', lambda e, c=c: e.bn_stats(st6[0:tsz, c, :], z[0:tsz, c * BNW:(c + 1) * BNW]),
                     R=[(tag + 'z', 0), (tag + 'z', 1)], W=[(tag + 'st', c)])
            S.op('dve', lambda e: e.bn_aggr(mv[0:tsz, :], st6[0:tsz, :, :]),
                 R=[(tag + 'st', c) for c in range(NBN)], W=[tag + 'mv'])
            rsqrt_act(sc[0:tsz, 0:1], mv[0:tsz, 1:2], 1.0, 1e-5, [tag + 'mv'], [tag + 'sc0'], sc[0:tsz, 2:3], tag + 'sc2')
            S.op('dve', lambda e: e.scalar_tensor_tensor(out=sc[0:tsz, 1:2], in0=mv[0:tsz, 0:1], scalar=-1.0,
                                                         in1=sc[0:tsz, 0:1], op0=ALU.mult, op1=ALU.mult),
                 R=[tag + 'mv', tag + 'sc0'], W=[tag + 'sc1'])
            S.op('act', lambda e: e.activation(out=xn[0:tsz, :], in_=z[0:tsz, :], func=AF.Identity,
                                               bias=sc[0:tsz, 1:2], scale=sc[0:tsz, 0:1]),
                 R=[tag + 'sc0', tag + 'sc1'], W=[(tag + 'z', 0), (tag + 'z', 1)])
            S.op('dve', lambda e: e.tensor_tensor(xn[0:tsz, :], xn[0:tsz, :], g_b[0:tsz, :], ALU.mult),
                 R=['lnp'], W=[(tag + 'z', 0), (tag + 'z', 1)])
            S.op('dve', lambda e: e.tensor_tensor(xn[0:tsz, :], xn[0:tsz, :], b_b[0:tsz, :], ALU.add),
                 R=['lnp'], W=[(tag + 'z', 0), (tag + 'z', 1)])
            S.dma('sp', dst, xn[0:tsz, :], R=[(tag + 'z', 0), (tag + 'z', 1)], W=dst_keys)
            if write_xT:
                S.op('act', lambda e: e.activation(out=xb[0:tsz, :], in_=xn[0:tsz, :], func=AF.Copy),
                     R=[(tag + 'z', 0), (tag + 'z', 1)], W=[tag + 'xb'])
                for hf in range(2):
                    fns = []
                    for k in range(4):
                        kk = hf * 4 + k
                        fns.append(lambda e, k=k, kk=kk: e.matmul(PS[pk[hf]][:, k * 128:k * 128 + tsz],
                                                                 xb[0:tsz, kk * 128:(kk + 1) * 128],
                                                                 IDB[0:tsz, 0:tsz], start=True, stop=True))
                    S.mm(fns, R=[tag + 'xb', 'CBb'], W=[('ps', pk[hf])])
                    c0, _ = xt_cols
                    S.op('act' if hf == 0 else 'dve', lambda e, hf=hf, c0=c0: (
                        e.activation(out=xT[:, hf * 4:(hf + 1) * 4, c0:c0 + tsz],
                                     in_=PS[pk[hf]][:, :].rearrange("p (k t) -> p k t", t=128)[:, :, 0:tsz],
                                     func=AF.Copy)
                        if hf == 0 else
                        e.tensor_copy(xT[:, hf * 4:(hf + 1) * 4, c0:c0 + tsz],
                                      PS[pk[hf]][:, :].rearrange("p (k t) -> p k t", t=128)[:, :, 0:tsz])),
                         R=[('ps', pk[hf])], W=[('xT', c0 // 128, hf)])

        def xT_R(c0, n):
            ks = []
            for t in tiles_of(c0, n):
                ks.append(('xT', t, 0))
                ks.append(('xT', t, 1))
            return ks

        import os
        STAGE = os.environ.get("KSTAGE", "full")

        def stub_zero(st, job, mixB):
            S.op('dve', lambda e: e.memset(mixB[:], 0.0), W=[('mixB', t) for t in range(len(job.tiles))])

        def gdn_phase(st, job, l, mixB):
            if STAGE in ("ffn",):
                return stub_zero(st, job, mixB)
            return gdn_phase_real(st, job, l, mixB)

        def conv_phase(st, job, l, mixB):
            if STAGE in ("ffn", "gdn"):
                return
            return conv_phase_real(st, job, l, mixB)

        def mla_proj_phase(st, job, l, QT, ckvT, kpeT, ckvN, kpeN, rope_src):
            if STAGE in ("ffn", "gdn", "conv"):
                return
            return mla_proj_phase_real(st, job, l, QT, ckvT, kpeT, ckvN, kpeN, rope_src)

        def attn_phase(st, job, l, QT, ckvT, kpeT, ckvN, kpeN):
            if STAGE in ("ffn", "gdn", "conv", "mlaproj"):
                for ti, (r0, tsz) in enumerate(job.tiles):
                    S.op('dve', lambda e: e.memset(xT[:, 0:4, r0:r0 + tsz], 0.0), W=[('xT', ti, 0)])
                return
            return attn_phase_real(st, job, l, QT, ckvT, kpeT, ckvN, kpeN)


        CSTOP = int(os.environ.get('KCSTOP', '0'))

        def conv_phase_real(st, job, l, mixB):
            wgl = sb(st, "wgl", [128, 8, 512], BF16)
            diag = sb(st, "diag", [128, 2, 31, 128], BF16)
            cwT = sb(st, "cwT", [128, 2, 32])
            cw_st = sb(st, "cw_st", [32, 256])
            prm = sb(st, "cprm", [128, 3, 2])
            c32 = sb(st, "c32", [128, 2, 30 + 512])
            cbf = sb(st, "cbf", [128, 2, 30 + 512], BF16)
            sgm = sb(st, "sgm", [128, 2, 512])
            cb = sb(st, "cb", [128, 2, 512])
            sq = sb(st, "csq", [128, 2, 512])
            mean = sb(st, "cmean", [128, 512])
            msq = sb(st, "cmsq", [128, 512])
            var = sb(st, "cvar", [128, 512])
            rstd = sb(st, "crstd", [128, 512])
            ltmp = sb(st, "cltmp", [128, 512])
            tt = sb(st, "ctt", [128, 2, 512])
            stg = sb(st, "cstg", [30, 256])
            S.dma('pool', wgl[:], w_in[l, :, 1704:2216].rearrange("(k p) c -> p k c", p=128), W=['wgl'])
            S.dma('sp', cw_st[0:31, :], conv_w[l], W=['cw_st'])
            for fc in range(2):
                tr_f32(PS[6][:, fc * 32:fc * 32 + 31], cw_st[0:31, fc * 128:(fc + 1) * 128], 31, ['cw_st'], [('ps', 6)])
            S.op('dve', lambda e: e.tensor_copy(cwT[:, :, 0:31],
                                                PS[6][:, 0:64].rearrange("p (f j) -> p f j", j=32)[:, :, 0:31]),
                 R=[('ps', 6)], W=['cwT'])
            if CSTOP == 1:
                return
            for fc in range(2):
                for j in range(31):
                    S.op('dve', lambda e, fc=fc, j=j: e.tensor_scalar(diag[:, fc, j, :], IDB, cwT[:, fc, j:j + 1], None,
                                                                     ALU.mult),
                         R=['cwT', 'CBb'], W=[('diag', fc)])
            if CSTOP == 2:
                return
            for wi, src in enumerate((conv_b, cln_g, cln_b)):
                for fc in range(2):
                    S.dma('sp', prm[:, wi, fc:fc + 1],
                          src[l, fc * 128:(fc + 1) * 128].rearrange("(p o) -> p o", o=1), W=['cprm'])
            if CSTOP == 3:
                return
            for seq in job.seqs:
                if seq.prompt:
                    S.op('dve', lambda e: e.memset(c32[:, :, 0:30], 0.0), W=['c32'])
                    o_cv = o_cv_p[l, seq.idx]
                else:
                    S.dma('sp', stg[:], s_cv[l, seq.idx], W=['cstg'])
                    for fc in range(2):
                        tr_f32(PS[6][:, fc * 32:fc * 32 + 30], stg[0:30, fc * 128:(fc + 1) * 128], 30, ['cstg'],
                               [('ps', 6)])
                    S.op('dve', lambda e: e.tensor_copy(
                        c32[:, :, 0:30], PS[6][:, 0:64].rearrange("p (f j) -> p f j", j=32)[:, :, 0:30]),
                         R=[('ps', 6)], W=['c32'])
                    o_cv = o_cv_s[l, seq.idx]
                for bi, (b0, nt) in enumerate(seq.blocks):
                    cols = seq.c0 + b0
                    lastb = (bi == len(seq.blocks) - 1)
                    for fc in range(2):
                        for which in range(2):
                            pk = 2 * which + fc
                            fns = []
                            for k in range(8):
                                fns.append(lambda e, k=k, pk=pk, which=which, fc=fc: e.matmul(
                                    PS[pk][:, 0:nt], wgl[:, k, which * 256 + fc * 128:which * 256 + (fc + 1) * 128],
                                    xT[:, k, cols:cols + nt], start=(k == 0), stop=(k == 7)))
                            S.mm(fns, R=['wgl'] + xT_R(cols, nt), W=[('ps', pk)])
                        S.op('act', lambda e, fc=fc: e.activation(out=sgm[:, fc, 0:nt], in_=PS[2 + fc][:, 0:nt],
                                                                 func=AF.Sigmoid),
                             R=[('ps', 2 + fc)], W=[('sgm', fc)])
                        S.op('dve', lambda e, fc=fc: e.tensor_tensor(c32[:, fc, 30:30 + nt], PS[fc][:, 0:nt],
                                                                    sgm[:, fc, 0:nt], ALU.mult),
                             R=[('ps', fc), ('sgm', fc)], W=['c32'])
                    if CSTOP == 5:
                        return
                    S.op('act', lambda e: e.activation(out=cbf[:, :, 0:30 + nt], in_=c32[:, :, 0:30 + nt],
                                                       func=AF.Copy), R=['c32'], W=['cbf'])
                    if CSTOP == 6:
                        return
                    for fc in range(2):
                        fns = []
                        for j in range(31):
                            fns.append(lambda e, j=j, fc=fc: e.matmul(PS[4 + fc][:, 0:nt], diag[:, fc, j, :],
                                                                      cbf[:, fc, j:j + nt], start=(j == 0),
                                                                      stop=(j == 30)))
                        S.mm(fns, R=['cbf', ('diag', fc)], W=[('ps', 4 + fc)])
                        S.op('act', lambda e, fc=fc: e.activation(out=cb[:, fc, 0:nt], in_=PS[4 + fc][:, 0:nt],
                                                                 func=AF.Identity, bias=prm[:, 0, fc:fc + 1],
                                                                 scale=1.0),
                             R=[('ps', 4 + fc), 'cprm'], W=[('cb', fc)])
                        S.op('act', lambda e, fc=fc: e.activation(out=sq[:, fc, 0:nt], in_=cb[:, fc, 0:nt],
                                                                 func=AF.Square), R=[('cb', fc)], W=[('csq', fc)])
                    if CSTOP == 7:
                        return
                    S.mm([lambda e, fc=fc: e.matmul(PS[6][:, 0:nt], ONES[:, :], cb[:, fc, 0:nt], start=(fc == 0),
                                                    stop=(fc == 1)) for fc in range(2)],
                         R=[('cb', 0), ('cb', 1), 'ONES'], W=[('ps', 6)])
                    S.mm([lambda e, fc=fc: e.matmul(PS[7][:, 0:nt], ONES[:, :], sq[:, fc, 0:nt], start=(fc == 0),
                                                    stop=(fc == 1)) for fc in range(2)],
                         R=[('csq', 0), ('csq', 1), 'ONES'], W=[('ps', 7)])
                    if CSTOP == 8:
                        return
                    S.op('act', lambda e: e.activation(out=mean[:, 0:nt], in_=PS[6][:, 0:nt], func=AF.Identity,
                                                       bias=0.0, scale=1.0 / 256), R=[('ps', 6)], W=['cmean'])
                    S.op('dve', lambda e: e.tensor_tensor(msq[:, 0:nt], mean[:, 0:nt], mean[:, 0:nt], ALU.mult),
                         R=['cmean'], W=['cmsq'])
                    S.op('dve', lambda e: e.scalar_tensor_tensor(out=var[:, 0:nt], in0=PS[7][:, 0:nt],
                                                                 scalar=1.0 / 256, in1=msq[:, 0:nt], op0=ALU.mult,
                                                                 op1=ALU.subtract),
                         R=[('ps', 7), 'cmsq'], W=['cvar'])
                    rsqrt_act(rstd[:, 0:nt], var[:, 0:nt], 1.0, 1e-5, ['cvar'], ['crstd'], ltmp[:, 0:nt], 'cltmp')
                    if CSTOP == 9:
                        return
                    for fc in range(2):
                        S.op('dve', lambda e, fc=fc: e.tensor_tensor(tt[:, fc, 0:nt], cb[:, fc, 0:nt], mean[:, 0:nt],
                                                                    ALU.subtract),
                             R=[('cb', fc), 'cmean'], W=[('ctt', fc)])
                        S.op('dve', lambda e, fc=fc: e.tensor_tensor(tt[:, fc, 0:nt], tt[:, fc, 0:nt], rstd[:, 0:nt],
                                                                    ALU.mult),
                             R=['crstd'], W=[('ctt', fc)])
                        S.op('act', lambda e, fc=fc: e.activation(out=mixB[:, 2 + fc, cols:cols + nt],
                                                                 in_=tt[:, fc, 0:nt], func=AF.Silu,
                                                                 bias=prm[:, 2, fc:fc + 1], scale=prm[:, 1, fc:fc + 1]),
                             R=[('ctt', fc), 'cprm'], W=[('mixB', t) for t in tiles_of(cols, nt)])
                    if CSTOP == 10:
                        return
                    if lastb:
                        for fc in range(2):
                            tr_f32(PS[4][0:30, fc * 128:(fc + 1) * 128], c32[:, fc, nt:nt + 30], 128, ['c32'],
                                   [('ps', 4)])
                        S.op('dve', lambda e: e.tensor_copy(stg[0:30, :], PS[4][0:30, 0:256]), R=[('ps', 4)],
                             W=['cstg'])
                        S.dma('sp', o_cv, stg[0:30, :], R=['cstg'], W=['o_cv'])
                    else:
                        S.op('act', lambda e: e.activation(out=c32[:, :, 0:30], in_=c32[:, :, nt:nt + 30],
                                                           func=AF.Copy), R=['c32'], W=['c32'])


        MSTOP = int(os.environ.get('KMSTOP', '0'))
        KVAR = int(os.environ.get('KVAR', '0'))

        def mla_proj_phase_real(st, job, l, QT, ckvT, kpeT, ckvN, kpeN, rope_src):
            nt_tiles = len(job.tiles)
            wm = sb(st, "wm", [128, 8, 672], BF16)
            wq = sb(st, "wq", [128, 3, 768], BF16)
            gq_b = sb(st, "gq_b", [128, 384])
            gk_b = sb(st, "gk_b", [128, 256])
            rope_t = sb(st, "rope_t", [128, nt_tiles, 32])
            S.dma('pool', wm[:], w_in[l, :, 0:672].rearrange("(k p) c -> p k c", p=128), W=['wm'])
            S.dma('pool', wq[:], w_uq[l].rearrange("(k p) c -> p k c", p=128), W=['wq'])
            S.dma('sp', gq_b[:], q_norm[l:l + 1, :].partition_broadcast(128), W=['gq_b'])
            S.dma('sp', gk_b[:], kv_norm[l:l + 1, :].partition_broadcast(128), W=['gk_b'])
            if job.prompt:
                S.dma('sp', rope_t[:], rope_src.rearrange("(t p) c -> p t c", p=128), W=['rope_t'])
                o_ckv = o_ckv_p[l, job.pidx]
                o_kpe = o_kpe_p[l, job.pidx]
            else:
                S.dma('sp', rope_t[0:job.T, 0, :], rope_src, W=['rope_t'])
                o_ckv = o_ckv_s[l].rearrange("s t c -> (s t) c")
                o_kpe = o_kpe_s[l].rearrange("s t c -> (s t) c")
            if MSTOP == 1:
                return
            tl = []
            for i in range(2):
                tl.append(dict(
                    sqt=sb(st, "sqt%d" % i, [128, 384]), ssq=sb(st, "ssq%d" % i, [128, 2]),
                    rr=sb(st, "rr%d" % i, [128, 4]), cqn=sb(st, "cqn%d" % i, [128, 384], BF16),
                    ckv32=sb(st, "ckv32%d" % i, [128, 256]), ckvb=sb(st, "ckvb%d" % i, [128, 256], BF16),
                    kpe32=sb(st, "kpe32%d" % i, [128, 32]), kpeb=sb(st, "kpeb%d" % i, [128, 32], BF16),
                    rt=sb(st, "rt%d" % i, [128, 4, 16]), cqT=sb(st, "cqT%d" % i, [128, 3, 128], BF16),
                    qb=sb(st, "qb%d" % i, [128, 8, 96], BF16), qt=sb(st, "qt%d" % i, [128, 4, 4, 16])))
            for ti, (r0, tsz) in enumerate(job.tiles):
                b = ti % 2
                t = tl[b]
                A, B, Q0, Q1 = 4 * b, 4 * b + 1, 4 * b + 2, 4 * b + 3
                kA, kB = ('ps', A), ('ps', B)
                tg = 'm%d' % b
                for (pk, c0w, c1w) in ((A, 0, 384), (B, 384, 672)):
                    fns = []
                    for k in range(8):
                        fns.append(lambda e, k=k, pk=pk, c0w=c0w, c1w=c1w: e.matmul(
                            PS[pk][0:tsz, 0:c1w - c0w], xT[:, k, r0:r0 + tsz], wm[:, k, c0w:c1w], start=(k == 0),
                            stop=(k == 7)))
                    S.mm(fns, R=['wm', ('xT', ti, 0), ('xT', ti, 1)], W=[('ps', pk)])
                if MSTOP == 2:
                    return
                S.op('act', lambda e: e.activation(out=t['sqt'][0:tsz, 0:384], in_=PS[A][0:tsz, 0:384], func=AF.Square),
                     R=[kA], W=[tg + 'sqt'])
                S.op('dve', lambda e: e.tensor_reduce(out=t['ssq'][0:tsz, 0:1], in_=t['sqt'][0:tsz, 0:384], axis=AX.X,
                                                      op=ALU.add), R=[tg + 'sqt'], W=[tg + 'ssq0'])
                S.op('act', lambda e: e.activation(out=t['sqt'][0:tsz, 0:256], in_=PS[B][0:tsz, 0:256], func=AF.Square),
                     R=[kB, tg + 'ssq0'], W=[tg + 'sqt'])
                S.op('dve', lambda e: e.tensor_reduce(out=t['ssq'][0:tsz, 1:2], in_=t['sqt'][0:tsz, 0:256], axis=AX.X,
                                                      op=ALU.add), R=[tg + 'sqt'], W=[tg + 'ssq1'])
                rsqrt_act(t['rr'][0:tsz, 0:1], t['ssq'][0:tsz, 0:1], 1.0 / 384, 1e-6, [tg + 'ssq0'], [tg + 'rr0'],
                          t['rr'][0:tsz, 2:3], tg + 'rr2')
                rsqrt_act(t['rr'][0:tsz, 1:2], t['ssq'][0:tsz, 1:2], 1.0 / 256, 1e-6, [tg + 'ssq1'], [tg + 'rr1'],
                          t['rr'][0:tsz, 3:4], tg + 'rr3')
                S.op('dve', lambda e: e.scalar_tensor_tensor(out=t['cqn'][0:tsz, :], in0=PS[A][0:tsz, 0:384],
                                                             scalar=t['rr'][0:tsz, 0:1], in1=gq_b[0:tsz, :],
                                                             op0=ALU.mult, op1=ALU.mult),
                     R=[kA, tg + 'rr0', 'gq_b'], W=[tg + 'cqn'])
                S.op('dve', lambda e: e.scalar_tensor_tensor(out=t['ckv32'][0:tsz, :], in0=PS[B][0:tsz, 0:256],
                                                             scalar=t['rr'][0:tsz, 1:2], in1=gk_b[0:tsz, :],
                                                             op0=ALU.mult, op1=ALU.mult),
                     R=[kB, tg + 'rr1', 'gk_b'], W=[tg + 'ckv32'])
                S.dma('sp', o_ckv[r0:r0 + tsz, :], t['ckv32'][0:tsz, :], R=[tg + 'ckv32'], W=[('o_ckv', ti)])
                S.op('act', lambda e: e.activation(out=t['ckvb'][0:tsz, :], in_=t['ckv32'][0:tsz, :], func=AF.Copy),
                     R=[tg + 'ckv32'], W=[tg + 'ckvb'])
                if MSTOP == 3:
                    return
                cos = rope_t[0:tsz, ti, 0:16]
                sin = rope_t[0:tsz, ti, 16:32]
                x1 = PS[B][0:tsz, 256:272]
                x2 = PS[B][0:tsz, 272:288]
                rt = t['rt']
                for idx, (xa, cs) in enumerate(((x1, cos), (x2, sin), (x1, sin), (x2, cos))):
                    S.op('dve', lambda e, idx=idx, xa=xa, cs=cs: e.tensor_tensor(rt[0:tsz, idx, :], xa, cs, ALU.mult),
                         R=[kB, 'rope_t'], W=[(tg + 'rt', idx)])
                S.op('dve', lambda e: e.tensor_tensor(t['kpe32'][0:tsz, 0:16], rt[0:tsz, 0, :], rt[0:tsz, 1, :],
                                                      ALU.subtract),
                     R=[(tg + 'rt', 0), (tg + 'rt', 1)], W=[tg + 'kpe32a'])
                S.op('dve', lambda e: e.tensor_tensor(t['kpe32'][0:tsz, 16:32], rt[0:tsz, 2, :], rt[0:tsz, 3, :],
                                                      ALU.add),
                     R=[(tg + 'rt', 2), (tg + 'rt', 3)], W=[tg + 'kpe32b'])
                S.dma('sp', o_kpe[r0:r0 + tsz, :], t['kpe32'][0:tsz, :], R=[tg + 'kpe32a', tg + 'kpe32b'],
                      W=[('o_kpe', ti)])
                S.op('act', lambda e: e.activation(out=t['kpeb'][0:tsz, :], in_=t['kpe32'][0:tsz, :], func=AF.Copy),
                     R=[tg + 'kpe32a', tg + 'kpe32b'], W=[tg + 'kpeb'])
                if MSTOP == 4:
                    return
                S.mm([lambda e, k3=k3: e.matmul(PS[A][:, k3 * 128:k3 * 128 + tsz],
                                                t['cqn'][0:tsz, k3 * 128:(k3 + 1) * 128], IDB[0:tsz, 0:tsz],
                                                start=True, stop=True) for k3 in range(3)],
                     R=[tg + 'cqn', 'CBb'], W=[kA])
                if KVAR == 2:
                    return
                S.op('act', lambda e: e.activation(
                    out=t['cqT'][:, :, 0:tsz],
                    in_=PS[A][:, 0:384].rearrange("p (k t) -> p k t", t=128)[:, :, 0:tsz], func=AF.Copy),
                     R=[kA], W=[tg + 'cqT'])
                if KVAR == 3:
                    return
                fns = [lambda e, c=c: e.matmul(PS[B][:, c * 128:c * 128 + tsz], t['ckvb'][0:tsz, c * 128:(c + 1) * 128],
                                               IDB[0:tsz, 0:tsz], start=True, stop=True) for c in range(2)]
                if KVAR != 1:
                    fns.append(lambda e: e.matmul(PS[B][0:32, 256:256 + tsz], t['kpeb'][0:tsz, 0:32], IDB[0:tsz, 0:tsz],
                                                  start=True, stop=True))
                S.mm(fns, R=[tg + 'ckvb', tg + 'kpeb', 'CBb'], W=[kB])
                if KVAR == 4:
                    return
                cdst = ckvT if job.prompt else ckvN
                kdst = kpeT if job.prompt else kpeN
                if KVAR != 6:
                    S.op('dve', lambda e: e.tensor_copy(
                        cdst[:, :, r0:r0 + tsz], PS[B][:, 0:256].rearrange("p (k t) -> p k t", t=128)[:, :, 0:tsz]),
                         R=[kB], W=[('ckvT', ti)])
                if KVAR != 5:
                    S.op('act', lambda e: e.activation(out=kdst[0:32, r0:r0 + tsz], in_=PS[B][0:32, 256:256 + tsz],
                                                       func=AF.Copy), R=[kB], W=[('kpeT', ti)])
                if MSTOP == 5:
                    return
                for b2, pk in enumerate((Q0, Q1)):
                    S.mm([lambda e, k=k, pk=pk, b2=b2: e.matmul(PS[pk][0:tsz, 0:384], t['cqT'][:, k, 0:tsz],
                                                                wq[:, k, b2 * 384:(b2 + 1) * 384], start=(k == 0),
                                                                stop=(k == 2)) for k in range(3)],
                         R=[tg + 'cqT', 'wq'], W=[('ps', pk)])
                    view = PS[pk][0:tsz, 0:384].rearrange("p (h e) -> p h e", e=96)
                    qb = t['qb']
                    qt = t['qt']
                    S.op('act', lambda e, view=view, b2=b2: e.activation(out=qb[0:tsz, 4 * b2:4 * b2 + 4, 0:64],
                                                                        in_=view[:, :, 0:64], func=AF.Copy),
                         R=[('ps', pk)], W=[(tg + 'qb', b2, 0)])
                    cosb = cos.unsqueeze(1).to_broadcast([tsz, 4, 16])
                    sinb = sin.unsqueeze(1).to_broadcast([tsz, 4, 16])
                    xx1 = view[:, :, 64:80]
                    xx2 = view[:, :, 80:96]
                    for idx, (xa, cs) in enumerate(((xx1, cosb), (xx2, sinb), (xx1, sinb), (xx2, cosb))):
                        S.op('dve', lambda e, idx=idx, xa=xa, cs=cs: e.tensor_tensor(qt[0:tsz, idx, :, :], xa, cs,
                                                                                  ALU.mult),
                             R=[('ps', pk), 'rope_t'], W=[(tg + 'qt', idx)])
                    S.op('dve', lambda e, b2=b2: e.tensor_tensor(qb[0:tsz, 4 * b2:4 * b2 + 4, 64:80], qt[0:tsz, 0, :, :],
                                                                qt[0:tsz, 1, :, :], ALU.subtract),
                         R=[(tg + 'qt', 0), (tg + 'qt', 1)], W=[(tg + 'qb', b2, 1)])
                    S.op('dve', lambda e, b2=b2: e.tensor_tensor(qb[0:tsz, 4 * b2:4 * b2 + 4, 80:96], qt[0:tsz, 2, :, :],
                                                                qt[0:tsz, 3, :, :], ALU.add),
                         R=[(tg + 'qt', 2), (tg + 'qt', 3)], W=[(tg + 'qb', b2, 2)])
                    if MSTOP == 6:
                        return
                    S.mm([lambda e, hh=hh, pk=pk, b2=b2: e.matmul(PS[pk][0:96, hh * 128:hh * 128 + tsz],
                                                                  qb[0:tsz, 4 * b2 + hh, :], IDB[0:tsz, 0:tsz],
                                                                  start=True, stop=True) for hh in range(4)],
                         R=[(tg + 'qb', b2, 0), (tg + 'qb', b2, 1), (tg + 'qb', b2, 2), 'CBb'], W=[('ps', pk)])
                    S.op('act' if b2 == 0 else 'dve', lambda e, b2=b2, pk=pk: (
                        e.activation(out=QT[0:96, 4 * b2:4 * b2 + 4, r0:r0 + tsz],
                                     in_=PS[pk][0:96, :].rearrange("p (h t) -> p h t", t=128)[:, :, 0:tsz],
                                     func=AF.Copy) if b2 == 0 else
                        e.tensor_copy(QT[0:96, 4 * b2:4 * b2 + 4, r0:r0 + tsz],
                                      PS[pk][0:96, :].rearrange("p (h t) -> p h t", t=128)[:, :, 0:tsz])),
                         R=[('ps', pk)], W=[('QT', ti, b2)])

        def attn_phase_real(st, job, l, QT, ckvT, kpeT, ckvN, kpeN):
            Tk = job.Tk
            NKT = (Tk + 127) // 128
            wk = sb(st, "wk", [128, 2, 8, 96], BF16)
            wv = sb(st, "wv", [128, 2, 8, 64], BF16)
            KT = [sb(st, "KT%d" % i, [96, Tk], BF16) for i in range(2)]
            VPall = sb(st, "VPall", [128, NKT, 8, 128], BF16)
            PT = [sb(st, "PT%d" % i, [128, 512], BF16) for i in range(3)]
            rs = [sb(st, "rs%d" % i, [128, 512]) for i in range(2)]
            S.op('dve', lambda e: e.memset(wk[:], 0.0), W=['wk'])
            for c in range(2):
                S.dma('pool', wk[:, c, :, 0:64], w_uk[l, c * 128:(c + 1) * 128], W=['wk'])
                S.dma('pool', wv[:, c, :, :], w_uv[l, c * 128:(c + 1) * 128], W=['wv'])
            for kt in range(NKT):
                S.op('dve', lambda e, kt=kt: e.memset(VPall[:, kt, :, 64:128], 1.0), W=['VP1'])
            if not job.prompt:
                cst = sb(st, "cst", [128, 16, 256], BF16)
                cpt = sb(st, "cpt", [128, 16, 32], BF16)
                ckvT = sb(st, "ckvTs", [128, 2, Tk], BF16)
                kpeT = sb(st, "kpeTs", [32, Tk], BF16)
            hcount = 0
            pocount = 0
            for seq in job.seqs:
                if not seq.prompt:
                    S.dma('pool', cst[:], c_ckv[l, seq.idx].rearrange("(t p) c -> p t c", p=128), W=['cst'])
                    S.dma('pool', cpt[:], c_kpe[l, seq.idx].rearrange("(t p) c -> p t c", p=128), W=['cpt'])
                    for t4 in range(4):
                        for c in range(2):
                            S.mm([lambda e, tt=tt, c=c, t4=t4: e.matmul(PS[7][:, tt * 128:(tt + 1) * 128],
                                                                        cst[:, t4 * 4 + tt, c * 128:(c + 1) * 128],
                                                                        IDB, start=True, stop=True) for tt in range(4)],
                                 R=['cst', 'CBb'], W=[('ps', 7)])
                            S.op('act', lambda e, c=c, t4=t4: e.activation(out=ckvT[:, c, t4 * 512:(t4 + 1) * 512],
                                                                          in_=PS[7][:, 0:512], func=AF.Copy),
                                 R=[('ps', 7)], W=['ckvT'])
                        S.mm([lambda e, tt=tt, t4=t4: e.matmul(PS[0][0:32, tt * 128:(tt + 1) * 128],
                                                               cpt[:, t4 * 4 + tt, 0:32], IDB, start=True, stop=True)
                              for tt in range(4)], R=['cpt', 'CBb'], W=[('ps', 0)])
                        S.op('dve', lambda e, t4=t4: e.tensor_copy(kpeT[0:32, t4 * 512:(t4 + 1) * 512],
                                                                   PS[0][0:32, 0:512]),
                             R=[('ps', 0)], W=['kpeT'])
                    S.op('dve', lambda e: e.tensor_copy(ckvT[:, :, PAST:PAST + DSEQ],
                                                        ckvN[:, :, seq.c0:seq.c0 + DSEQ]), W=['ckvT'])
                    S.op('dve', lambda e: e.tensor_copy(kpeT[0:32, PAST:PAST + DSEQ],
                                                        kpeN[0:32, seq.c0:seq.c0 + DSEQ]), W=['kpeT'])
                for kt in range(NKT):
                    ksz = min(128, Tk - kt * 128)
                    pk = kt % 2
                    S.mm([lambda e, c=c: e.matmul(PS[pk][0:ksz, 0:512], ckvT[:, c, kt * 128:kt * 128 + ksz],
                                                  wv[:, c, :, :].rearrange("p h e -> p (h e)"), start=(c == 0),
                                                  stop=(c == 1)) for c in range(2)],
                         R=['wv', 'ckvT'], W=[('ps', pk)])
                    S.op('dve' if kt % 2 == 0 else 'act', lambda e: (
                        e.tensor_copy(VPall[0:ksz, kt, :, 0:64],
                                      PS[pk][0:ksz, 0:512].rearrange("p (h e) -> p h e", e=64))
                        if kt % 2 == 0 else
                        e.activation(out=VPall[0:ksz, kt, :, 0:64],
                                     in_=PS[pk][0:ksz, 0:512].rearrange("p (h e) -> p h e", e=64), func=AF.Copy)),
                         R=[('ps', pk)], W=['VP'])
                for h in range(H):
                    hb = hcount % 2
                    hcount += 1
                    kt_ = KT[hb]
                    vp_ = VPall[:, :, h, :]
                    for bi, kb0 in enumerate(range(0, Tk, 512)):
                        nk = min(512, Tk - kb0)
                        pk = bi % 2
                        S.mm([lambda e: e.matmul(PS[pk][0:96, 0:nk], wk[:, 0, h, :], ckvT[:, 0, kb0:kb0 + nk],
                                                 start=True, stop=False),
                              lambda e: e.matmul(PS[pk][0:96, 0:nk], wk[:, 1, h, :], ckvT[:, 1, kb0:kb0 + nk],
                                                 start=False, stop=False),
                              lambda e: e.matmul(PS[pk][0:96, 0:nk], E32[:, :], kpeT[0:32, kb0:kb0 + nk],
                                                 start=False, stop=True)],
                             R=['wk', 'ckvT', 'kpeT', 'E32'], W=[('ps', pk)])
                        S.op('act', lambda e: e.activation(out=kt_[0:96, kb0:kb0 + nk], in_=PS[pk][0:96, 0:nk],
                                                           func=AF.Copy), R=[('ps', pk)], W=[('KT', hb)])
                    for (q0, nq) in seq.blocks:
                        jq0 = seq.c0 + q0
                        po = 5 + (pocount % 2)
                        pocount += 1
                        if seq.prompt:
                            kts = list(range((q0 + nq) // 128))
                        else:
                            kts = list(range(NKT))
                        pend = None
                        nk_ = len(kts)
                        if not seq.prompt:
                            psb = 2
                            pt_ = PT[0]
                            kszs = [min(128, Tk - kt * 128) for kt in kts]
                            S.mm([lambda e, kt=kt, ksz=ksz: e.matmul(PS[psb][0:ksz, kt * nq:(kt + 1) * nq],
                                                                     kt_[0:96, kt * 128:kt * 128 + ksz],
                                                                     QT[0:96, h, jq0:jq0 + nq], start=True, stop=True)
                                  for kt, ksz in zip(kts, kszs)],
                                 R=[('KT', hb)] + [('QT', t, h // 4) for t in tiles_of(jq0, nq)], W=[('ps', psb)])
                            nfull = len([k_ for k_ in kszs if k_ == 128])
                            S.op('act', lambda e: e.activation(out=pt_[:, 0:nfull * nq], in_=PS[psb][:, 0:nfull * nq],
                                                               func=AF.Exp, scale=MLA_SCALE),
                                 R=[('ps', psb)], W=[('PT', 0)])
                            for kt, ksz in list(zip(kts, kszs))[nfull:]:
                                S.op('act', lambda e, kt=kt, ksz=ksz: e.activation(
                                    out=pt_[0:ksz, kt * nq:(kt + 1) * nq], in_=PS[psb][0:ksz, kt * nq:(kt + 1) * nq],
                                    func=AF.Exp, scale=MLA_SCALE), R=[('ps', psb)], W=[('PT', 0)])
                            S.mm([lambda e, i=i, kt=kt, ksz=ksz: e.matmul(PS[po][:, 0:nq], vp_[0:ksz, kt, :],
                                                                          pt_[0:ksz, kt * nq:(kt + 1) * nq],
                                                                          start=(i == 0), stop=(i == nk_ - 1))
                                  for i, (kt, ksz) in enumerate(zip(kts, kszs))],
                                 R=['VP', 'VP1', ('PT', 0)], W=[('ps', po)])
                            kts = []
                        for i, kt in enumerate(kts):
                            ksz = min(128, Tk - kt * 128)
                            qs = max(q0, kt * 128) if seq.prompt else q0
                            w = q0 + nq - qs
                            psb = 2 + (i % 3)
                            pt_ = PT[i % 3]
                            S.mm([lambda e: e.matmul(PS[psb][0:ksz, 0:w], kt_[0:96, kt * 128:kt * 128 + ksz],
                                                     QT[0:96, h, seq.c0 + qs:seq.c0 + qs + w], start=True, stop=True)],
                                 R=[('KT', hb)] + [('QT', t, h // 4) for t in tiles_of(seq.c0 + qs, w)],
                                 W=[('ps', psb)])
                            S.op('act', lambda e: e.activation(out=pt_[0:ksz, 0:w], in_=PS[psb][0:ksz, 0:w],
                                                               func=AF.Exp, scale=MLA_SCALE),
                                 R=[('ps', psb)], W=[('PT', i % 3)])
                            if seq.prompt and qs == kt * 128:
                                S.op('dve', lambda e: e.memset(pt_[64:128, 0:64], 0.0), W=[('PT', i % 3)])
                            if pend is not None:
                                pend()
                            def pv(i=i, kt=kt, ksz=ksz, qs=qs, w=w, pt_=pt_):
                                S.mm([lambda e: e.matmul(PS[po][:, qs - q0:qs - q0 + w], vp_[0:ksz, kt, :],
                                                         pt_[0:ksz, 0:w], start=(i == 0), stop=(i == nk_ - 1))],
                                     R=['VP', 'VP1', ('PT', i % 3)], W=[('ps', po)])
                            pend = pv
                        if pend is not None:
                            pend()
                        rs_ = rs[pocount % 2]
                        S.op('dve', lambda e: e.reciprocal(rs_[64:128, 0:nq], PS[po][64:128, 0:nq]),
                             R=[('ps', po)], W=[('rs', pocount % 2)])
                        pr = (h % 2) * 64
                        S.op('dve', lambda e: e.tensor_tensor(xT[pr:pr + 64, h // 2, jq0:jq0 + nq], PS[po][0:64, 0:nq],
                                                              rs_[64:128, 0:nq], ALU.mult),
                             R=[('ps', po), ('rs', pocount % 2)], W=[('xT', t, 0) for t in tiles_of(jq0, nq)])


        GSTOP = int(os.environ.get('KGSTOP', '0'))

        def gdn_phase_real(st, job, l, mixB):
            if GSTOP:
                stub_zero(st, job, mixB)
            wg = wgP
            gcw = sb(st, "gcw", [128, 6, 4])
            gst = sb(st, "gst", [4, 768])
            gn2 = sb(st, "gn2", [128, 1])
            dtb = sb(st, "dtb", [64, 4])
            negA = sb(st, "negA", [64, 4])
            raw = sb(st, "raw", [128, 6, 3 + 512])
            acc = [sb(st, "gacc%d" % i, [128, 512]) for i in range(2)]
            s32 = [sb(st, "s32%d" % i, [128, 512]) for i in range(2)]
            sq32 = sb(st, "sq32", [128, 512])
            rn = sb(st, "grn", [128, 512])
            rtmp = sb(st, "grtmp", [128, 512])
            qk64 = sb(st, "qk64", [64, 8, 512], BF16)
            vT = sb(st, "vT", [128, 2, 512], BF16)
            zsT = sb(st, "zsT", [128, 2, 512], BF16)
            Sst = sb(st, "Sst", [64, 4, 64])
            Sb = sb(st, "Sb", [64, 4, 64], BF16)
            beta = sb(st, "beta", [64, 32])
            nbeta = sb(st, "nbeta", [64, 32])
            t4 = sb(st, "t4", [64, 32])
            gal = sb(st, "gal", [64, 32])
            Gs = sb(st, "Gs", [64, 32])
            gam = sb(st, "gam", [64, 32])
            ee = sb(st, "ee", [64, 32])
            bg = sb(st, "bg", [64, 32])
            gamL = sb(st, "gamL", [128, 32])
            CH = []
            for bs_ in range(3):
                n_ = lambda x: "%s%d" % (x, bs_)
                CH.append((sb(st, n_("kbg"), [64, 4, 64], BF16), sb(st, n_("ke"), [64, 4, 64], BF16),
                           sb(st, n_("vb"), [64, 4, 64], BF16), sb(st, n_("gTri"), [64, 4, 64]),
                           sb(st, n_("D0"), [64, 4, 64]), sb(st, n_("Dn"), [64, 4, 64]), sb(st, n_("Dp"), [64, 4, 64]),
                           sb(st, n_("decT"), [64, 4, 64]), sb(st, n_("decA"), [64, 4, 64]),
                           sb(st, n_("gamB"), [128, 4, 64]), sb(st, n_("qgT"), [64, 4, 64], BF16),
                           [sb(st, n_("BC%d" % i), [64, 8, 64], BF16) for i in range(2)],
                           sb(st, n_("Pm"), [64, 4, 64], BF16), sb(st, n_("MTd"), [64, 4, 64], BF16),
                           sb(st, n_("nWT"), [64, 4, 64], BF16), sb(st, n_("Ub"), [64, 4, 64], BF16)))
            NPAR = int(os.environ.get("KNPAR", "3"))
            osb = sb(st, "osb", [64, 8, 256])
            sqo = sb(st, "sqo", [64, 8, 256])
            on_ = sb(st, "on_", [64, 8, 256], BF16)
            sso = sb(st, "sso", [64, 32])
            rno = sb(st, "rno", [64, 32])
            rtm = sb(st, "rtm", [64, 32])

            S.dma('sp', gst[0:4, :], gcw_d[l], W=['gst'])
            for fc in range(6):
                tr_f32(PS[1][:, fc * 4:fc * 4 + 4], gst[0:4, fc * 128:(fc + 1) * 128], 4, ['gst'], [('ps', 1)])
            S.op('dve', lambda e: e.tensor_copy(gcw[:, :, :], PS[1][:, 0:24].rearrange("p (f j) -> p f j", j=4)),
                 R=[('ps', 1)], W=['gcw'])
            for hf in range(2):
                S.dma('sp', gn2[hf * 64:(hf + 1) * 64, :], gdn_norm[l, :].rearrange("(p o) -> p o", o=1), W=['gn2'])
            S.dma('sp', dtb[:], dt_bias[l:l + 1, :].partition_broadcast(64), W=['dtb'])
            S.dma('sp', negA[:], a_log[l:l + 1, :].partition_broadcast(64), W=['negA'])
            S.op('act', lambda e: e.activation(out=negA[:], in_=negA[:], func=AF.Exp), W=['negA'])
            S.op('dve', lambda e: e.tensor_scalar(negA[:], negA[:], -1.0, None, ALU.mult), W=['negA'])

            if GSTOP == 1:
                return
            for seq in job.seqs:
                L = seq.L
                L4 = 4 * L
                nlev = {64: 5, 16: 3}[L]
                if seq.prompt:
                    S.op('dve', lambda e: e.memset(Sst[:], 0.0), W=['Sst'])
                    S.op('dve', lambda e: e.memset(raw[:, :, 0:3], 0.0), W=['raw'])
                    o_gdn = o_gdn_p[l, seq.idx]
                    o_gcv = o_gcv_p[l, seq.idx]
                else:
                    S.dma('sp', Sst[:], s_gdn[l, seq.idx].rearrange("h d e -> d h e"), W=['Sst'])
                    S.dma('sp', gst[0:3, :], s_gcv[l, seq.idx], W=['gst'])
                    for fc in range(6):
                        tr_f32(PS[1][:, fc * 4:fc * 4 + 3], gst[0:3, fc * 128:(fc + 1) * 128], 3, ['gst'], [('ps', 1)])
                    S.op('dve', lambda e: e.tensor_copy(raw[:, :, 0:3],
                                                        PS[1][:, 0:24].rearrange("p (f j) -> p f j", j=4)[:, :, 0:3]),
                         R=[('ps', 1)], W=['raw'])
                    o_gdn = o_gdn_s[l, seq.idx]
                    o_gcv = o_gcv_s[l, seq.idx]
                S.op('act', lambda e: e.activation(out=Sb[:], in_=Sst[:], func=AF.Copy), R=['Sst'], W=['Sb'])
                for bi, (b0, nt) in enumerate(seq.blocks):
                    cols = seq.c0 + b0
                    lastb = (bi == len(seq.blocks) - 1)
                    nch = nt // L
                    n4 = nch * 4
                    xr = xT_R(cols, nt)
                    QKR = [('qkT', f_, h_) for f_ in range(4) for h_ in range(2)]
                    for fc in range(6):
                        pk = fc % 2
                        S.mm([lambda e, k=k, fc=fc, pk=pk: e.matmul(PS[pk][:, 0:nt], wg[:, k, fc * 128:(fc + 1) * 128],
                                                                    xT[:, k, cols:cols + nt], start=(k == 0),
                                                                    stop=(k == 7)) for k in range(8)],
                             R=['wg'] + xr, W=[('ps', pk)])
                        S.op('act', lambda e, fc=fc, pk=pk: e.activation(out=raw[:, fc, 3:3 + nt], in_=PS[pk][:, 0:nt],
                                                                        func=AF.Copy), R=[('ps', pk)], W=['raw'])
                    if GSTOP == 2:
                        return
                    for fc in range(6):
                        a_ = acc[fc % 2]
                        ak = ('gacc', fc % 2)
                        S.op('dve', lambda e, fc=fc, a_=a_: e.tensor_scalar(a_[:, 0:nt], raw[:, fc, 0:nt],
                                                                           gcw[:, fc, 0:1], None, ALU.mult),
                             R=['raw', 'gcw'], W=[ak])
                        for j in range(1, 4):
                            S.op('dve', lambda e, fc=fc, a_=a_, j=j: e.scalar_tensor_tensor(
                                out=a_[:, 0:nt], in0=raw[:, fc, j:j + nt], scalar=gcw[:, fc, j:j + 1], in1=a_[:, 0:nt],
                                op0=ALU.mult, op1=ALU.add), R=['raw', 'gcw'], W=[ak])
                        if fc >= 4:
                            S.op('act', lambda e, fc=fc, a_=a_: e.activation(out=vT[:, fc - 4, 0:nt], in_=a_[:, 0:nt],
                                                                            func=AF.Silu), R=[ak], W=['vT'])
                            continue
                        s_ = s32[fc % 2]
                        sk = ('s32', fc % 2)
                        S.op('act', lambda e, a_=a_, s_=s_: e.activation(out=s_[:, 0:nt], in_=a_[:, 0:nt], func=AF.Silu),
                             R=[ak], W=[sk])
                        S.op('act', lambda e, s_=s_: e.activation(out=sq32[:, 0:nt], in_=s_[:, 0:nt], func=AF.Square),
                             R=[sk], W=['sq32'])
                        pk = fc % 2
                        S.mm([lambda e, pk=pk: e.matmul(PS[pk][:, 0:nt], BONES, sq32[:, 0:nt], start=True, stop=True)],
                             R=['sq32', 'CB'], W=[('ps', pk)])
                        rsqrt_act(rn[:, 0:nt], PS[pk][:, 0:nt], 1.0, 1e-6, [('ps', pk)], ['grn'], rtmp[:, 0:nt], 'grtmp')
                        cq_ = 0.125 if fc < 2 else 1.0
                        for half in range(2):
                            hp = slice(half * 64, half * 64 + 64)
                            S.op('dve', lambda e, fc=fc, s_=s_, cq_=cq_, hp=hp, half=half: e.scalar_tensor_tensor(
                                out=qk64[0:64, fc * 2 + half, 0:nt], in0=s_[hp, 0:nt], scalar=cq_, in1=rn[hp, 0:nt],
                                op0=ALU.mult, op1=ALU.mult), R=[sk, 'grn'], W=[('qkT', fc, half)])
                    if GSTOP == 3:
                        return
                    if lastb:
                        S.mm([lambda e, fc=fc: e.matmul(PS[0][0:3, fc * 128:(fc + 1) * 128], raw[:, fc, nt:nt + 3], IDF,
                                                        start=True, stop=True) for fc in range(4)],
                             R=['raw', 'CB'], W=[('ps', 0)])
                        S.mm([lambda e, fc=fc: e.matmul(PS[1][0:3, (fc - 4) * 128:(fc - 3) * 128], raw[:, fc, nt:nt + 3],
                                                        IDF, start=True, stop=True) for fc in range(4, 6)],
                             R=['raw', 'CB'], W=[('ps', 1)])
                        S.op('dve', lambda e: e.tensor_copy(gst[0:3, 0:512], PS[0][0:3, 0:512]), R=[('ps', 0)],
                             W=['gst'])
                        S.op('dve', lambda e: e.tensor_copy(gst[0:3, 512:768], PS[1][0:3, 0:256]), R=[('ps', 1)],
                             W=['gst'])
                        S.dma('sp', o_gcv, gst[0:3, :], R=['gst'], W=['o_gcv'])
                    else:
                        S.op('act', lambda e: e.activation(out=raw[:, :, 0:3], in_=raw[:, :, nt:nt + 3], func=AF.Copy),
                             R=['raw'], W=['raw'])
                    if GSTOP == 4:
                        return
                    for fc in range(2):
                        S.mm([lambda e, k=k, fc=fc: e.matmul(PS[fc][:, 0:nt], wg[:, k, 776 + fc * 128:776 + (fc + 1) * 128],
                                                             xT[:, k, cols:cols + nt], start=(k == 0), stop=(k == 7))
                              for k in range(8)], R=['wg'] + xr, W=[('ps', fc)])
                        S.op('act', lambda e, fc=fc: e.activation(out=zsT[:, fc, 0:nt], in_=PS[fc][:, 0:nt],
                                                                 func=AF.Silu), R=[('ps', fc)], W=['zsT'])
                    if GSTOP == 5:
                        return
                    for ci in range(nch):
                        cc = cols + ci * L
                        S.mm([lambda e, k=k, ci=ci, cc=cc: e.matmul(PS[1][0:L, ci * 8:(ci + 1) * 8], xT[:, k, cc:cc + L],
                                                                    wg[:, k, 768:776], start=(k == 0), stop=(k == 7))
                              for k in range(8)], R=['wg'] + xr, W=[('ps', 1)])
                    pba = PS[1][0:L, 0:nch * 8].rearrange("p (c e) -> p c e", e=8)
                    v3 = lambda tl_: tl_[0:L, 0:n4].rearrange("p (c h) -> p c h", h=4)
                    S.op('act', lambda e: e.activation(out=v3(beta), in_=pba[:, :, 0:4], func=AF.Sigmoid),
                         R=[('ps', 1)], W=['beta'])
                    S.op('dve', lambda e: e.tensor_scalar(nbeta[0:L, 0:n4], beta[0:L, 0:n4], -1.0, None, ALU.mult),
                         R=['beta'], W=['nbeta'])
                    S.op('dve', lambda e: e.tensor_tensor(v3(t4), pba[:, :, 4:8],
                                                          dtb[0:L, :].unsqueeze(1).to_broadcast([L, nch, 4]), ALU.add),
                         R=[('ps', 1), 'dtb'], W=['t4'])
                    S.op('act', lambda e: e.activation(out=t4[0:L, 0:n4], in_=t4[0:L, 0:n4], func=AF.Exp), W=['t4'])
                    S.op('act', lambda e: e.activation(out=t4[0:L, 0:n4], in_=t4[0:L, 0:n4], func=AF.Ln, bias=1.0,
                                                       scale=1.0), W=['t4'])
                    S.op('dve', lambda e: e.tensor_tensor(v3(gal), v3(t4),
                                                          negA[0:L, :].unsqueeze(1).to_broadcast([L, nch, 4]), ALU.mult),
                         R=['t4', 'negA'], W=['gal'])
                    if GSTOP == 6:
                        return
                    S.mm([lambda e: e.matmul(PS[1][0:L, 64:64 + n4], maskU(L), gal[0:L, 0:n4], start=True, stop=True),
                          lambda e: e.matmul(PS[1][:, 96:96 + n4], ONES[0:L, :], gal[0:L, 0:n4], start=True, stop=True)],
                         R=['gal', 'CB', 'ONES'], W=[('ps', 1)])
                    S.op('dve', lambda e: e.tensor_copy(Gs[0:L, 0:n4], PS[1][0:L, 64:64 + n4]), R=[('ps', 1)],
                         W=['Gs'])
                    S.op('act', lambda e: e.activation(out=gam[0:L, 0:n4], in_=PS[1][0:L, 64:64 + n4], func=AF.Exp),
                         R=[('ps', 1)], W=['gam'])
                    S.op('act', lambda e: e.activation(out=gamL[:, 0:n4], in_=PS[1][:, 96:96 + n4], func=AF.Exp),
                         R=[('ps', 1)], W=['gamL'])
                    S.op('dve', lambda e: e.tensor_tensor(ee[0:L, 0:n4], PS[1][0:L, 96:96 + n4], Gs[0:L, 0:n4],
                                                          ALU.subtract), R=[('ps', 1), 'Gs'], W=['ee'])
                    S.op('act', lambda e: e.activation(out=ee[0:L, 0:n4], in_=ee[0:L, 0:n4], func=AF.Exp), W=['ee'])
                    S.op('dve', lambda e: e.tensor_tensor(bg[0:L, 0:n4], beta[0:L, 0:n4], gam[0:L, 0:n4], ALU.mult),
                         R=['beta', 'gam'], W=['bg'])
                    if GSTOP == 7:
                        return
                    tail_turn = [0]

                    def chunk_gen(ci):
                        ps_, bs = ci % 2, ci % 3
                        x0, x1, x2 = 2 + 3 * ps_, 3 + 3 * ps_, 4 + 3 * ps_
                        X0, X1, X2 = PS[x0], PS[x1], PS[x2]
                        kbg, ke, vb, gTri, D0, Dn, Dp, decT, decA, gamB, qgT, BC, Pm, MTd, nWT, Ub = CH[bs]
                        K_ = lambda n: (n, bs)
                        lc = ci * L
                        c4 = slice(ci * 4, ci * 4 + 4)
                        b3 = lambda tl_, n=L: tl_[0:L, c4].unsqueeze(2).to_broadcast([L, 4, n])
                        fns = []
                        for hh in range(4):
                            fns.append(lambda e, hh=hh: e.matmul(X0[0:L, hh * 64:(hh + 1) * 64],
                                                                 qk64[0:64, 4 + hh, lc:lc + L], IDB[0:64, 0:64],
                                                                 start=True, stop=True))
                        for fc in range(2):
                            fns.append(lambda e, fc=fc: e.matmul(X0[0:L, 256 + fc * 128:256 + (fc + 1) * 128],
                                                                 vT[:, fc, lc:lc + L], IDB, start=True, stop=True))
                        S.mm(fns, R=QKR + ['vT', 'CBb'], W=[('ps', x0, 'a'), ('ps', x0, 'b')])
                        yield
                        ktok = X0[0:L, 0:256].rearrange("p (h d) -> p h d", d=64)
                        vtok = X0[0:L, 256:512].rearrange("p (h d) -> p h d", d=64)
                        S.op('dve', lambda e: e.tensor_tensor(kbg[0:L, :, :], ktok, b3(bg, 64), ALU.mult),
                             R=[('ps', x0, 'a'), 'bg'], W=[K_('kbg')])
                        yield
                        S.op('dve', lambda e: e.tensor_tensor(ke[0:L, :, :], ktok, b3(ee, 64), ALU.mult),
                             R=[('ps', x0, 'a'), 'ee'], W=[K_('ke')])
                        yield
                        S.op('dve', lambda e: e.tensor_tensor(vb[0:L, :, :], vtok, b3(beta, 64), ALU.mult),
                             R=[('ps', x0, 'b'), 'beta'], W=[K_('vb')])
                        yield
                        S.op('dve', lambda e: e.tensor_tensor(gTri[0:L, :, 0:L],
                                                              maskU(L).unsqueeze(1).to_broadcast([L, 4, L]), b3(gal),
                                                              ALU.mult), R=['gal', 'CB'], W=[K_('gTri')])
                        yield
                        if L == 64:
                            gtv = gTri[0:L, :, :].rearrange("p h i -> p (h i)")
                            S.mm([lambda e: e.matmul(X1[:, 0:L4], ONES[0:L, :], gtv, start=True, stop=True)],
                                 R=[K_('gTri'), 'ONES'], W=[('ps', x1, 'a')])
                            yield
                        else:
                            S.mm([lambda e, hh=hh: e.matmul(X1[:, hh * L:(hh + 1) * L], ONES[0:L, :],
                                                            gTri[0:L, hh, 0:L], start=True, stop=True)
                                  for hh in range(4)], R=[K_('gTri'), 'ONES'], W=[('ps', x1, 'a')])
                            yield
                        pgb = X1[0:L, 0:L4].rearrange("p (h i) -> p h i", i=L)
                        S.op('dve', lambda e: e.tensor_tensor(D0[0:L, :, 0:L], pgb, b3(Gs), ALU.subtract),
                             R=[('ps', x1, 'a'), 'Gs'], W=[K_('D0')])
                        yield
                        S.op('dve', lambda e: e.tensor_scalar(Dn[0:L, :, 0:L], D0[0:L, :, 0:L], 0.0, None, ALU.min),
                             R=[K_('D0')], W=[K_('Dn')])
                        yield
                        S.op('dve', lambda e: e.tensor_scalar(Dp[0:L, :, 0:L], D0[0:L, :, 0:L], 0.0, None, ALU.max),
                             R=[K_('D0')], W=[K_('Dp')])
                        yield
                        S.op('act', lambda e: e.activation(out=decT[0:L, :, 0:L], in_=Dn[0:L, :, 0:L], func=AF.Exp),
                             R=[K_('Dn')], W=[K_('decT')])
                        yield
                        S.op('dve', lambda e: e.tensor_tensor(decT[0:L, :, 0:L], decT[0:L, :, 0:L],
                                                              maskU(L).unsqueeze(1).to_broadcast([L, 4, L]), ALU.mult),
                             R=['CB'], W=[K_('decT')])
                        yield
                        S.op('act', lambda e: e.activation(out=decA[0:L, :, 0:L], in_=Dp[0:L, :, 0:L], func=AF.Exp,
                                                           scale=-1.0), R=[K_('Dp')], W=[K_('decA')])
                        yield
                        S.op('dve', lambda e: e.tensor_tensor(decA[0:L, :, 0:L], decA[0:L, :, 0:L],
                                                              maskLs(L).unsqueeze(1).to_broadcast([L, 4, L]), ALU.mult),
                             R=['CB'], W=[K_('decA')])
                        yield
                        S.op('dve', lambda e: e.tensor_tensor(decA[0:L, :, 0:L], decA[0:L, :, 0:L], b3(nbeta), ALU.mult),
                             R=['nbeta'], W=[K_('decA')])
                        yield
                        S.op('act', lambda e: e.activation(out=gamB[:, :, 0:L],
                                                           in_=X1[:, 0:L4].rearrange("p (h i) -> p h i", i=L),
                                                           func=AF.Exp), R=[('ps', x1, 'a')], W=[K_('gamB')])
                        yield
                        S.op('dve', lambda e: e.tensor_tensor(qgT[0:64, :, 0:L], qk64[0:64, 0:4, lc:lc + L],
                                                              gamB[0:64, :, 0:L], ALU.mult),
                             R=QKR + [K_('gamB')], W=[K_('qgT')])
                        yield
                        fns = []
                        for hh in range(4):
                            kTh = qk64[0:64, 4 + hh, lc:lc + L]
                            qTh = qk64[0:64, hh, lc:lc + L]
                            fns.append(lambda e, hh=hh, kTh=kTh: e.matmul(X0[0:L, hh * L:(hh + 1) * L], kTh, kTh,
                                                                          start=True, stop=True))
                            fns.append(lambda e, hh=hh, kTh=kTh, qTh=qTh: e.matmul(
                                X0[0:L, 256 + hh * L:256 + (hh + 1) * L], kTh, qTh, start=True, stop=True))
                        S.mm(fns, R=QKR, W=[('ps', x0, 'a'), ('ps', x0, 'b')])
                        yield
                        B0, C0 = BC[0][0:L, 0:4, 0:L], BC[0][0:L, 4:8, 0:L]
                        S.op('dve', lambda e: e.tensor_tensor(B0, X0[0:L, 0:L4].rearrange("p (h i) -> p h i", i=L),
                                                              decA[0:L, :, 0:L], ALU.mult),
                             R=[('ps', x0, 'a'), K_('decA')], W=[('BC', bs, 0)])
                        yield
                        S.op('dve', lambda e: e.tensor_tensor(MTd[0:L, :, 0:L],
                                                              X0[0:L, 256:256 + L4].rearrange("p (h i) -> p h i", i=L),
                                                              decT[0:L, :, 0:L], ALU.mult),
                             R=[('ps', x0, 'b'), K_('decT')], W=[K_('MTd')])
                        yield
                        S.mm([lambda e, hh=hh: e.matmul(X1[0:L, 256 + hh * L:256 + (hh + 1) * L],
                                                        BC[0][0:L, hh, 0:L], IDB[0:L, 0:L], start=True, stop=True)
                              for hh in range(4)], R=[('BC', bs, 0), 'CBb'], W=[('ps', x1, 'b')])
                        yield
                        pc = X1[0:L, 256:256 + L4].rearrange("p (h i) -> p h i", i=L)
                        S.op('act', lambda e: e.activation(out=C0, in_=pc, func=AF.Copy), R=[('ps', x1, 'b')],
                             W=[('BC', bs, 0)])
                        yield
                        S.op('dve', lambda e: e.tensor_tensor(Pm[0:L, :, 0:L], pc,
                                                              IDF[0:L, 0:L].unsqueeze(1).to_broadcast([L, 4, L]),
                                                              ALU.add), R=[('ps', x1, 'b'), 'CB'], W=[K_('Pm')])
                        yield
                        cur = 0
                        for lev in range(1, nlev + 1):
                            nxt = 1 - cur
                            Bc = lambda hh: BC[cur][0:L, hh, 0:L]
                            Cc = lambda hh: BC[cur][0:L, 4 + hh, 0:L]
                            fns = []
                            for hh in range(4):
                                fns.append(lambda e, hh=hh: e.matmul(X2[0:L, hh * L:(hh + 1) * L], Cc(hh), Bc(hh),
                                                                     start=True, stop=True))
                                if lev < nlev:
                                    fns.append(lambda e, hh=hh: e.matmul(X2[0:L, 256 + hh * L:256 + (hh + 1) * L],
                                                                         Bc(hh), Cc(hh), start=True, stop=True))
                            S.mm(fns, R=[('BC', bs, cur)], W=[('ps', x2)])
                            yield
                            ncp = 8 if lev < nlev else 4
                            if L == 64:
                                S.op('act', lambda e: e.activation(
                                    out=BC[nxt][0:L, 0:ncp, 0:L],
                                    in_=X2[0:L, 0:ncp * L].rearrange("p (h i) -> p h i", i=L), func=AF.Copy),
                                     R=[('ps', x2)], W=[('BC', bs, nxt)])
                                yield
                            else:
                                for part in range(ncp // 4):
                                    S.op('act', lambda e, part=part: e.activation(
                                        out=BC[nxt][0:L, 4 * part:4 * part + 4, 0:L],
                                        in_=X2[0:L, 256 * part:256 * part + L4].rearrange("p (h i) -> p h i", i=L),
                                        func=AF.Copy), R=[('ps', x2)], W=[('BC', bs, nxt)])
                                    yield
                            cur = nxt
                            S.mm([lambda e, hh=hh: e.matmul(X1[0:L, hh * L:(hh + 1) * L], BC[cur][0:L, hh, 0:L],
                                                            Pm[0:L, hh, 0:L], start=True, stop=True)
                                  for hh in range(4)], R=[('BC', bs, cur), K_('Pm')], W=[('ps', x1, 'a')])
                            yield
                            S.op('dve', lambda e: e.tensor_tensor(
                                Pm[0:L, :, 0:L], X1[0:L, 0:L4].rearrange("p (h i) -> p h i", i=L), Pm[0:L, :, 0:L],
                                ALU.add), R=[('ps', x1, 'a')], W=[K_('Pm')])
                            yield
                        S.mm([lambda e, hh=hh: e.matmul(X1[0:64, 256 + hh * L:256 + (hh + 1) * L], kbg[0:L, hh, :],
                                                        Pm[0:L, hh, 0:L], start=True, stop=True) for hh in range(4)],
                             R=[K_('kbg'), K_('Pm')], W=[('ps', x1, 'b')])
                        yield
                        S.op('act', lambda e: e.activation(
                            out=nWT[0:64, :, 0:L], in_=X1[0:64, 256:256 + L4].rearrange("p (h i) -> p h i", i=L),
                            func=AF.Identity, bias=0.0, scale=-1.0), R=[('ps', x1, 'b')], W=[K_('nWT')])
                        yield
                        nsolve[0] -= 1
                        while tail_turn[0] != ci:
                            yield
                        fns = []
                        for hh in range(4):
                            fns.append(lambda e, hh=hh: e.matmul(PS[0][0:L, hh * 64:(hh + 1) * 64], Pm[0:L, hh, 0:L],
                                                                 vb[0:L, hh, :], start=True, stop=False))
                            fns.append(lambda e, hh=hh: e.matmul(PS[0][0:L, hh * 64:(hh + 1) * 64], nWT[0:64, hh, 0:L],
                                                                 Sb[0:64, hh, :], start=False, stop=True))
                        S.mm(fns, R=[K_('Pm'), K_('vb'), K_('nWT'), 'Sb'], W=[('ps', 0)])
                        yield
                        S.op('act', lambda e: e.activation(out=Ub[0:L, :, :],
                                                           in_=PS[0][0:L, 0:256].rearrange("p (h d) -> p h d", d=64),
                                                           func=AF.Copy), R=[('ps', 0)], W=[K_('Ub')])
                        yield
                        fns = []
                        for hh in range(4):
                            fns.append(lambda e, hh=hh: e.matmul(PS[0][0:L, 256 + hh * 64:256 + (hh + 1) * 64],
                                                                 MTd[0:L, hh, 0:L], Ub[0:L, hh, :], start=True,
                                                                 stop=False))
                            fns.append(lambda e, hh=hh: e.matmul(PS[0][0:L, 256 + hh * 64:256 + (hh + 1) * 64],
                                                                 qgT[0:64, hh, 0:L], Sb[0:64, hh, :], start=False,
                                                                 stop=True))
                        S.mm(fns, R=[K_('MTd'), K_('Ub'), K_('qgT'), 'Sb'], W=[('ps', 0)])
                        yield
                        S.op('act', lambda e: e.activation(out=osb[0:L, ci, :], in_=PS[0][0:L, 256:512], func=AF.Copy),
                             R=[('ps', 0)], W=[('osb', ci)])
                        yield
                        S.mm([lambda e, hh=hh: e.matmul(PS[1][0:64, hh * 64:(hh + 1) * 64], ke[0:L, hh, :],
                                                        Ub[0:L, hh, :], start=True, stop=True) for hh in range(4)],
                             R=[K_('ke'), K_('Ub')], W=[('ps', 1)])
                        yield
                        S.op('dve', lambda e: e.tensor_tensor(Sst[:, :, :], Sst[:, :, :],
                                                              gamL[0:64, c4].unsqueeze(2).to_broadcast([64, 4, 64]),
                                                              ALU.mult), R=['gamL'], W=['Sst'])
                        yield
                        S.op('dve', lambda e: e.tensor_tensor(Sst[:, :, :], Sst[:, :, :],
                                                              PS[1][0:64, 0:256].rearrange("p (h d) -> p h d", d=64),
                                                              ALU.add), R=[('ps', 1)], W=['Sst'])
                        yield
                        S.op('act', lambda e: e.activation(out=Sb[:], in_=Sst[:], func=AF.Copy), R=['Sst'], W=['Sb'])
                        yield
                        tail_turn[0] = ci + 1


                    pending = list(range(nch))
                    running = []
                    nsolve = [0]
                    while pending or running:
                        while pending and nsolve[0] < 2 and len(running) < NPAR:
                            c_ = pending.pop(0)
                            nsolve[0] += 1
                            running.append(chunk_gen(c_))
                        for g_ in list(running):
                            try:
                                next(g_)
                            except StopIteration:
                                running.remove(g_)
                    ov = osb[0:L, 0:nch, :]
                    S.op('act', lambda e: e.activation(out=sqo[0:L, 0:nch, :], in_=ov, func=AF.Square),
                         R=[('osb', ci) for ci in range(nch)], W=['sqo'])
                    S.op('dve', lambda e: e.tensor_reduce(out=sso[0:L, 0:n4],
                                                          in_=sqo[0:L, 0:nch, :].rearrange("p c (h d) -> p (c h) d", d=64),
                                                          axis=AX.X, op=ALU.add), R=['sqo'], W=['sso'])
                    rsqrt_act(rno[0:L, 0:n4], sso[0:L, 0:n4], 1.0 / 64, 1e-6, ['sso'], ['rno'], rtm[0:L, 0:n4], 'rtm')
                    S.op('dve', lambda e: e.tensor_tensor(
                        on_[0:L, 0:nch, :].rearrange("p c (h d) -> p (c h) d", d=64),
                        osb[0:L, 0:nch, :].rearrange("p c (h d) -> p (c h) d", d=64),
                        rno[0:L, 0:n4].unsqueeze(2).to_broadcast([L, n4, 64]), ALU.mult),
                         R=['rno'] + [('osb', ci) for ci in range(nch)], W=['on_'])
                    for fc in range(2):
                        S.mm([lambda e, ci=ci, fc=fc: e.matmul(PS[fc][:, ci * L:(ci + 1) * L],
                                                               on_[0:L, ci, fc * 128:(fc + 1) * 128], IDB[0:L, 0:L],
                                                               start=True, stop=True) for ci in range(nch)],
                             R=['on_', 'CBb'], W=[('ps', fc)])
                        S.op('dve', lambda e, fc=fc: e.scalar_tensor_tensor(
                            out=mixB[:, fc, cols:cols + nt], in0=PS[fc][:, 0:nt], scalar=gn2[:, 0:1], in1=zsT[:, fc, 0:nt],
                            op0=ALU.mult, op1=ALU.mult), R=[('ps', fc), 'gn2', 'zsT'],
                             W=[('mixB', t) for t in tiles_of(cols, nt)])
                S.dma('sp', o_gdn.rearrange("h d e -> d h e"), Sst[:], R=['Sst'], W=['o_gdn'])


        jobs = [Job(True, 0), Job(True, 1), Job(False, 0)]
        KJ = os.environ.get("KJOBS", "")
        if KJ == "s":
            jobs = [Job(False, 0)]
        elif KJ == "p":
            jobs = [Job(True, 0)]
        NLAY = int(os.environ.get("KLAYERS", str(DEPTH)))

        load_wg(0)
        for ji, job in enumerate(jobs):
            T = job.T
            if job.prompt:
                x_src = xp[job.pidx * SEQ:(job.pidx + 1) * SEQ, :]
                y_dst = y_p[job.pidx * SEQ:(job.pidx + 1) * SEQ, :]
                rope_src = rope_p
            else:
                x_src = xs
                y_dst = y_s
                rope_src = rope_s

            with contextlib.ExitStack() as st:
                xin = [sb(st, "xin%d" % i, [128, D]) for i in range(2)]
                xinb = [sb(st, "xinb%d" % i, [128, D], BF16) for i in range(2)]
                for ti, (r0, tsz) in enumerate(job.tiles):
                    b = ti % 2
                    S.dma('sp', xin[b][0:tsz, :], x_src[r0:r0 + tsz, :], W=[('xin', b)])
                    S.op('act', lambda e: e.activation(out=xinb[b][0:tsz, :], in_=xin[b][0:tsz, :], func=AF.Copy),
                         R=[('xin', b)], W=[('xinb', b)])
                    for hf in range(2):
                        pk = 2 * b + hf
                        fns = []
                        for k in range(4):
                            kk = hf * 4 + k
                            fns.append(lambda e, k=k, kk=kk, pk=pk: e.matmul(
                                PS[pk][:, k * 128:k * 128 + tsz], xinb[b][0:tsz, kk * 128:(kk + 1) * 128],
                                IDB[0:tsz, 0:tsz], start=True, stop=True))
                        S.mm(fns, R=[('xinb', b), 'CBb'], W=[('ps', pk)])
                        S.op('dve' if hf == 0 else 'act', lambda e, hf=hf, pk=pk: (
                            e.tensor_copy(xT[:, hf * 4:(hf + 1) * 4, r0:r0 + tsz],
                                          PS[pk][:, :].rearrange("p (k t) -> p k t", t=128)[:, :, 0:tsz])
                            if hf == 0 else
                            e.activation(out=xT[:, hf * 4:(hf + 1) * 4, r0:r0 + tsz],
                                         in_=PS[pk][:, :].rearrange("p (k t) -> p k t", t=128)[:, :, 0:tsz],
                                         func=AF.Copy)),
                             R=[('ps', pk)], W=[('xT', r0 // 128, hf)])
                S.barrier()

            for l in range(NLAY):
                last = (l == DEPTH - 1)
                res1_src = x_src if l == 0 else xs2[0:T, :]
                out2_dst = y_dst if last else xs2[0:T, :]
                with contextlib.ExitStack() as mix:
                    mixB = sb(mix, "mixB", [128, 4, T], BF16)
                    wo = sb(mix, "wo", [128, 8, D], BF16)
                    with contextlib.ExitStack() as st:
                        gdn_phase(st, job, l, mixB)
                        S.barrier()
                    with contextlib.ExitStack() as st:
                        conv_phase(st, job, l, mixB)
                        S.barrier()
                    with contextlib.ExitStack() as mla:
                        QT = sb(mla, "QT", [96, 8, T], BF16)
                        if job.prompt:
                            ckvT = sb(mla, "ckvT", [128, 2, job.Tk], BF16)
                            kpeT = sb(mla, "kpeT", [32, job.Tk], BF16)
                            ckvN = kpeN = None
                        else:
                            ckvT = kpeT = None
                            ckvN = sb(mla, "ckvN", [128, 2, T], BF16)
                            kpeN = sb(mla, "kpeN", [32, T], BF16)
                        with contextlib.ExitStack() as st:
                            mla_proj_phase(st, job, l, QT, ckvT, kpeT, ckvN, kpeN, rope_src)
                            S.barrier()
                        with contextlib.ExitStack() as st:
                            attn_phase(st, job, l, QT, ckvT, kpeT, ckvN, kpeN)
                            for k in range(8):
                                S.dma('pool', wo[:, k, :], w_out[l, k * 128:(k + 1) * 128, :], W=[('wo', k)])
                            S.barrier()
                    with contextlib.ExitStack() as st:
                        g_b = sb(st, "g_b", [128, D])
                        b_b = sb(st, "b_b", [128, D])
                        S.dma('sp', g_b[:], ln1_g[l:l + 1, :].partition_broadcast(128), W=['lnp'])
                        S.dma('sp', b_b[:], ln1_b[l:l + 1, :].partition_broadcast(128), W=['lnp'])
                        lnt = [(sb(st, "xres%d" % i, [128, D]),
                                sb(st, "xb%d" % i, [128, D], BF16), sb(st, "st6%d" % i, [128, NBN, 6]),
                                sb(st, "mv%d" % i, [128, 2]), sb(st, "sc%d" % i, [128, 3])) for i in range(2)]
                        for ti, (r0, tsz) in enumerate(job.tiles):
                            b = ti % 2
                            pk = (2 * b, 2 * b + 1)
                            for hf in range(2):
                                fns = []
                                for k in range(8):
                                    lhs = xT[:, k, r0:r0 + tsz] if k < 4 else mixB[:, k - 4, r0:r0 + tsz]
                                    fns.append(lambda e, k=k, lhs=lhs, hf=hf: e.matmul(
                                        PS[pk[hf]][0:tsz, 0:512], lhs, wo[:, k, hf * 512:(hf + 1) * 512],
                                        start=(k == 0), stop=(k == 7)))
                                S.mm(fns, R=[('wo', k) for k in range(8)] + [('xT', ti, 0), ('mixB', ti)],
                                     W=[('ps', pk[hf])])
                            layernorm_tile(lnt[b], pk, tsz, res1_src[r0:r0 + tsz, :], [('xs2', ti)], g_b, b_b,
                                           xs1[r0:r0 + tsz, :], [('xs1', ti)], (r0, tsz), True, 'l%d' % b)
                        S.barrier()

                with contextlib.ExitStack() as st:
                    wd = sb(st, "wd", [128, NF, D], BF16)
                    hT = sb(st, "hT", [128, NF, min(T, 1024)], BF16)
                    wgu = [sb(st, "wgu%d" % i, [128, 8, 256], BF16) for i in range(2)]
                    sg = [sb(st, "sg%d" % i, [128, 512]) for i in range(2)]
                    g_b = sb(st, "g2_b", [128, D])
                    b_b = sb(st, "b2_b", [128, D])
                    lnt = [(sb(st, "fxres%d" % i, [128, D]),
                            sb(st, "fxb%d" % i, [128, D], BF16), sb(st, "fst6%d" % i, [128, NBN, 6]),
                            sb(st, "fmv%d" % i, [128, 2]), sb(st, "fsc%d" % i, [128, 3])) for i in range(2)]
                    S.dma('sp', g_b[:], ln2_g[l:l + 1, :].partition_broadcast(128), W=['lnp'])
                    S.dma('sp', b_b[:], ln2_b[l:l + 1, :].partition_broadcast(128), W=['lnp'])
                    it = 0
                    wd_loaded = [False]
                    for (h0, hn) in job.halves:
                        blocks = [(b0, min(512, hn - b0)) for b0 in range(0, hn, 512)]
                        for f in range(NF):
                            wb = f % 2
                            S.dma('pool', wgu[wb][:, :, 0:128],
                                  w_gate[l, :, f * 128:(f + 1) * 128].rearrange("(k p) c -> p k c", p=128),
                                  W=[('wgu', wb, 0)])
                            S.dma('pool', wgu[wb][:, :, 128:256],
                                  w_up[l, :, f * 128:(f + 1) * 128].rearrange("(k p) c -> p k c", p=128),
                                  W=[('wgu', wb, 1)])
                            if not wd_loaded[0] and f >= 1:
                                S.dma('pool', wd[:, f - 1, :], w_down[l, (f - 1) * 128:f * 128, :], W=[('wd', f - 1)])
                                if f == NF - 1:
                                    S.dma('pool', wd[:, f, :], w_down[l, f * 128:(f + 1) * 128, :], W=[('wd', f)])
                                    wd_loaded[0] = True
                            for (b0, nt) in blocks:
                                c0 = h0 + b0
                                pb = it % 2
                                it += 1
                                pkg, pku = 4 + 2 * pb, 5 + 2 * pb
                                for which, pkk in ((0, pkg), (1, pku)):
                                    fns = []
                                    for k in range(8):
                                        fns.append(lambda e, k=k, which=which, pkk=pkk: e.matmul(
                                            PS[pkk][:, 0:nt], wgu[wb][:, k, which * 128:(which + 1) * 128],
                                            xT[:, k, c0:c0 + nt], start=(k == 0), stop=(k == 7)))
                                    S.mm(fns, R=[('wgu', wb, which)] + xT_R(c0, nt), W=[('ps', pkk)])
                                S.op('act', lambda e: e.activation(out=sg[pb][:, 0:nt], in_=PS[pkg][:, 0:nt],
                                                                   func=AF.Silu),
                                     R=[('ps', pkg)], W=[('sg', pb)])
                                S.op('dve', lambda e: e.tensor_tensor(hT[:, f, b0:b0 + nt], sg[pb][:, 0:nt],
                                                                      PS[pku][:, 0:nt], ALU.mult),
                                     R=[('sg', pb), ('ps', pku)], W=[('hT', f, b0 // 512)])
                        if (h0, hn) == job.halves[-1]:
                            if l + 1 < NLAY:
                                load_wg(l + 1)
                            elif ji + 1 < len(jobs):
                                load_wg(0)
                        for (r0, tsz) in [(r, s) for (r, s) in job.tiles if h0 <= r < h0 + hn]:
                            ti = r0 // 128
                            b = ti % 2
                            pk = (2 * b, 2 * b + 1)
                            lr = r0 - h0
                            for hf in range(2):
                                fns = []
                                for f in range(NF):
                                    fns.append(lambda e, f=f, hf=hf: e.matmul(
                                        PS[pk[hf]][0:tsz, 0:512], hT[:, f, lr:lr + tsz],
                                        wd[:, f, hf * 512:(hf + 1) * 512], start=(f == 0), stop=(f == NF - 1)))
                                S.mm(fns, R=[('wd', f) for f in range(NF)] + [('hT', f, lr // 512) for f in range(NF)],
                                     W=[('ps', pk[hf])])
                            layernorm_tile(lnt[b], pk, tsz, xs1[r0:r0 + tsz, :], [('xs1', ti)], g_b, b_b,
                                           out2_dst[r0:r0 + tsz, :], [('xs2', ti)], (r0, tsz), not last, 'f%d' % b)
                    S.barrier()
        S.barrier()
    return nc


_NC_CACHE = {}


def _consts():
    c = np.zeros((128, 512), np.float32)
    c[:, 0:128] = np.eye(128, dtype=np.float32)
    j = np.arange(64)[:, None]
    i = np.arange(64)[None, :]
    c[0:64, 128:192] = (i >= j)
    c[0:64, 192:256] = (i > j)
    c[0:64, 256:320] = (j >= i)
    c[0:64, 320:384] = (j > i)
    c[0:64, 384:448] = 1.0
    c[64:128, 448:512] = 1.0
    return c


def _rope(pos):
    inv = np.exp(-np.log(10000.0) * np.arange(16, dtype=np.float32) / 16).astype(np.float32)
    ang = (pos.astype(np.float32)[:, None] * inv[None, :]).astype(np.float32)
    return np.concatenate([np.cos(ang), np.sin(ang)], axis=1).astype(np.float32)


def kernel(**inputs):
    f = lambda a: np.ascontiguousarray(np.asarray(a, dtype=np.float32))
    inp = {k: f(v) for k, v in inputs.items()}
    if 'nc' not in _NC_CACHE:
        _NC_CACHE['nc'] = build_program()
    nc = _NC_CACHE['nc']
    consts = _consts()
    rope_p = _rope(np.arange(SEQ))
    rope_s = _rope(PAST + (np.arange(4 * DSEQ) % DSEQ))
    wnames = ['w_in', 'mla_q_norm', 'w_uq', 'mla_kv_norm', 'w_uk', 'w_uv', 'gdn_conv_w', 'gdn_a_log', 'gdn_dt_bias',
              'gdn_norm', 'conv_w', 'conv_b', 'conv_ln_g', 'conv_ln_b', 'w_out', 'ln1_g', 'ln1_b', 'w_gate', 'w_up',
              'w_down', 'ln2_g', 'ln2_b']
    in_maps = []
    for c in range(NCORES):
        m = {w: inp[w] for w in wnames}
        m['xp'] = inp['x_prompt'][2 * c:2 * c + 2].reshape(2 * SEQ, D)
        m['xs'] = inp['x_sample'][4 * c:4 * c + 4].reshape(4 * DSEQ, D)
        m['c_ckv'] = np.ascontiguousarray(inp['cache_mla_ckv'][:, 4 * c:4 * c + 4])
        m['c_kpe'] = np.ascontiguousarray(inp['cache_mla_kpe'][:, 4 * c:4 * c + 4])
        m['s_gdn'] = np.ascontiguousarray(inp['state_gdn'][:, 4 * c:4 * c + 4])
        m['s_gcv'] = np.ascontiguousarray(inp['state_gdn_conv'][:, 4 * c:4 * c + 4])
        m['s_cv'] = np.ascontiguousarray(inp['state_conv'][:, 4 * c:4 * c + 4])
        m['consts'] = consts
        m['rope_p'] = rope_p
        m['rope_s'] = rope_s
        in_maps.append(m)
    res = run_bass_kernel_spmd(nc, in_maps, core_ids=list(range(NCORES)))
    R = res.results
    cat = lambda name, ax: np.concatenate([R[c][name] for c in range(NCORES)], axis=ax)
    y_p = cat('y_p', 0).reshape(16, SEQ, D)
    y_s = cat('y_s', 0).reshape(32, DSEQ, D)
    outs = (y_p, y_s, cat('o_ckv_p', 1), cat('o_kpe_p', 1), cat('o_gdn_p', 1), cat('o_gcv_p', 1), cat('o_cv_p', 1),
            cat('o_ckv_s', 1), cat('o_kpe_s', 1), cat('o_gdn_s', 1), cat('o_gcv_s', 1), cat('o_cv_s', 1))
    return tuple(np.ascontiguousarray(o.astype(np.float32)) for o in outs)
```
